# Optimizing a Trainium2 kernel written in Bass

```python
import math
import jax
import jax.numpy as jnp
from jax import lax
import numpy as np

D_MODEL = 1024
BATCH = 16
SEQ = 256
DEPTH = 2
DEC_BATCH = 8
DEC_SEQ = 1024
PAST_LEN = 256

GRID_W = 64
N_BRANCH = 4
MIX_W = 256
H_A = 4
NOPE_A = 64
ROPE_A = 32
VH_A = 64
Q_LORA = 256
KV_LORA = 128
D_B = 256
G_B = 4
CHUNK_B = 128
H_C = 4
DH_C = 64
D_C = H_C * DH_C
CHUNK_C = 64
H_D = 4
DH_D = 32
D_D = H_D * 2 * DH_D
D_FF = 4 * D_MODEL
IN_A = Q_LORA + KV_LORA + ROPE_A
IN_B = 2 * D_B
IN_C = 4 * D_C + 4 * H_C
IN_D = 3 * D_D
IN_W = IN_A + IN_B + IN_C + IN_D
ROPE_BASE = 10000.0
EPS = 1e-6
MLA_SCALE = (NOPE_A + ROPE_A) ** -0.5
DIFF_SCALE = DH_D ** -0.5
DENSE_MAX_KEYS = 2048
Q_BLOCK = 128

kernel_name = "hybrid_diffusion_mla_gmlp_mlstm_diffattn_step"


def _rms(x, g):
    xf = x.astype(jnp.float32)
    y = xf * lax.rsqrt(jnp.mean(xf * xf, axis=-1, keepdims=True) + EPS)
    return (y * g.astype(jnp.float32)).astype(x.dtype)


def _axial_tables(rows, dim):
    row = jnp.repeat(jnp.arange(rows), GRID_W).astype(jnp.float32)
    col = jnp.tile(jnp.arange(GRID_W), rows).astype(jnp.float32)
    nf = dim // 4
    inv = jnp.exp(-math.log(ROPE_BASE) * jnp.arange(nf, dtype=jnp.float32) / nf)
    ang = jnp.concatenate([row[:, None] * inv, col[:, None] * inv], axis=-1)
    return jnp.cos(ang), jnp.sin(ang)


def _rope(x, cos, sin):
    half = x.shape[-1] // 2
    xf = x.astype(jnp.float32)
    x1, x2 = xf[..., :half], xf[..., half:]
    c = cos[None, :, None, :]
    s = sin[None, :, None, :]
    return jnp.concatenate([x1 * c - x2 * s, x1 * s + x2 * c], axis=-1).astype(x.dtype)


def _attend(q, k, v, scale):
    def block(qb):
        s = jnp.einsum('bqhd,bkhd->bhqk', qb, k).astype(jnp.float32) * scale
        p = jax.nn.softmax(s, axis=-1).astype(v.dtype)
        return jnp.einsum('bhqk,bkhe->bqhe', p, v)
    B, Lq = q.shape[0], q.shape[1]
    if k.shape[1] < DENSE_MAX_KEYS or Lq % Q_BLOCK != 0:
        return block(q)
    qb = q.reshape(B, Lq // Q_BLOCK, Q_BLOCK, *q.shape[2:]).swapaxes(0, 1)
    out = lax.map(block, qb)
    return out.swapaxes(0, 1).reshape(B, Lq, *out.shape[3:])


def _mla(z, q_norm, w_uq, kv_norm, w_ukv, rope_cs, ctx_ckv, ctx_kr):
    B, L, _ = z.shape
    cq = _rms(z[..., :Q_LORA], q_norm)
    ckv = _rms(z[..., Q_LORA:Q_LORA + KV_LORA], kv_norm)
    kr = z[..., Q_LORA + KV_LORA:IN_A]
    qh = (cq @ w_uq).reshape(B, L, H_A, NOPE_A + ROPE_A)
    q_nope, q_rope = qh[..., :NOPE_A], qh[..., NOPE_A:]
    if rope_cs is None:
        ckv_all, kr_all = ckv, kr
    else:
        cos, sin = rope_cs
        q_rope = _rope(q_rope, cos, sin)
        kr_lat = _rope(kr[:, :, None, :], cos, sin)[:, :, 0]
        ckv_all = jnp.concatenate([ctx_ckv.astype(ckv.dtype), ckv], axis=1)
        kr_all = jnp.concatenate([ctx_kr.astype(kr.dtype), kr_lat], axis=1)
    Lk = ckv_all.shape[1]
    kv = (ckv_all @ w_ukv).reshape(B, Lk, H_A, NOPE_A + VH_A)
    k = jnp.concatenate([kv[..., :NOPE_A], jnp.broadcast_to(kr_all[:, :, None, :], (B, Lk, H_A, ROPE_A))], axis=-1)
    q = jnp.concatenate([q_nope, q_rope], axis=-1)
    out = _attend(q, k, kv[..., NOPE_A:], MLA_SCALE)
    return out.reshape(B, L, H_A * VH_A), ckv, kr


def _chunk_mlp(z, v_norm, w_s, b_s):
    B, L, _ = z.shape
    u = z[..., :D_B]
    v = _rms(z[..., D_B:], v_norm)
    vc = v.reshape(B, L // CHUNK_B, CHUNK_B, G_B, D_B // G_B)
    mixed = jnp.einsum('gpq,bnqgd->bnpgd', w_s, vc) + b_s.T[None, None, :, :, None]
    return u * mixed.reshape(B, L, D_B)


def _mlstm_dir(q, k, v, i_pre, f_pre, state):
    B, H, L, _ = q.shape
    nc = L // CHUNK_C

    def chunks(a):
        return jnp.moveaxis(a.reshape(B, H, nc, CHUNK_C, *a.shape[3:]), 2, 0)

    tril = jnp.tril(jnp.ones((CHUNK_C, CHUNK_C), dtype=bool))

    def step(carry, xs):
        C, n, m = carry
        qc, kc, vc, ic, lfc = xs
        b = jnp.cumsum(lfc, axis=-1)
        d_log = jnp.where(tril, b[..., :, None] - b[..., None, :] + ic[..., None, :], -jnp.inf)
        inter = b + m[..., None]
        mt = jnp.maximum(inter, jnp.max(d_log, axis=-1))
        w_inter = jnp.exp(inter - mt)
        s = jnp.einsum('bhtd,bhsd->bhts', qc, kc) * jnp.exp(d_log - mt[..., None])
        num = w_inter[..., None] * jnp.einsum('bhtd,bhde->bhte', qc, C) + jnp.einsum('bhts,bhse->bhte', s, vc)
        den = w_inter * jnp.einsum('bhtd,bhd->bht', qc, n) + jnp.sum(s, axis=-1)
        hc = num / jnp.maximum(jnp.abs(den), jnp.exp(-mt))[..., None]
        bT = b[..., -1]
        a_s = bT[..., None] - b + ic
        m_new = jnp.maximum(bT + m, jnp.max(a_s, axis=-1))
        w_old = jnp.exp(bT + m - m_new)
        w_s = jnp.exp(a_s - m_new[..., None])
        C_new = w_old[..., None, None] * C + jnp.einsum('bhs,bhsd,bhse->bhde', w_s, kc, vc)
        n_new = w_old[..., None] * n + jnp.einsum('bhs,bhsd->bhd', w_s, kc)
        return (C_new, n_new, m_new), hc

    xs = (chunks(q), chunks(k), chunks(v), chunks(i_pre), chunks(jax.nn.log_sigmoid(f_pre)))
    state, hs = lax.scan(step, state, xs)
    h = jnp.moveaxis(hs, 0, 2).reshape(B, H, L, v.shape[-1])
    return h, state


def _mlstm(z, gate_bias, head_norm, state):
    B, L, _ = z.shape

    def heads(a):
        return a.reshape(B, L, H_C, DH_C).transpose(0, 2, 1, 3).astype(jnp.float32)

    q = heads(z[..., :D_C])
    k = heads(z[..., D_C:2 * D_C]) * (DH_C ** -0.5)
    v = heads(z[..., 2 * D_C:3 * D_C])
    o = z[..., 3 * D_C:4 * D_C]
    g = (z[..., 4 * D_C:IN_C].reshape(B, L, 2, 2, H_C) + gate_bias).astype(jnp.float32)
    g = g.transpose(2, 3, 0, 4, 1)
    C0, n0, m0 = state
    st0_fw = (C0[:, 0].astype(jnp.float32), n0[:, 0].astype(jnp.float32), m0[:, 0].astype(jnp.float32))
    st0_bw = (C0[:, 1].astype(jnp.float32), n0[:, 1].astype(jnp.float32), m0[:, 1].astype(jnp.float32))
    flip = lambda a: jnp.flip(a, axis=2)
    h_fw, st_fw = _mlstm_dir(q, k, v, g[0, 0], g[0, 1], st0_fw)
    h_bw, st_bw = _mlstm_dir(flip(q), flip(k), flip(v), flip(g[1, 0]), flip(g[1, 1]), st0_bw)
    h = (h_fw + flip(h_bw)).transpose(0, 2, 1, 3)
    h = _rms(h, head_norm.reshape(H_C, DH_C)).reshape(B, L, D_C)
    out = (jax.nn.sigmoid(o.astype(jnp.float32)) * h).astype(z.dtype)
    new_C = jnp.stack([st_fw[0], st_bw[0]], axis=1)
    new_n = jnp.stack([st_fw[1], st_bw[1]], axis=1)
    new_m = jnp.stack([st_fw[2], st_bw[2]], axis=1)
    return out, (new_C, new_n, new_m)


def _diff_attn(z, lam, sub_norm, lam_init, rope_cs, ctx_k, ctx_v):
    B, L, _ = z.shape
    q = z[..., :D_D].reshape(B, L, H_D, 2, DH_D)
    k = z[..., D_D:2 * D_D].reshape(B, L, H_D, 2, DH_D)
    v = z[..., 2 * D_D:3 * D_D].reshape(B, L, H_D, 2 * DH_D)
    if rope_cs is None:
        q_use, k_all, v_all = q, k, v
    else:
        cos, sin = rope_cs
        q_use = _rope(q.reshape(B, L, 2 * H_D, DH_D), cos, sin).reshape(B, L, H_D, 2, DH_D)
        k_rot = _rope(k.reshape(B, L, 2 * H_D, DH_D), cos, sin).reshape(B, L, H_D, 2, DH_D)
        k_all = jnp.concatenate([ctx_k.astype(k.dtype), k_rot], axis=1)
        v_all = jnp.concatenate([ctx_v.astype(v.dtype), v], axis=1)
    lf = lam.astype(jnp.float32)
    lam_val = jnp.exp(jnp.sum(lf[0] * lf[1])) - jnp.exp(jnp.sum(lf[2] * lf[3])) + lam_init
    o1 = _attend(q_use[:, :, :, 0], k_all[:, :, :, 0], v_all, DIFF_SCALE).astype(jnp.float32)
    o2 = _attend(q_use[:, :, :, 1], k_all[:, :, :, 1], v_all, DIFF_SCALE).astype(jnp.float32)
    o = _rms(o1 - lam_val * o2, sub_norm) * (1.0 - lam_init)
    return o.reshape(B, L, D_D).astype(z.dtype), k, v


def _trunk_layer(x, cond, rope_cs, ctx, l, w_mod, b_mod, norm_g, w_in, mla_q_norm, w_uq, mla_kv_norm, w_ukv,
                 gmlp_v_norm, gmlp_w_s, gmlp_b_s, mlstm_gate_bias, mlstm_head_norm, diff_lambda, diff_sub_norm,
                 w_branch, w_merge, b_merge, w_out, w_ff1, w_ff2):
    B, L, _ = x.shape
    mod = (jax.nn.silu(cond) @ w_mod[l] + b_mod[l])[:, None, :]
    sh1, sc1, g1, sh2, sc2, g2 = jnp.split(mod, 6, axis=-1)
    h = _rms(x, norm_g[l, 0]) * (1.0 + sc1) + sh1
    z = h @ w_in[l]
    za = z[..., :IN_A]
    zb = z[..., IN_A:IN_A + IN_B]
    zc = z[..., IN_A + IN_B:IN_A + IN_B + IN_C]
    zd = z[..., IN_A + IN_B + IN_C:]
    if ctx is None:
        ctx_ckv = ctx_kr = ctx_k = ctx_v = None
        st0 = (jnp.zeros((B, 2, H_C, DH_C, DH_C), jnp.float32), jnp.zeros((B, 2, H_C, DH_C), jnp.float32),
               jnp.zeros((B, 2, H_C), jnp.float32))
        rope_a = rope_d = None
    else:
        ctx_ckv, ctx_kr, ctx_k, ctx_v, sC, sn, sm = ctx
        st0 = (sC, sn, sm)
        rope_a, rope_d = rope_cs
    lam_init = 0.8 - 0.6 * math.exp(-0.3 * l)
    ya, ckv, kr = _mla(za, mla_q_norm[l], w_uq[l], mla_kv_norm[l], w_ukv[l], rope_a, ctx_ckv, ctx_kr)
    yb = _chunk_mlp(zb, gmlp_v_norm[l], gmlp_w_s[l], gmlp_b_s[l])
    yc, st = _mlstm(zc, mlstm_gate_bias[l], mlstm_head_norm[l], st0)
    yd, kd, vd = _diff_attn(zd, diff_lambda[l], diff_sub_norm[l], lam_init, rope_d, ctx_k, ctx_v)
    br = jnp.einsum('blnw,nwd->blnd', jnp.stack([ya, yb, yc, yd], axis=2), w_branch[l])
    gates = jax.nn.sigmoid(h @ w_merge[l] + b_merge[l]).reshape(B, L, N_BRANCH, D_MODEL)
    y = jnp.sum(gates * br, axis=2) @ w_out[l]
    x = x + g1 * _rms(y, norm_g[l, 1])
    h2 = _rms(x, norm_g[l, 2]) * (1.0 + sc2) + sh2
    f = jnp.square(jax.nn.relu(h2 @ w_ff1[l])) @ w_ff2[l]
    x = x + g2 * _rms(f, norm_g[l, 3])
    if ctx is None:
        return x, (ckv, kr, kd, vd, st[0], st[1], st[2])
    return x, None


def setup_inputs(seed: int = 0) -> dict:
    key = jax.random.key(seed)
    ks = jax.random.split(key, 40)
    f32 = jnp.float32

    def nrm(i, shape, scale):
        return jax.random.normal(ks[i], shape, f32) * scale

    def gain(i, shape):
        return 1.0 + nrm(i, shape, 0.05)

    gate_bias = nrm(16, (DEPTH, 2, 2, H_C), 0.1) + jnp.array([0.0, 3.0], f32)[None, None, :, None]
    return {
        "x_prompt": nrm(0, (BATCH, SEQ, D_MODEL), 1.0),
        "x_sample": nrm(1, (DEC_BATCH, DEC_SEQ, D_MODEL), 1.0),
        "cache_mla_ckv": nrm(2, (DEC_BATCH, DEPTH, PAST_LEN, KV_LORA), 1.0),
        "cache_mla_krope": nrm(3, (DEC_BATCH, DEPTH, PAST_LEN, ROPE_A), 1.0),
        "cache_diff_k": nrm(4, (DEC_BATCH, DEPTH, PAST_LEN, H_D, 2, DH_D), 1.0),
        "cache_diff_v": nrm(5, (DEC_BATCH, DEPTH, PAST_LEN, H_D, 2 * DH_D), 1.0),
        "state_mlstm_C": nrm(6, (DEC_BATCH, DEPTH, 2, H_C, DH_C, DH_C), 0.5),
        "state_mlstm_n": nrm(7, (DEC_BATCH, DEPTH, 2, H_C, DH_C), 0.5),
        "state_mlstm_m": nrm(8, (DEC_BATCH, DEPTH, 2, H_C), 0.5),
        "c": nrm(9, (DEC_BATCH, D_MODEL), 1.0),
        "c_ctx": nrm(10, (D_MODEL,), 1.0),
        "w_mod": nrm(11, (DEPTH, D_MODEL, 6 * D_MODEL), 0.5 * D_MODEL ** -0.5),
        "b_mod": nrm(12, (DEPTH, 6 * D_MODEL), 0.02),
        "norm_g": gain(13, (DEPTH, 4, D_MODEL)),
        "w_in": nrm(14, (DEPTH, D_MODEL, IN_W), D_MODEL ** -0.5),
        "mla_q_norm": gain(15, (DEPTH, Q_LORA)),
        "w_uq": nrm(17, (DEPTH, Q_LORA, H_A * (NOPE_A + ROPE_A)), Q_LORA ** -0.5),
        "mla_kv_norm": gain(18, (DEPTH, KV_LORA)),
        "w_ukv": nrm(19, (DEPTH, KV_LORA, H_A * (NOPE_A + VH_A)), KV_LORA ** -0.5),
        "gmlp_v_norm": gain(20, (DEPTH, D_B)),
        "gmlp_w_s": nrm(21, (DEPTH, G_B, CHUNK_B, CHUNK_B), CHUNK_B ** -0.5),
        "gmlp_b_s": 1.0 + nrm(22, (DEPTH, G_B, CHUNK_B), 0.05),
        "mlstm_gate_bias": gate_bias,
        "mlstm_head_norm": gain(23, (DEPTH, D_C)),
        "diff_lambda": nrm(24, (DEPTH, 4, DH_D), 0.1),
        "diff_sub_norm": gain(25, (DEPTH, 2 * DH_D)),
        "w_branch": nrm(26, (DEPTH, N_BRANCH, MIX_W, D_MODEL), MIX_W ** -0.5),
        "w_merge": nrm(27, (DEPTH, D_MODEL, N_BRANCH * D_MODEL), D_MODEL ** -0.5),
        "b_merge": nrm(28, (DEPTH, N_BRANCH * D_MODEL), 0.02),
        "w_out": nrm(29, (DEPTH, D_MODEL, D_MODEL), D_MODEL ** -0.5),
        "w_ff1": nrm(30, (DEPTH, D_MODEL, D_FF), D_MODEL ** -0.5),
        "w_ff2": nrm(31, (DEPTH, D_FF, D_MODEL), D_FF ** -0.5),
    }


def reference(x_prompt, x_sample, cache_mla_ckv, cache_mla_krope, cache_diff_k, cache_diff_v,
              state_mlstm_C, state_mlstm_n, state_mlstm_m, c, c_ctx, w_mod, b_mod, norm_g, w_in,
              mla_q_norm, w_uq, mla_kv_norm, w_ukv, gmlp_v_norm, gmlp_w_s, gmlp_b_s, mlstm_gate_bias,
              mlstm_head_norm, diff_lambda, diff_sub_norm, w_branch, w_merge, b_merge, w_out, w_ff1, w_ff2):
    weights = (w_mod, b_mod, norm_g, w_in, mla_q_norm, w_uq, mla_kv_norm, w_ukv, gmlp_v_norm, gmlp_w_s,
               gmlp_b_s, mlstm_gate_bias, mlstm_head_norm, diff_lambda, diff_sub_norm, w_branch, w_merge,
               b_merge, w_out, w_ff1, w_ff2)
    xp = x_prompt
    ents = []
    for l in range(DEPTH):
        xp, ent = _trunk_layer(xp, c_ctx[None, :], None, None, l, *weights)
        ents.append(ent)
    new_mla_ckv = jnp.stack([e[0] for e in ents], axis=1)
    new_mla_krope = jnp.stack([e[1] for e in ents], axis=1)
    new_diff_k = jnp.stack([e[2] for e in ents], axis=1)
    new_diff_v = jnp.stack([e[3] for e in ents], axis=1)
    new_mlstm_C = jnp.stack([e[4] for e in ents], axis=1)
    new_mlstm_n = jnp.stack([e[5] for e in ents], axis=1)
    new_mlstm_m = jnp.stack([e[6] for e in ents], axis=1)
    rows = x_sample.shape[1] // GRID_W
    rope_cs = (_axial_tables(rows, ROPE_A), _axial_tables(rows, DH_D))
    xs = x_sample
    for l in range(DEPTH):
        ctx = (cache_mla_ckv[:, l], cache_mla_krope[:, l], cache_diff_k[:, l], cache_diff_v[:, l],
               state_mlstm_C[:, l], state_mlstm_n[:, l], state_mlstm_m[:, l])
        xs, _ = _trunk_layer(xs, c, rope_cs, ctx, l, *weights)
    return (xp, xs, new_mla_ckv, new_mla_krope, new_diff_k, new_diff_v, new_mlstm_C, new_mlstm_n, new_mlstm_m)
```

```python
import math
from contextlib import ExitStack
import numpy as np
import concourse.bass as bass
import concourse.mybir as mybir
from concourse.bass_utils import run_bass_kernel_spmd

F32 = mybir.dt.float32
BF16 = mybir.dt.bfloat16
AF = mybir.ActivationFunctionType
ALU = mybir.AluOpType
AX = mybir.AxisListType

D = 1024
T = 1536
NT = 3
DEPTH = 2
EPS = 1e-6
DFF = 4096
MLA_SCALE = 96 ** -0.5
DIFF_SCALE = 32 ** -0.5
OA, OB, OC, OD = 0, 416, 928, 1968
SEQS = [(0, 256, False), (256, 256, False), (512, 1024, True)]


class Sched:
    def __init__(self, nc, es):
        self.nc = nc
        self.es = es
        self.eng = {'pe': nc.tensor, 'act': nc.scalar, 'dve': nc.vector, 'pool': nc.gpsimd, 'sp': nc.sync}
        self.sem = {e: es.enter_context(nc.semaphore('s_' + e)) for e in self.eng}
        self.cnt = {e: 0 for e in self.eng}
        self.pending = {e: False for e in self.eng}
        self.waited = {e: {} for e in self.eng}
        self.lastw = {}
        self.readers = {}
        self.dsem = {}
        self.ninst = 0

    def _wait(self, e, tok):
        name, sem, val = tok
        if name == e and e == 'pe':
            return
        if self.waited[e].get(name, 0) >= val:
            return
        self.eng[e].wait_ge(sem, val)
        self.waited[e][name] = val

    def _deps(self, e, reads, writes):
        best = {}

        def add(t):
            if t is None:
                return
            b = best.get(t[0])
            if b is None or b[2] < t[2]:
                best[t[0]] = t
        relax = False
        for r in reads:
            add(self.lastw.get(r))
        for w in writes:
            t = self.lastw.get(w)
            if t is not None and not (relax and t[0] == e and w not in reads):
                add(t)
            for t in self.readers.get(w, {}).values():
                if not (relax and t[0] == e):
                    add(t)
        for t in best.values():
            self._wait(e, t)

    def _commit(self, tok, reads, writes):
        for w in writes:
            self.lastw[w] = tok
            self.readers[w] = {}
        for r in reads:
            d = self.readers.setdefault(r, {})
            b = d.get(tok[0])
            if b is None or b[2] < tok[2]:
                d[tok[0]] = tok

    def op(self, e, fn, reads=(), writes=(), inc=True):
        writes = list(writes) + [r for r in reads if r.startswith('ps') and r[2:].isdigit() and r not in writes]
        self._deps(e, reads, writes)
        inst = fn(self.eng[e])
        self.ninst += 1
        if inc:
            self.cnt[e] += 1
            inst.then_inc(self.sem[e], 1)
            tok = (e, self.sem[e], self.cnt[e])
            self.pending[e] = False
        else:
            tok = (e, self.sem[e], self.cnt[e] + 1)
            self.pending[e] = True
        self._commit(tok, reads, writes)
        return tok

    def dma(self, q, key, out, in_, reads=(), writes=()):
        self._deps(q, reads, writes)
        if key not in self.dsem:
            self.dsem[key] = [self.es.enter_context(self.nc.semaphore('d_' + key)), 0]
        d = self.dsem[key]
        d[1] += 16
        self.eng[q].dma_start(out=out, in_=in_).then_inc(d[0], 16)
        self.ninst += 1
        tok = ('d_' + key, d[0], d[1])
        self._commit(tok, reads, writes)
        return tok

    def barrier(self):
        toks = []
        for e in self.eng:
            assert not self.pending[e], e
            if self.cnt[e] > 0:
                toks.append((e, self.sem[e], self.cnt[e]))
        for k, d in self.dsem.items():
            if k[0] == 'w' or k.startswith('c_'):
                continue
            toks.append(('d_' + k, d[0], d[1]))
        for e in self.eng:
            for t in toks:
                if t[0] != e:
                    self._wait(e, t)

    def finish(self, out_keys):
        for k in out_keys:
            d = self.dsem[k]
            self._wait('sp', ('d_' + k, d[0], d[1]))


def ktile(W, ng):
    K, N = W.shape
    assert K % 128 == 0 and N % ng == 0
    return np.ascontiguousarray(W.reshape(K // 128, 128, N // ng, ng).transpose(2, 1, 0, 3))


def fm_vec(v):
    sh = v.shape
    n = sh[-1] // 128
    a = v.reshape(*sh[:-1], n, 128)
    return np.ascontiguousarray(np.moveaxis(a, -1, 0))


def pick_cols(W, cols):
    cols = np.asarray(cols)
    out = np.zeros((W.shape[0], len(cols)), np.float32)
    m = cols >= 0
    out[:, m] = W[:, cols[m]]
    return out


def pad_to(lst, n):
    return list(lst) + [-1] * (n - len(lst))


def win_groups():
    g = {}
    kr = [OA + 384 + i for i in range(32)]
    kr_sw = [OA + 384 + (i + 16) % 32 for i in range(32)]
    g['A1'] = list(range(OA, OA + 384)) + [-1] * 64 + kr
    g['A2'] = [-1] * 64 + kr_sw
    g['A3'] = list(range(OA + 256, OA + 384)) + kr
    g['B1'] = list(range(OB, OB + 256))
    g['B2'] = list(range(OB + 256, OB + 512))
    g['C1'] = list(range(OC, OC + 512))
    gi = pad_to([OC + 1024 + h for h in range(4)], 32) + [OC + 1024 + 8 + h for h in range(4)]
    gf = pad_to([OC + 1024 + 4 + h for h in range(4)], 32) + [OC + 1024 + 12 + h for h in range(4)]
    g['C2'] = pad_to(gi, 64) + pad_to(gf, 64)
    g['C3'] = list(range(OC + 512, OC + 1024))
    g['C4'] = list(range(OC + 256, OC + 512))

    def pairs(base, ps, sw):
        out = []
        for p in ps:
            out += [base + p * 32 + ((d + 16) % 32 if sw else d) for d in range(32)]
        return pad_to(out, 128)
    q, k = OD, OD + 256
    g['D1'] = pairs(q, [0, 1, 2], False) + pairs(q, [0, 1, 2], True) + pairs(q, [3, 4, 5], False) + pairs(q, [3, 4, 5], True)
    g['D2'] = pairs(q, [6, 7], False) + pairs(q, [6, 7], True) + pairs(k, [0, 1, 2], False) + pairs(k, [0, 1, 2], True)
    g['D3'] = pairs(k, [3, 4, 5], False) + pairs(k, [3, 4, 5], True) + pairs(k, [6, 7], False) + pairs(k, [6, 7], True)
    g['D4'] = list(range(OD + 512, OD + 768)) + list(range(OD + 256, OD + 512))
    return g


WG = win_groups()
WG_ORDER = ['A1', 'A2', 'A3', 'B1', 'B2', 'C1', 'C2', 'C3', 'C4', 'D1', 'D2', 'D3', 'D4']


def rope_tables():
    t = np.arange(1024)
    row = (t // 64).astype(np.float32)
    col = (t % 64).astype(np.float32)
    nf = 8
    inv = np.exp(-math.log(10000.0) * np.arange(nf, dtype=np.float32) / nf).astype(np.float32)
    ang = np.concatenate([row[:, None] * inv, col[:, None] * inv], axis=-1).astype(np.float32)
    c, s = np.cos(ang).T, np.sin(ang).T
    cos2 = np.concatenate([c, c], 0)
    sin2 = np.concatenate([-s, s], 0)
    return (np.ascontiguousarray(np.tile(cos2, (4, 1)), dtype=np.float32),
            np.ascontiguousarray(np.tile(sin2, (4, 1)), dtype=np.float32))


def prep_shared(inp):
    sh = {}
    f = lambda a: np.asarray(a, dtype=np.float32)
    w_mod = f(inp['w_mod'])
    sh['wmod'] = np.stack([ktile(w_mod[l], 512) for l in range(DEPTH)])
    sh['bmod'] = fm_vec(f(inp['b_mod']))
    sh['ng'] = fm_vec(f(inp['norm_g']))
    w_in = f(inp['w_in'])
    wg = np.zeros((DEPTH, len(WG_ORDER), 128, 8, 512), np.float32)
    for l in range(DEPTH):
        for gi, name in enumerate(WG_ORDER):
            cols = WG[name]
            wsel = pick_cols(w_in[l], cols)
            wg[l, gi, :, :, :len(cols)] = wsel.reshape(8, 128, len(cols)).transpose(1, 0, 2)
    sh['win'] = wg
    w_merge = f(inp['w_merge'])
    wm = w_merge.reshape(DEPTH, D, 4, 8, 128).transpose(0, 1, 3, 2, 4).reshape(DEPTH, D, 4096)
    sh['wmerge'] = np.stack([ktile(wm[l], 512) for l in range(DEPTH)])
    bm = f(inp['b_merge']).reshape(DEPTH, 4, 8, 128)
    sh['bmerge'] = np.ascontiguousarray(bm.transpose(3, 0, 2, 1))
    wb = f(inp['w_branch']).reshape(DEPTH, 4, 2, 128, 8, 128)
    sh['wbranch'] = np.ascontiguousarray(wb.transpose(0, 4, 3, 1, 2, 5))
    w_out = f(inp['w_out'])
    sh['wout'] = np.stack([ktile(w_out[l], 512) for l in range(DEPTH)])
    w_ff1 = f(inp['w_ff1'])
    sh['wff1'] = np.stack([ktile(w_ff1[l], 512) for l in range(DEPTH)])
    w_ff2 = f(inp['w_ff2'])
    sh['wff2'] = np.ascontiguousarray(w_ff2.reshape(DEPTH, 8, 4, 128, 1024).transpose(0, 1, 3, 2, 4))
    w_uq = f(inp['w_uq'])
    cols, cols_sw = [], []
    for h in range(4):
        nope = [h * 96 + i for i in range(64)]
        cols += nope + [h * 96 + 64 + i for i in range(32)]
        cols_sw += nope + [h * 96 + 64 + (i + 16) % 32 for i in range(32)]
    wuq = np.stack([np.concatenate([w_uq[l][:, cols], w_uq[l][:, cols_sw]], 1) for l in range(DEPTH)])
    sh['wuq'] = np.ascontiguousarray(wuq.reshape(DEPTH, 2, 128, 768).transpose(2, 0, 1, 3))
    w_ukv = f(inp['w_ukv']).reshape(DEPTH, 128, 4, 128)
    kn = w_ukv[:, :, :, :64].reshape(DEPTH, 128, 256)
    vv = w_ukv[:, :, :, 64:].reshape(DEPTH, 128, 256)
    sh['wukv'] = np.ascontiguousarray(np.concatenate([kn, vv], -1).transpose(1, 0, 2))
    sh['qn'] = fm_vec(f(inp['mla_q_norm']))
    sh['kvn'] = fm_vec(f(inp['mla_kv_norm']))
    sh['kvn_row'] = f(inp['mla_kv_norm']).reshape(1, DEPTH * 128)
    sh['gvn'] = fm_vec(f(inp['gmlp_v_norm']))
    bs = f(inp['gmlp_b_s'])
    bsB = np.zeros((128, DEPTH, 2, 128), np.float32)
    for kk in range(2):
        bsB[0:64, :, kk, :] = bs[:, 2 * kk, :][None]
        bsB[64:128, :, kk, :] = bs[:, 2 * kk + 1, :][None]
    sh['bsB'] = bsB
    sh['wsT'] = np.ascontiguousarray(f(inp['gmlp_w_s']).transpose(3, 0, 1, 2))
    gb = f(inp['mlstm_gate_bias'])
    gbT = np.zeros((36, DEPTH, 2), np.float32)
    for l in range(DEPTH):
        for gate in range(2):
            gbT[0:4, l, gate] = gb[l, 0, gate]
            gbT[32:36, l, gate] = gb[l, 1, gate]
    sh['gbT'] = gbT
    sh['hn_row'] = f(inp['mlstm_head_norm']).reshape(1, DEPTH * 256)
    sh['lam_row'] = f(inp['diff_lambda']).reshape(1, DEPTH * 128)
    sh['sn_row'] = f(inp['diff_sub_norm']).reshape(1, DEPTH * 64)
    cos2, sin2 = rope_tables()
    sh['cosT'] = cos2
    sh['sinT'] = sin2
    ii = np.arange(128)
    sh['ident'] = np.eye(128, dtype=np.float32)
    sh['mask_le'] = (ii[:, None] <= ii[None, :]).astype(np.float32)
    sh['mask_ge'] = (ii[:, None] >= ii[None, :]).astype(np.float32)
    sel = np.zeros((36, 4, 128), np.float32)
    for r in range(4):
        sel[r, r, :] = 1.0
        sel[32 + r, r, :] = 1.0
    sh['sel'] = sel
    return sh


def prep_core(inp, c):
    f = lambda a: np.asarray(a, dtype=np.float32)
    xp = f(inp['x_prompt'])
    xs = f(inp['x_sample'])
    xtok = np.concatenate([xp[2 * c], xp[2 * c + 1], xs[c]], axis=0)
    m = {}
    m['xT'] = np.ascontiguousarray(xtok.reshape(T, 8, 128).transpose(2, 1, 0))
    cond = np.stack([f(inp['c_ctx']), f(inp['c'])[c]], axis=-1)
    m['condT'] = np.ascontiguousarray(cond.reshape(8, 128, 2).transpose(1, 0, 2))
    m['c_ckvT'] = np.ascontiguousarray(f(inp['cache_mla_ckv'])[c].transpose(0, 2, 1))
    m['c_krT'] = np.ascontiguousarray(f(inp['cache_mla_krope'])[c].transpose(0, 2, 1))
    dk = f(inp['cache_diff_k'])[c].reshape(DEPTH, 256, 8, 32)
    dkT = np.zeros((DEPTH, 96, 3, 256), np.float32)
    for p in range(8):
        dkT[:, (p % 3) * 32:(p % 3) * 32 + 32, p // 3, :] = dk[:, :, p, :].transpose(0, 2, 1)
    m['c_dkT'] = dkT
    m['c_dv'] = np.ascontiguousarray(f(inp['cache_diff_v'])[c].reshape(DEPTH, 2, 128, 4, 64).transpose(2, 0, 1, 3, 4))
    sC = f(inp['state_mlstm_C'])[c]
    sn = f(inp['state_mlstm_n'])[c]
    c0 = np.zeros((128, DEPTH, 2, 2, 65), np.float32)
    for h in range(4):
        c0[(h % 2) * 64:(h % 2) * 64 + 64, :, :, h // 2, 0:64] = sC[:, :, h].transpose(2, 0, 1, 3)
        c0[(h % 2) * 64:(h % 2) * 64 + 64, :, :, h // 2, 64] = sn[:, :, h].transpose(2, 0, 1)
    m['c_C0'] = c0
    sm = f(inp['state_mlstm_m'])[c]
    m0T = np.zeros((36, DEPTH), np.float32)
    m0T[0:4] = sm[:, 0].T
    m0T[32:36] = sm[:, 1].T
    m['c_m0T'] = m0T
    m['c_m0row'] = np.ascontiguousarray(sm.reshape(1, DEPTH * 8))
    return m


class StopBuild(Exception):
    pass


class Builder:
    def cut(self, name):
        if self.stop_after == name:
            raise StopBuild()

    def __init__(self, debug=(), nlayers=DEPTH, stop_after=None, wplan=None):
        self.stop_after = stop_after
        self.wplan = wplan
        self.debug = set(debug)
        self.nlayers = nlayers
        self.nc = bass.Bass("TRN2", target_bir_lowering=False)
        self.es = ExitStack()
        self.S = Sched(self.nc, self.es)
        self.ins = {}
        self.outs = {}
        self.out_keys = []
        self.pools = {'all': list(range(8)), 'lo': list(range(4))}
        self.rr = {'all': 0, 'lo': 0}

    def din(self, name, shape):
        ap = self.nc.dram_tensor(name, list(shape), F32, kind="ExternalInput").ap()
        self.ins[name] = ap
        return ap

    def dout(self, name, shape):
        ap = self.nc.dram_tensor(name, list(shape), F32, kind="ExternalOutput").ap()
        self.outs[name] = ap
        return ap

    def sb(self, name, shape, dt):
        return self.es.enter_context(self.nc.sbuf_tensor(name, list(shape), dt))

    def psum(self, pool='all'):
        lst = self.pools[pool]
        b = lst[self.rr[pool] % len(lst)]
        self.rr[pool] += 1
        return b

    def rsqrt(self, out_ap, in_ap, c, reads, writes):
        cb = self.cbias(c, out_ap)
        if getattr(self, 'rsqrt_lnexp', False):
            self.S.op('act', lambda e: e.activation(out=out_ap, in_=in_ap, func=AF.Ln, bias=cb, scale=1.0),
                      reads=list(reads) + ['cbias'], writes=writes)
            self.S.op('act', lambda e: e.activation(out=out_ap, in_=out_ap, func=AF.Exp, scale=-0.5), reads=writes, writes=writes)
            return
        self.S.op('act', lambda e: e.activation(out=out_ap, in_=in_ap, func=AF.Sqrt, bias=cb, scale=1.0),
                  reads=list(reads) + ['cbias'], writes=writes)
        self.S.op('dve', lambda e: e.reciprocal(out=out_ap, in_=out_ap), reads=writes, writes=writes)

    def cbias(self, c, like=None):
        a = self._cb[round(float(c), 12)]
        if like is None:
            return a
        bp, n = like.base_partition(), like.shape[0]
        return a[bp:bp + n, :]

    def store(self, key, out_ap, in_ap, reads):
        if key not in self.out_keys:
            self.out_keys.append(key)
        self.S.dma('sp', key, out_ap, in_ap, reads=reads, writes=['dram_' + key])

    def build(self):
        nc, S, es = self.nc, self.S, self.es
        NG = len(WG_ORDER)
        xT_d = self.din('xT', [128, 8, T])
        condT_d = self.din('condT', [128, 8, 2])
        wmod_d = self.din('wmod', [DEPTH, 12, 128, 8, 512])
        bmod_d = self.din('bmod', [128, DEPTH, 48])
        ng_d = self.din('ng', [128, DEPTH, 4, 8])
        win_d = self.din('win', [DEPTH, NG, 128, 8, 512])
        wmerge_d = self.din('wmerge', [DEPTH, 8, 128, 8, 512])
        bmerge_d = self.din('bmerge', [128, DEPTH, 8, 4])
        wbranch_d = self.din('wbranch', [DEPTH, 8, 128, 4, 2, 128])
        wout_d = self.din('wout', [DEPTH, 2, 128, 8, 512])
        wff1_d = self.din('wff1', [DEPTH, 8, 128, 8, 512])
        wff2_d = self.din('wff2', [DEPTH, 8, 128, 4, 1024])
        wuq_d = self.din('wuq', [128, DEPTH, 2, 768])
        wukv_d = self.din('wukv', [128, DEPTH, 512])
        qn_d = self.din('qn', [128, DEPTH, 2])
        kvn_d = self.din('kvn', [128, DEPTH, 1])
        kvnrow_d = self.din('kvn_row', [1, DEPTH * 128])
        gvn_d = self.din('gvn', [128, DEPTH, 2])
        bsB_d = self.din('bsB', [128, DEPTH, 2, 128])
        wsT_d = self.din('wsT', [128, DEPTH, 4, 128])
        gbT_d = self.din('gbT', [36, DEPTH, 2])
        hnrow_d = self.din('hn_row', [1, DEPTH * 256])
        lamrow_d = self.din('lam_row', [1, DEPTH * 128])
        snrow_d = self.din('sn_row', [1, DEPTH * 64])
        cos_d = self.din('cosT', [128, 1024])
        sin_d = self.din('sinT', [128, 1024])
        ident_d = self.din('ident', [128, 128])
        mle_d = self.din('mask_le', [128, 128])
        mge_d = self.din('mask_ge', [128, 128])
        sel_d = self.din('sel', [36, 4, 128])
        cckv_d = self.din('c_ckvT', [DEPTH, 128, 256])
        ckr_d = self.din('c_krT', [DEPTH, 32, 256])
        cdk_d = self.din('c_dkT', [DEPTH, 96, 3, 256])
        cdv_d = self.din('c_dv', [128, DEPTH, 2, 4, 64])
        cC0_d = self.din('c_C0', [128, DEPTH, 2, 2, 65])
        cm0T_d = self.din('c_m0T', [36, DEPTH])
        cm0row_d = self.din('c_m0row', [1, DEPTH * 8])
        yT_d = self.dout('yT', [128, 8, T])
        o_ckv = self.dout('o_ckv', [2, DEPTH, 256, 128])
        o_kr = self.dout('o_kr', [2, DEPTH, 256, 32])
        o_dk = self.dout('o_dk', [2, DEPTH, 256, 256])
        o_dv = self.dout('o_dv', [2, DEPTH, 256, 256])
        o_C = self.dout('o_C', [2, DEPTH, 2, 4, 64, 64])
        o_n = self.dout('o_n', [2, DEPTH, 2, 4, 64])
        o_m = self.dout('o_m', [2, DEPTH, 2, 4])

        xT = self.sb('xT_sb', [128, 8, T], F32)
        hT = self.sb('hT_sb', [128, 8, T], BF16)
        AR_N = 43008
        arena = self.sb('arena', [128, AR_N], BF16)
        WSLOT = 3
        wbuf = [self.sb(f'wbuf{i}', [128, 4096], BF16) for i in range(WSLOT)]
        wbr = [self.sb(f'wbr{i}', [128, 1024], BF16) for i in range(2)]
        ones = self.sb('ones', [128, 128], BF16)
        identb = self.sb('identb', [128, 128], BF16)
        identf = self.sb('identf', [36, 36], F32)
        mle = self.sb('mle', [128, 128], BF16)
        mge = self.sb('mge', [128, 128], BF16)
        sel = self.sb('sel_sb', [36, 4, 128], F32)
        cosT = self.sb('cos_sb', [128, 1024], F32)
        sinT = self.sb('sin_sb', [128, 1024], F32)
        condT = self.sb('condT_sb', [128, 8, 2], F32)
        scond = self.sb('scond', [128, 8, 2], BF16)
        bmod = self.sb('bmod_sb', [128, DEPTH, 48], F32)
        ng = self.sb('ng_sb', [128, DEPTH, 4, 8], F32)
        bmerge = self.sb('bmerge_sb', [128, DEPTH, 8, 4], F32)
        modT = self.sb('modT', [128, 48, 2], F32)
        msc = self.sb('msc', [128, 6, 8, 2], F32)
        wuq = self.sb('wuq_sb', [128, 1, 2, 768], BF16)
        wukv = self.sb('wukv_sb', [128, 1, 512], BF16)
        qn = self.sb('qn_sb', [128, DEPTH, 2], F32)
        kvn = self.sb('kvn_sb', [128, DEPTH, 1], F32)
        kvnB = self.sb('kvnB', [128, 128], F32)
        gvn = self.sb('gvn_sb', [128, DEPTH, 2], F32)
        bsB = self.sb('bsB_sb', [128, 1, 2, 128], F32)
        wsT = self.sb('wsT_sb', [128, 1, 4, 128], BF16)
        gbT = self.sb('gbT_sb', [36, DEPTH, 2], F32)
        hnB = self.sb('hnB', [128, 256], F32)
        lamB = self.sb('lamB', [128, 128], F32)
        snB = self.sb('snB', [128, 64], F32)
        m0T = self.sb('m0T', [36, DEPTH], F32)
        m0B = self.sb('m0B', [128, DEPTH * 8], F32)
        C0 = self.sb('C0_sb', [128, DEPTH, 2, 2, 65], BF16)
        smalls = self.sb('smalls', [128, 64], F32)
        zero1 = self.sb('zero1', [128, 1], F32)
        ps = [es.enter_context(nc.psum_tensor(f'ps{i}', [128, 512], F32)) for i in range(8)]

        def carve(off_b, shape, dt):
            esz = 2 if dt == BF16 else 4
            n = int(np.prod(shape[1:]))
            a = arena[0:shape[0], off_b // 2: off_b // 2 + n * esz // 2]
            if dt != BF16:
                a = a.bitcast(dt)
            if len(shape) == 3:
                a = a.rearrange("p (a b) -> p a b", a=shape[1])
            elif len(shape) == 4:
                a = a.rearrange("p (a b c) -> p a b c", a=shape[1], b=shape[2])
            return a
        KB = 1024
        assert AR_N * 2 == 84 * KB
        ybr = carve(0, [128, 4, 2, T], BF16)
        mrg = carve(24 * KB, [128, 8, T], BF16)
        ffo = carve(0, [128, 8, T], F32)
        f1g = [carve(48 * KB + i * 12 * KB, [128, 4, T], BF16) for i in range(2)]
        yf = [carve(48 * KB + i * 12 * KB, [128, 8, 384], F32)[:, :, 0:384] for i in range(2)]
        sq = carve(72 * KB, [128, 8, 512], BF16)
        rstd = [carve(80 * KB + i * 2 * KB, [128, 512], F32) for i in range(2)]
        MS = 24 * KB

        self.wslot = 0

        self.wreq = []
        self.wissued = 0

        def w_issue(j):
            name, off, apl, ncols = self.wplan[j]
            src = bass.AP(self.ins[name].tensor, off, [list(x) for x in apl])
            i = j % WSLOT
            S.dma('pool', f'w{i}', wbuf[i][:, 0:ncols], src, writes=[f'wbuf{i}'])

        def wload(src3, ncols_total, pf=2):
            j = len(self.wreq)
            desc = (src3.name, int(src3.offset), tuple(tuple(x) for x in src3.ap), int(ncols_total))
            self.wreq.append(desc)
            i = j % WSLOT
            if self.wplan is None:
                S.dma('pool', f'w{i}', wbuf[i][:, 0:ncols_total], src3, writes=[f'wbuf{i}'])
            else:
                assert self.wplan[j] == desc, (j, self.wplan[j], desc)
                while self.wissued <= min(j + pf, len(self.wplan) - 1):
                    w_issue(self.wissued)
                    self.wissued += 1
            return wbuf[i], f'wbuf{i}'

        def tl(tt):
            return slice(tt * 512, (tt + 1) * 512)

        self.rs_i = 0

        def rms_rstd(src3, src_key, nk, dtot, n=512):
            i = self.rs_i
            self.rs_i ^= 1
            b = self.psum()
            kf = src_key if callable(src_key) else (lambda k: src_key)
            for k in range(nk):
                S.op('act', lambda e: e.activation(out=sq[:, k, 0:n], in_=src3[:, k, :], func=AF.Square),
                     reads=[kf(k)], writes=[f'sq{k}'])
                S.op('pe', lambda e: e.matmul(ps[b][:, 0:n], lhsT=ones[:, :], rhs=sq[:, k, 0:n],
                                              start=(k == 0), stop=(k == nk - 1)),
                     reads=[f'sq{k}', 'ones'], writes=[f'ps{b}'], inc=(k == nk - 1))
            self.rsqrt(rstd[i][:, 0:n], ps[b][:, 0:n], float(dtot * EPS), [f'ps{b}'], [f'rstd{i}'])
            return rstd[i], f'rstd{i}'

        def fm_mm(out_ap, okey, w3, wk, c0, M, rhs_fn, rkeys, nk=8):
            for k in range(nk):
                rk_ = [(x + f'_{k}') if x.startswith('hT') else x for x in rkeys]
                S.op('pe', lambda e: e.matmul(out_ap, lhsT=w3[:, k, c0:c0 + M], rhs=rhs_fn(k),
                                              start=(k == 0), stop=(k == nk - 1)),
                     reads=[wk] + rk_, writes=[okey], inc=(k == nk - 1))

        def tm_mm(out_ap, okey, w3, wk, c0, N, tok0, nk=8):
            tt = tok0 // 512
            for k in range(nk):
                S.op('pe', lambda e: e.matmul(out_ap, lhsT=hT[:, k, tok0:tok0 + 128], rhs=w3[:, k, c0:c0 + N],
                                              start=(k == 0), stop=(k == nk - 1)),
                     reads=[wk, f'hT{tt}_{k}'], writes=[okey], inc=(k == nk - 1))

        def to_fm(src_bf, skey, br, blk):
            b = self.psum('lo')
            pb_ = ps[b][:, :].bitcast(BF16)
            for kk in range(2):
                S.op('pe', lambda e: e.transpose(out=pb_[:, kk * 128:(kk + 1) * 128], in_=src_bf[:, kk * 128:(kk + 1) * 128],
                                                 identity=identb[:, :]),
                     reads=[skey, 'identb'], writes=[f'ps{b}'], inc=(kk == 1))
            S.op('act', lambda e: e.copy(out=ybr[:, br, :, blk * 128:(blk + 1) * 128],
                                         in_=pb_[:, 0:256].rearrange("p (k t) -> p k t", k=2)),
                 reads=[f'ps{b}'], writes=['ybr'])

        S.op('pool', lambda e: e.memset(ones[:], 1.0), writes=['ones'])
        S.op('pool', lambda e: e.memset(zero1[:], 0.0), writes=['zero1'])
        cbt = self.sb('cbt', [128, 8], F32)
        self._cb = {}
        for i, c in enumerate([D * EPS, 256 * EPS, 128 * EPS, 64 * EPS, 1.0]):
            S.op('pool', lambda e: e.memset(cbt[:, i:i + 1], float(c)), writes=['cbias'])
            self._cb[round(float(c), 12)] = cbt[:, i:i + 1]
        S.dma('sp', 'x', xT[:], xT_d, writes=[f'xT{a}_{b}' for a in range(NT) for b in range(8)])
        cl = [(condT, condT_d, 'condT'), (bmod, bmod_d, 'bmod'), (ng, ng_d, 'ng'), (bmerge, bmerge_d, 'bmerge'),
              (qn, qn_d, 'qn'), (kvn, kvn_d, 'kvn'), (gvn, gvn_d, 'gvn'), (gbT, gbT_d, 'gbT'),
              (cosT, cos_d, 'cosT'), (sinT, sin_d, 'sinT'), (sel, sel_d, 'sel'), (m0T, cm0T_d, 'm0T'),
              (identf, ident_d[0:36, 0:36], 'identf')]
        for (dst, src, key) in cl:
            S.dma('sp', 'c_' + key, dst[:], src, writes=[key])
        for (dst, src, key, n) in [(m0B, cm0row_d, 'm0B', DEPTH * 8)]:
            S.dma('sp', 'c_' + key, dst[:], src.partition_broadcast(128), writes=[key])
        for (dst, src, key) in [(identb, ident_d, 'identb'), (mle, mle_d, 'mle'), (mge, mge_d, 'mge'),
                                (C0, cC0_d, 'C0')]:
            S.dma('pool', 'c_' + key, dst[:], src, writes=[key])
        S.op('act', lambda e: e.activation(out=scond[:], in_=condT[:], func=AF.Silu),
             reads=['condT'], writes=['scond'])

        def norm_rms(tt):
            return rms_rstd(xT[:, :, tl(tt)], (lambda k, tt=tt: f'xT{tt}_{k}'), 8, D)

        def norm_apply(l, which, tt, rr_):
            ia, ib = (0, 1) if which == 0 else (3, 4)
            c = 0 if tt == 0 else 1
            r, rk = rr_
            for k in range(8):
                t_ = carve(18 * KB + (k % 2) * 2 * KB, [128, 512], F32)
                tk = f'nm_tmp{k % 2}'
                S.op('dve', lambda e: e.scalar_tensor_tensor(
                    out=t_, in0=xT[:, k, tl(tt)], scalar=msc[:, ia, k, c:c + 1], in1=r[:, :],
                    op0=ALU.mult, op1=ALU.mult), reads=[f'xT{tt}_{k}', 'msc', rk], writes=[tk])
                S.op('act', lambda e: e.activation(out=hT[:, k, tl(tt)], in_=t_, func=AF.Identity,
                                                   bias=msc[:, ib, k, c:c + 1], scale=1.0),
                     reads=[tk, 'msc'], writes=[f'hT{tt}_{k}'])

        def norm_mod_tile(l, which, tt):
            norm_apply(l, which, tt, norm_rms(tt))

        def norm_mod(l, which):
            for tt in range(NT):
                norm_mod_tile(l, which, tt)

        def resid_apply(src3, skey, tt, ig, tmp_off, rr_):
            c = 0 if tt == 0 else 1
            kf = skey if callable(skey) else (lambda k: skey)
            r, rk = rr_
            for k in range(8):
                t_ = carve(tmp_off + (k % 2) * 2 * KB, [128, 512], F32)
                tk = f'rs_tmp{k % 2}'
                S.op('dve', lambda e: e.scalar_tensor_tensor(
                    out=t_, in0=src3[:, k, :], scalar=msc[:, ig, k, c:c + 1], in1=r[:, :],
                    op0=ALU.mult, op1=ALU.mult), reads=[kf(k), 'msc', rk], writes=[tk])
                S.op('dve', lambda e: e.tensor_tensor(out=xT[:, k, tl(tt)], in0=xT[:, k, tl(tt)], in1=t_,
                                                      op=ALU.add), reads=[tk, f'xT{tt}_{k}'], writes=[f'xT{tt}_{k}'])

        def resid(src3, skey, tt, ig, tmp_off):
            resid_apply(src3, skey, tt, ig, tmp_off, rms_rstd(src3, skey, 8, D))

        modN = self.sb('modN', [128, DEPTH, 48, 2], F32)

        def p0_step(lay, g, scratch_off):
            wt, wk = wload(wmod_d[lay, g].rearrange("p k j -> p (k j)"), 4096)
            w3 = wt[:, 0:4096].rearrange("p (k j) -> p k j", k=8)
            bg = self.psum('lo')
            for k in range(8):
                S.op('pe', lambda e: e.matmul(ps[bg][0:2, :], lhsT=scond[:, k, :], rhs=w3[:, k, :], start=(k == 0), stop=(k == 7)),
                     reads=[wk, 'scond'], writes=[f'ps{bg}'], inc=(k == 7))
            mt = carve(scratch_off, [2, 512], F32)
            S.op('act', lambda e: e.copy(out=mt, in_=ps[bg][0:2, :]), reads=[f'ps{bg}'], writes=['mtmp'])
            bt = self.psum('lo')
            for jj in range(4):
                S.op('pe', lambda e: e.matmul(ps[bt][:, 2 * jj:2 * jj + 2], lhsT=mt[0:2, jj * 128:(jj + 1) * 128], rhs=identf[0:2, 0:2],
                                              start=True, stop=True), reads=['mtmp', 'identf'], writes=[f'ps{bt}'], inc=(jj == 3))
            pm = ps[bt][:, 0:8].rearrange("p (j c) -> p j c", c=2)
            S.op('dve', lambda e: e.tensor_tensor(out=modN[:, lay, 4 * g:4 * g + 4, :], in0=pm,
                                                  in1=bmod[:, lay, 4 * g:4 * g + 4].unsqueeze(2).to_broadcast([128, 4, 2]), op=ALU.add),
                 reads=[f'ps{bt}', 'bmod'], writes=['modN'])

        self.p0_queue = [(lay, g) for lay in range(self.nlayers) for g in range(12)]

        def p0_flush(lay, gmax, scratch_off):
            while self.p0_queue and (self.p0_queue[0][0] < lay or (self.p0_queue[0][0] == lay and self.p0_queue[0][1] <= gmax)):
                la, g = self.p0_queue.pop(0)
                p0_step(la, g, scratch_off)

        def p0_hook(scratch_off):
            if self.p0_queue:
                la, g = self.p0_queue.pop(0)
                p0_step(la, g, scratch_off)
        self.p0_hook = p0_hook

        def msc_derive(l, which):
            for c in range(2):
                for (dst, isc, ign) in (((0, 1, 0),) if which == 0 else ((3, 4, 2),)):
                    S.op('dve', lambda e: e.scalar_tensor_tensor(
                        out=msc[:, dst, :, c], in0=modN[:, l, isc * 8:(isc + 1) * 8, c], scalar=1.0,
                        in1=ng[:, l, ign, :], op0=ALU.add, op1=ALU.mult), reads=['modN', 'ng'], writes=['msc'])
                    S.op('dve', lambda e: e.tensor_scalar(
                        out=msc[:, dst, :, c], in0=msc[:, dst, :, c], scalar1=32.0, scalar2=None, op0=ALU.mult),
                        reads=['msc'], writes=['msc'])
                for (dst, ish) in (((1, 0),) if which == 0 else ((4, 3),)):
                    S.op('dve', lambda e: e.tensor_copy(out=msc[:, dst, :, c], in_=modN[:, l, ish * 8:(ish + 1) * 8, c]),
                         reads=['modN'], writes=['msc'])
                if which == 1:
                    for (dst, ig, ign) in ((2, 2, 1), (5, 5, 3)):
                        S.op('dve', lambda e: e.scalar_tensor_tensor(
                            out=msc[:, dst, :, c], in0=modN[:, l, ig * 8:(ig + 1) * 8, c], scalar=32.0,
                            in1=ng[:, l, ign, :], op0=ALU.mult, op1=ALU.mult), reads=['modN', 'ng'], writes=['msc'])

        try:
          self.cut('init')
          for l in range(self.nlayers):
              S.dma('sp', 'c_bsB', bsB[:, 0], bsB_d[:, l], writes=['bsB'])
              S.dma('sp', 'c_kvnB', kvnB[:], kvnrow_d[:, l * 128:(l + 1) * 128].partition_broadcast(128), writes=['kvnB'])
              S.dma('sp', 'c_hnB', hnB[:], hnrow_d[:, l * 256:(l + 1) * 256].partition_broadcast(128), writes=['hnB'])
              S.dma('sp', 'c_lamB', lamB[:], lamrow_d[:, l * 128:(l + 1) * 128].partition_broadcast(128), writes=['lamB'])
              S.dma('sp', 'c_snB', snB[:], snrow_d[:, l * 64:(l + 1) * 64].partition_broadcast(128), writes=['snB'])
              S.dma('pool', 'c_wuq', wuq[:, 0], wuq_d[:, l], writes=['wuq'])
              S.dma('pool', 'c_wukv', wukv[:, 0], wukv_d[:, l], writes=['wukv'])
              S.dma('pool', 'c_wsT', wsT[:, 0], wsT_d[:, l], writes=['wsT'])
              p0_flush(l, 3, MS)
              msc_derive(l, 0)

              nrm = {0: norm_rms(0), 1: norm_rms(1)}

              def p1_tile(tt):
                  norm_apply(l, 0, tt, nrm[tt])
                  if tt == 0:
                      nrm[2] = norm_rms(2)
              self.mixer_B(l, locals(), pre_tile=p1_tile)
              S.barrier()
              self.cut(f'B_{l}')
              for samp in (False, True):
                  self.rsqrt_lnexp = True
                  self.mixer_A(l, locals(), samp)
                  S.barrier()
                  self.cut(f'A{int(samp)}_{l}')
                  self.mixer_D(l, locals(), samp)
                  S.barrier()
                  self.cut(f'D{int(samp)}_{l}')
                  self.mixer_C(l, locals(), samp)
                  S.barrier()
                  self.cut(f'C{int(samp)}_{l}')
              self.rsqrt_lnexp = False

              if l == 0 and 'ybr' in self.debug:
                  o = self.dout('dbg_ybr', [128, 8 * T])
                  S.dma('pool', 'dbg_ybr', o, arena[:, 0:8 * T], reads=['ybr'], writes=['dbgo1'])
                  self.out_keys.append('dbg_ybr')
              gsb_all = [carve(MS + 24 * KB + i * 2 * KB, [128, 512], F32) for i in range(8)]
              self.gs_i = 0
              for n in range(8):
                  wt, wk = wload(wmerge_d[l, n].rearrange("p k j -> p (k j)"), 4096)
                  wm3 = wt[:, 0:4096].rearrange("p (k j) -> p k j", k=8)
                  ib = n % 2
                  if n == 0:
                      S.dma('pool', 'wbr0', wbr[0][:, :], wbranch_d[l, 0].rearrange("p b k j -> p (b k j)"), writes=['wbr0'])
                  if n + 1 < 8:
                      S.dma('pool', f'wbr{1 - ib}', wbr[1 - ib][:, :], wbranch_d[l, n + 1].rearrange("p b k j -> p (b k j)"),
                            writes=[f'wbr{1 - ib}'])
                  wb4 = wbr[ib][:, :].rearrange("p (b k j) -> p b k j", b=4, k=2)
                  for tt in range(NT):
                      self.gs_i ^= 1
                      gsb = gsb_all[4 * self.gs_i:4 * self.gs_i + 4]
                      go = 4 * self.gs_i
                      for br in range(4):
                          bg = self.psum()
                          fm_mm(ps[bg][:, :], f'ps{bg}', wm3, wk, br * 128, 128, lambda k: hT[:, k, tl(tt)], [f'hT{tt}'])
                          S.op('act', lambda e: e.activation(
                              out=gsb[br], in_=ps[bg][:, :], func=AF.Sigmoid, bias=bmerge[:, l, n, br:br + 1], scale=1.0),
                              reads=[f'ps{bg}', 'bmerge'], writes=[f'gsb{go + br}'])
                          bb = self.psum()
                          for kk in range(2):
                              S.op('pe', lambda e: e.matmul(ps[bb][:, :], lhsT=wb4[:, br, kk, :], rhs=ybr[:, br, kk, tl(tt)],
                                                            start=(kk == 0), stop=(kk == 1)),
                                   reads=[f'wbr{ib}', 'ybr'], writes=[f'ps{bb}'], inc=(kk == 1))
                          S.op('dve', lambda e: e.tensor_tensor(out=gsb[br], in0=ps[bb][:, :], in1=gsb[br], op=ALU.mult),
                               reads=[f'ps{bb}', f'gsb{go + br}'], writes=[f'gsb{go + br}'])
                      S.op('pool', lambda e: e.tensor_tensor(out=gsb[0], in0=gsb[0], in1=gsb[1], op=ALU.add),
                           reads=[f'gsb{go}', f'gsb{go + 1}'], writes=[f'gsb{go}'])
                      S.op('pool', lambda e: e.tensor_tensor(out=gsb[2], in0=gsb[2], in1=gsb[3], op=ALU.add),
                           reads=[f'gsb{go + 2}', f'gsb{go + 3}'], writes=[f'gsb{go + 2}'])
                      S.op('pool', lambda e: e.tensor_tensor(out=mrg[:, n, tl(tt)], in0=gsb[0], in1=gsb[2], op=ALU.add),
                           reads=[f'gsb{go}', f'gsb{go + 2}'], writes=['mrg'])
              S.barrier()

              if l == 0 and 'mrg' in self.debug:
                  o = self.dout('dbg_mrg', [128, 8 * T])
                  S.dma('pool', 'dbg_mrg', o, arena[:, 8 * T:16 * T], reads=['mrg'], writes=['dbgo2'])
                  self.out_keys.append('dbg_mrg')
              self.cut(f'P4_{l}')
              wo = []
              for g in range(2):
                  wt, wk = wload(wout_d[l, g].rearrange("p k j -> p (k j)"), 4096, pf=(2 if g == 0 else 1))
                  wo.append((wt[:, 0:4096].rearrange("p (k j) -> p k j", k=8), wk))
              p0_flush(l, 11, 64 * KB + 4 * KB)
              msc_derive(l, 1)
              yfs = [carve(0, [128, 8, 512], F32), carve(48 * KB, [128, 8, 512], F32)]

              def p5_mm(tt):
                  yfull = yfs[tt % 2]
                  yk = f'yfull{tt % 2}'
                  for n in range(8):
                      w3, wk = wo[n // 4]
                      b = self.psum()
                      fm_mm(ps[b][:, :], f'ps{b}', w3, wk, (n % 4) * 128, 128, lambda k: mrg[:, k, tl(tt)], ['mrg'])
                      S.op('act', lambda e: e.copy(out=yfull[:, n, :], in_=ps[b][:, :]), reads=[f'ps{b}'], writes=[f'{yk}_{n}'])

              def p5_rms(tt):
                  return rms_rstd(yfs[tt % 2], (lambda k, tt=tt: f'yfull{tt % 2}_{k}'), 8, D)

              def p5_app(tt, rr_):
                  resid_apply(yfs[tt % 2], (lambda k, tt=tt: f'yfull{tt % 2}_{k}'), tt, 2, 64 * KB, rr_)
                  if l == 0 and 'x1' in self.debug and tt == NT - 1:
                      o = self.dout('dbg_x1', [128, 8, T])
                      S.dma('sp', 'dbg_x1', o, xT[:], reads=[f'xT{a}_{b}' for a in range(NT) for b in range(8)], writes=['dbgo3'])
                      self.out_keys.append('dbg_x1')

              p5_mm(0)
              p5_mm(1)
              p5_app(0, p5_rms(0))
              p5_mm(2)
              n0 = norm_rms(0)
              r1 = p5_rms(1)
              norm_apply(l, 1, 0, n0)
              p5_app(1, r1)
              n1 = norm_rms(1)
              r2 = p5_rms(2)
              norm_apply(l, 1, 1, n1)
              p5_app(2, r2)
              norm_apply(l, 1, 2, norm_rms(2))
              self.cut(f'P5_{l}')
              S.barrier()

              for g in range(8):
                  wt, wk = wload(wff1_d[l, g].rearrange("p k j -> p (k j)"), 4096)
                  w3 = wt[:, 0:4096].rearrange("p (k j) -> p k j", k=8)
                  fi = g % 2
                  for jj in range(4):
                      for tt in range(NT):
                          b = self.psum()
                          fm_mm(ps[b][:, :], f'ps{b}', w3, wk, jj * 128, 128, lambda k: hT[:, k, tl(tt)], [f'hT{tt}'])
                          self.ft_i = (getattr(self, 'ft_i', 0) + 1) % 2
                          ftmp = carve(72 * KB + self.ft_i * 2 * KB, [128, 512], F32)
                          S.op('act', lambda e: e.activation(out=ftmp, in_=ps[b][:, :], func=AF.Relu),
                               reads=[f'ps{b}'], writes=[f'sq{2 * self.ft_i}', f'sq{2 * self.ft_i + 1}'])
                          S.op('act', lambda e: e.activation(out=f1g[fi][:, jj, tl(tt)], in_=ftmp, func=AF.Square),
                               reads=[f'sq{2 * self.ft_i}', f'sq{2 * self.ft_i + 1}'], writes=[f'f1g{fi}'])
                  wt2, wk2 = wload(wff2_d[l, g].rearrange("p k j -> p (k j)"), 4096)
                  w23 = wt2[:, 0:4096].rearrange("p (k j) -> p k j", k=4)
                  for n in range(8):
                      for tt in range(NT):
                          b = self.psum()
                          fm_mm(ps[b][:, :], f'ps{b}', w23, wk2, n * 128, 128, lambda k: f1g[fi][:, k, tl(tt)], [f'f1g{fi}'], nk=4)
                          if g == 0:
                              S.op('act', lambda e: e.copy(out=ffo[:, n, tl(tt)], in_=ps[b][:, :]),
                                   reads=[f'ps{b}'], writes=[f'ffo{tt}'])
                          else:
                              S.op('dve', lambda e: e.tensor_tensor(out=ffo[:, n, tl(tt)], in0=ps[b][:, :], in1=ffo[:, n, tl(tt)],
                                                                    op=ALU.add),
                                   reads=[f'ps{b}', f'ffo{tt}'], writes=[f'ffo{tt}'])
              S.barrier()
              for tt in range(NT):
                  resid(ffo[:, :, tl(tt)], f'ffo{tt}', tt, 5, 48 * KB)
              S.barrier()

        except StopBuild:
            S.barrier()
        self.store('out_y', yT_d, xT[:], [f'xT{a}_{b}' for a in range(NT) for b in range(8)])
        S.finish(self.out_keys)
        return nc

    def mixer_B(self, l, L, pre_tile=None):
        S = self.S
        g_ = lambda n: L[n]
        ps, hT, ybr, carve, wload, fm_mm, tm_mm, win_d = (g_('ps'), g_('hT'), g_('ybr'), g_('carve'), g_('wload'),
                                                             g_('fm_mm'), g_('tm_mm'), g_('win_d'))
        gvn, bsB, wsT, smalls, tl, MS, KB = g_('gvn'), g_('bsB'), g_('wsT'), g_('smalls'), g_('tl'), g_('MS'), g_('KB')
        gB1, gB2 = WG_ORDER.index('B1'), WG_ORDER.index('B2')
        wt1, wk1 = wload(win_d[l, gB1][:, :, 0:256], 8 * 256)
        w31 = wt1[:, 0:2048].rearrange("p (k j) -> p k j", k=8)
        wt2, wk2 = wload(win_d[l, gB2][:, :, 0:256], 8 * 256, pf=1)
        w32 = wt2[:, 0:2048].rearrange("p (k j) -> p k j", k=8)
        vr = [carve(MS + i * 512, [128, 256], BF16) for i in range(2)]
        junks = [carve(MS + 2 * KB + i * KB, [128, 256], F32) for i in range(2)]
        tmpms = [carve(MS + 4 * KB + i * KB, [128, 2, 128], F32) for i in range(2)]
        for tt in range(NT):
            if pre_tile is not None:
                pre_tile(tt)
            ub = [5 + (2 * tt) % 3, 5 + (2 * tt + 1) % 3]
            for kk in range(2):
                fm_mm(ps[ub[kk]][:, :], f'ps{ub[kk]}', w31, wk1, kk * 128, 128, lambda k: hT[:, k, tl(tt)], [f'hT{tt}'])
            for bi in range(4):
                blk = tt * 4 + bi
                vb = self.psum('lo')
                tm_mm(ps[vb][:, 0:256], f'ps{vb}', w32, wk2, 0, 256, blk * 128)
                pq = bi % 2
                junk, tmpm = junks[pq], tmpms[pq]
                ss = smalls[:, 2 * pq:2 * pq + 1]
                rs_ = smalls[:, 2 * pq + 1:2 * pq + 2]
                S.op('act', lambda e: e.activation(out=junk, in_=ps[vb][:, 0:256], func=AF.Square, accum_out=ss),
                     reads=[f'ps{vb}'], writes=[f'junkB{pq}', f'ssB{pq}'])
                self.rsqrt(rs_, ss, float(256 * EPS), [f'ssB{pq}'], [f'rsB{pq}'])
                v_ = vr[bi % 2]
                S.op('dve', lambda e: e.tensor_scalar(out=v_, in0=ps[vb][:, 0:256], scalar1=rs_, scalar2=16.0,
                                                      op0=ALU.mult, op1=ALU.mult),
                     reads=[f'ps{vb}', f'rsB{pq}'], writes=[f'vrB{bi % 2}'])
                mb = self.psum('lo')
                for g in range(4):
                    S.op('pe', lambda e: e.matmul(ps[mb][(g % 2) * 64:(g % 2) * 64 + 64, (g // 2) * 128:(g // 2) * 128 + 128],
                                                  lhsT=v_[:, g * 64:(g + 1) * 64], rhs=wsT[:, 0, g, :], start=True, stop=True),
                         reads=[f'vrB{bi % 2}', 'wsT'], writes=[f'ps{mb}'], inc=(g == 3))
                for kk in range(2):
                    S.op('dve', lambda e: e.scalar_tensor_tensor(
                        out=tmpm[:, kk, :], in0=ps[mb][:, kk * 128:(kk + 1) * 128], scalar=gvn[:, l, kk:kk + 1],
                        in1=bsB[:, 0, kk, :], op0=ALU.mult, op1=ALU.add),
                        reads=[f'ps{mb}', 'gvn', 'bsB'], writes=[f'tmpmB{pq}'])
                    S.op('dve', lambda e: e.tensor_tensor(
                        out=ybr[:, 1, kk, blk * 128:(blk + 1) * 128], in0=ps[ub[kk]][:, bi * 128:(bi + 1) * 128],
                        in1=tmpm[:, kk, :], op=ALU.mult),
                        reads=[f'ps{ub[kk]}', f'tmpmB{pq}'], writes=['ybr'])

    def attn_core(self, L, streams, nkc, Nt, scale):
        S = self.S
        ps, carve, MS, KB = L['ps'], L['carve'], L['MS'], L['KB']
        nb = Nt // 128
        qs = [st[0]() for st in streams]

        def stage1(si, sc):
            qa, qk = qs[si]
            ka, kk_ = streams[si][1](sc)
            sb_ = self.psum('lo')
            S.op('pe', lambda e: e.matmul(ps[sb_][:, 0:Nt], lhsT=ka, rhs=qa, start=True, stop=True),
                 reads=[kk_, qk], writes=[f'ps{sb_}'])
            pi = self.pt_i
            self.pt_i = (self.pt_i + 1) % 6
            pT = carve(MS + 42 * KB + pi * KB, [128, 512], BF16)
            S.op('act', lambda e: e.activation(out=pT[:, 0:Nt], in_=ps[sb_][:, 0:Nt], func=AF.Exp, scale=float(scale)),
                 reads=[f'ps{sb_}'], writes=[f'pT{pi}'])
            return pT, pi

        cur = [stage1(si, 0) for si in range(len(streams))]
        for sc in range(nkc):
            nxt = [stage1(si, sc + 1) if sc + 1 < nkc else None for si in range(len(streams))]
            for si, st in enumerate(streams):
                pT, pi = cur[si]
                va, vk = st[2](sc)
                accb = st[3]
                for tb in range(nb):
                    S.op('pe', lambda e: e.matmul(ps[accb][:, tb * 65:(tb + 1) * 65], lhsT=pT[:, tb * 128:(tb + 1) * 128], rhs=va,
                                                  start=(sc == 0 and tb == 0), stop=(sc == nkc - 1 and tb == nb - 1),
                                                  skip_group_check=True),
                         reads=[f'pT{pi}', vk], writes=[f'ps{accb}'], inc=(tb == nb - 1))
            cur = nxt

    @staticmethod
    def geo(samp):
        if samp:
            return dict(t0=512, nt=1024, tiles=[1, 2], seqs=[(0, 1024)], kofs=256, nkb=10)
        return dict(t0=0, nt=512, tiles=[0], seqs=[(0, 256), (256, 256)], kofs=0, nkb=4)

    def mixer_A(self, l, L, samp):
        S = self.S
        g_ = lambda n: L[n]
        ps, hT, carve, wload, fm_mm, tm_mm, win_d, tl, MS, KB = (g_('ps'), g_('hT'), g_('carve'), g_('wload'), g_('fm_mm'),
                                                                  g_('tm_mm'), g_('win_d'), g_('tl'), g_('MS'), g_('KB'))
        wuq, wukv, qn, kvn, kvnB, cosT, sinT, smalls = (g_('wuq'), g_('wukv'), g_('qn'), g_('kvn'), g_('kvnB'), g_('cosT'),
                                                         g_('sinT'), g_('smalls'))
        rms_rstd, to_fm = g_('rms_rstd'), g_('to_fm')
        G = self.geo(samp)
        t0, tiles, kofs, nkb = G['t0'], G['tiles'], G['kofs'], G['nkb']
        nk = nkb * 128
        self.pt_i = 0
        qT = carve(MS, [96, 4, 1024], BF16)
        ckvnT = carve(MS + 8 * KB, [128, 1280], BF16)
        krT = carve(MS + 8 * KB + 2560, [96, 1280], BF16)
        KT = carve(MS + 13 * KB, [96, 4, 1280], BF16)
        cqf = carve(MS + 13 * KB, [128, 2, 512], F32)
        cqn = carve(MS + 17 * KB, [128, 2, 512], BF16)
        ckf = carve(MS + 19 * KB, [128, 1, 512], F32)
        vaug = carve(MS + 23 * KB, [128, 10, 4, 65], BF16)
        krf = carve(MS + 28 * KB + 512, [96, 1024], F32)
        rt = [carve(MS + 32 * KB + 512 + i * KB, [96, 256], F32) for i in range(2)]
        ya = [carve(MS + 34 * KB + 512 + i * 512, [128, 256], BF16) for i in range(4)]
        stage = carve(MS + 36 * KB + 512, [128, 160], F32)

        if samp:
            S.dma('pool', 'ctxA0', ckvnT[:, 0:256], L['cckv_d'][l], writes=['ckvnT'])
            S.dma('pool', 'ctxA1', krT[64:96, 0:256], L['ckr_d'][l], writes=['krT'])
        gA1, gA2, gA3 = WG_ORDER.index('A1'), WG_ORDER.index('A2'), WG_ORDER.index('A3')
        wt, wk = wload(win_d[l, gA1][:, :, 0:480], 8 * 480)
        w3 = wt[:, 0:3840].rearrange("p (k j) -> p k j", k=8)
        for kk in range(2):
            S.op('dve', lambda e: e.tensor_scalar(out=smalls[:, 2 + kk:3 + kk], in0=qn[:, l, kk:kk + 1], scalar1=16.0,
                                                  scalar2=None, op0=ALU.mult), reads=['qn'], writes=['qn16'])
        S.op('dve', lambda e: e.tensor_scalar(out=smalls[:, 4:5], in0=kvn[:, l, 0:1], scalar1=float(math.sqrt(128.0)),
                                              scalar2=None, op0=ALU.mult), reads=['kvn'], writes=['kvn11'])
        for tt in tiles:
            lc = (tt - tiles[0]) * 512
            kc = kofs + lc
            rh = lambda k: hT[:, k, tl(tt)]
            for kk in range(2):
                b = self.psum('lo')
                fm_mm(ps[b][:, :], f'ps{b}', w3, wk, kk * 128, 128, rh, [f'hT{tt}'])
                S.op('act', lambda e: e.copy(out=cqf[:, kk, :], in_=ps[b][:, :]), reads=[f'ps{b}'], writes=['cqf'])
            r, rk = rms_rstd(cqf[:, :, :], 'cqf', 2, 256)
            for kk in range(2):
                S.op('dve', lambda e: e.scalar_tensor_tensor(out=cqn[:, kk, :], in0=cqf[:, kk, :], scalar=smalls[:, 2 + kk:3 + kk],
                                                             in1=r[:, :], op0=ALU.mult, op1=ALU.mult),
                     reads=['cqf', 'qn16', rk], writes=['cqn'])
            for h in range(4):
                bq = self.psum('lo')
                for kk in range(2):
                    S.op('pe', lambda e: e.matmul(ps[bq][0:96, :], lhsT=wuq[:, 0, kk, h * 96:(h + 1) * 96], rhs=cqn[:, kk, :],
                                                  start=(kk == 0), stop=(kk == 1)),
                         reads=['wuq', 'cqn'], writes=[f'ps{bq}'], inc=(kk == 1))
                S.op('act', lambda e: e.copy(out=qT[0:64, h, lc:lc + 512], in_=ps[bq][0:64, :]), reads=[f'ps{bq}'], writes=['qTA'])
                if not samp:
                    S.op('act', lambda e: e.copy(out=qT[64:96, h, lc:lc + 512], in_=ps[bq][64:96, :]), reads=[f'ps{bq}'], writes=['qTA'])
                else:
                    bs_ = self.psum('lo')
                    for kk in range(2):
                        S.op('pe', lambda e: e.matmul(ps[bs_][0:96, :], lhsT=wuq[:, 0, kk, 384 + h * 96:384 + (h + 1) * 96],
                                                      rhs=cqn[:, kk, :], start=(kk == 0), stop=(kk == 1)),
                             reads=['wuq', 'cqn'], writes=[f'ps{bs_}'], inc=(kk == 1))
                    for hh in range(2):
                        cs = slice(hh * 256, (hh + 1) * 256)
                        p0 = lc + hh * 256
                        S.op('dve', lambda e: e.tensor_tensor(out=rt[0][64:96, :], in0=ps[bq][64:96, cs], in1=cosT[64:96, p0:p0 + 256],
                                                              op=ALU.mult), reads=[f'ps{bq}', 'cosT'], writes=['rt0'])
                        S.op('dve', lambda e: e.tensor_tensor(out=rt[1][64:96, :], in0=ps[bs_][64:96, cs], in1=sinT[64:96, p0:p0 + 256],
                                                              op=ALU.mult), reads=[f'ps{bs_}', 'sinT'], writes=['rt1'])
                        S.op('pool', lambda e: e.tensor_tensor(out=qT[64:96, h, p0:p0 + 256], in0=rt[0][64:96, :], in1=rt[1][64:96, :],
                                                               op=ALU.add), reads=['rt0', 'rt1'], writes=['qTA'])
            b = self.psum('lo')
            fm_mm(ps[b][:, :], f'ps{b}', w3, wk, 256, 128, rh, [f'hT{tt}'])
            S.op('act', lambda e: e.copy(out=ckf[:, 0, :], in_=ps[b][:, :]), reads=[f'ps{b}'], writes=['ckf'])
            r, rk = rms_rstd(ckf[:, :, :], 'ckf', 1, 128)
            S.op('dve', lambda e: e.scalar_tensor_tensor(out=ckvnT[:, kc:kc + 512], in0=ckf[:, 0, :], scalar=smalls[:, 4:5],
                                                         in1=r[:, :], op0=ALU.mult, op1=ALU.mult),
                 reads=['ckf', 'kvn11', rk], writes=['ckvnT'])
            b = self.psum('lo')
            fm_mm(ps[b][0:96, :], f'ps{b}', w3, wk, 384, 96, rh, [f'hT{tt}'])
            if not samp:
                S.op('act', lambda e: e.copy(out=krT[64:96, kc:kc + 512], in_=ps[b][64:96, :]), reads=[f'ps{b}'], writes=['krT'])
            else:
                S.op('act', lambda e: e.copy(out=krf[64:96, lc:lc + 512], in_=ps[b][64:96, :]), reads=[f'ps{b}'], writes=['krf'])
        if samp:
            wt, wk = wload(win_d[l, gA2][:, :, 0:96], 8 * 96)
            w3 = wt[:, 0:768].rearrange("p (k j) -> p k j", k=8)
            for tt in tiles:
                lc = (tt - tiles[0]) * 512
                b = self.psum('lo')
                fm_mm(ps[b][0:96, :], f'ps{b}', w3, wk, 0, 96, lambda k: hT[:, k, tl(tt)], [f'hT{tt}'])
                for hh in range(2):
                    p0 = lc + hh * 256
                    S.op('pool', lambda e: e.tensor_tensor(out=rt[0][64:96, :], in0=krf[64:96, p0:p0 + 256], in1=cosT[64:96, p0:p0 + 256],
                                                           op=ALU.mult), reads=['krf', 'cosT'], writes=['rt0'])
                    S.op('dve', lambda e: e.tensor_tensor(out=rt[1][64:96, :], in0=ps[b][64:96, hh * 256:(hh + 1) * 256],
                                                          in1=sinT[64:96, p0:p0 + 256], op=ALU.mult),
                         reads=[f'ps{b}', 'sinT'], writes=['rt1'])
                    S.op('pool', lambda e: e.tensor_tensor(out=krT[64:96, 256 + p0:256 + p0 + 256], in0=rt[0][64:96, :],
                                                           in1=rt[1][64:96, :], op=ALU.add), reads=['rt0', 'rt1'], writes=['krT'])
        else:
            wt, wk = wload(win_d[l, gA3][:, :, 0:160], 8 * 160)
            w3 = wt[:, 0:1280].rearrange("p (k j) -> p k j", k=8)
            for blk in range(4):
                b = self.psum('lo')
                tm_mm(ps[b][:, 0:160], f'ps{b}', w3, wk, 0, 160, blk * 128)
                S.op('act', lambda e: e.activation(out=stage[:, 0:128], in_=ps[b][:, 0:128], func=AF.Square, accum_out=smalls[:, 5:6]),
                     reads=[f'ps{b}'], writes=['stageA', 'ssA'])
                self.rsqrt(smalls[:, 6:7], smalls[:, 5:6], float(128 * EPS), ['ssA'], ['rsA'])
                S.op('dve', lambda e: e.tensor_scalar(out=stage[:, 0:128], in0=ps[b][:, 0:128], scalar1=smalls[:, 6:7],
                                                      scalar2=float(math.sqrt(128.0)), op0=ALU.mult, op1=ALU.mult),
                     reads=[f'ps{b}', 'rsA', 'stageA'], writes=['stageA'])
                S.op('dve', lambda e: e.tensor_tensor(out=stage[:, 0:128], in0=stage[:, 0:128], in1=kvnB[:, 0:128],
                                                      op=ALU.mult), reads=['stageA', 'kvnB'], writes=['stageA'])
                S.op('act', lambda e: e.copy(out=stage[:, 128:160], in_=ps[b][:, 128:160]), reads=[f'ps{b}', 'stageA'], writes=['stageA'])
                sq_, bl = blk // 2, blk % 2
                self.store('o_ckv', L['o_ckv'][sq_, l, bl * 128:(bl + 1) * 128, :], stage[:, 0:128], ['stageA'])
                self.store('o_kr', L['o_kr'][sq_, l, bl * 128:(bl + 1) * 128, :], stage[:, 128:160], ['stageA'])
        S.barrier()
        S.op('pool', lambda e: e.memset(vaug[:, :, :, 64:65], 1.0), writes=['vaugA'])
        for c0 in range(0, nk, 512):
            c1 = min(nk, c0 + 512)
            n = c1 - c0
            for h in range(4):
                b = self.psum('lo')
                S.op('pe', lambda e: e.matmul(ps[b][0:64, 0:n], lhsT=wukv[:, 0, h * 64:(h + 1) * 64], rhs=ckvnT[:, c0:c1],
                                              start=True, stop=True), reads=['wukv', 'ckvnT'], writes=[f'ps{b}'])
                S.op('act', lambda e: e.copy(out=KT[0:64, h, c0:c1], in_=ps[b][0:64, 0:n]), reads=[f'ps{b}'], writes=['KTA'])
                S.op('pool', lambda e: e.tensor_copy(out=KT[64:96, h, c0:c1], in_=krT[64:96, c0:c1]), reads=['krT'], writes=['KTA'])
        for kb in range(nkb):
            b = self.psum('lo')
            S.op('pe', lambda e: e.matmul(ps[b][:, 0:256], lhsT=ckvnT[:, kb * 128:(kb + 1) * 128], rhs=wukv[:, 0, 256:512],
                                          start=True, stop=True), reads=['wukv', 'ckvnT'], writes=[f'ps{b}'])
            S.op('dve', lambda e: e.tensor_copy(out=vaug[:, kb, :, 0:64], in_=ps[b][:, 0:256].rearrange("p (h e) -> p h e", h=4)),
                 reads=[f'ps{b}'], writes=['vaugA'])
        yaall = [carve(MS + 34 * KB + 512 + i * 2 * KB, [128, 4, 256], BF16) for i in range(2)]
        yakey = {id(yaall[0]): 'yaA0', id(yaall[1]): 'yaA1'}
        deferred = []
        qi = 0

        def make_fin(hs, banks, nb, ya_):
            def fin():
                for i, h in enumerate(hs):
                    accb = banks[i]
                    av = ps[accb][:, 0:nb * 65].rearrange("p (b e) -> p b e", e=65)
                    S.op('dve', lambda e: e.reciprocal(out=smalls[:, 8 + 4 * i:8 + 4 * i + nb], in_=av[:, :, 64]),
                         reads=[f'ps{accb}'], writes=[f'rdA{i}'])
                    S.op('dve', lambda e: e.tensor_tensor(out=ya_[:, 0:nb, h * 64:(h + 1) * 64], in0=av[:, :, 0:64],
                                                          in1=smalls[:, 8 + 4 * i:8 + 4 * i + nb].unsqueeze(2).to_broadcast([128, nb, 64]),
                                                          op=ALU.mult),
                         reads=[f'ps{accb}', f'rdA{i}'], writes=[yakey[id(ya_)]])
            return fin

        for (s0, Ls) in G['seqs']:
            kb0 = 0 if samp else s0 // 128
            nkc = (Ls + kofs) // 128
            Nt = min(512, Ls)
            nb = Nt // 128
            for tq in range(Ls // Nt):
                q0 = s0 + tq * Nt
                ya_ = yaall[qi % 2]
                qi += 1
                for hp in range(2):
                    banks = (4, 5) if hp == 0 else (6, 7)
                    hs = (2 * hp, 2 * hp + 1)
                    self.attn_core(L, [((lambda h=h: (qT[0:96, h, q0:q0 + Nt], 'qTA')),
                                        (lambda sc, h=h: (KT[0:96, h, (kb0 + sc) * 128:(kb0 + sc + 1) * 128], 'KTA')),
                                        (lambda sc, h=h: (vaug[:, kb0 + sc, h, :], 'vaugA')), banks[i]) for i, h in enumerate(hs)],
                                   nkc, Nt, MLA_SCALE)
                    self.p0_hook(MS + 40 * KB)
                    for f in deferred:
                        f()
                    deferred = [make_fin(hs, banks, nb, ya_)]
                    if hp == 1:
                        def tofm(ya_=ya_, nb=nb, q0=q0):
                            for tb in range(nb):
                                to_fm(ya_[:, tb, :], yakey[id(ya_)], 0, (t0 + q0) // 128 + tb)
                        deferred.append(tofm)
        for f in deferred:
            f()

    def mixer_D(self, l, L, samp):
        S = self.S
        g_ = lambda n: L[n]
        ps, hT, carve, wload, fm_mm, tm_mm, win_d, tl, MS, KB = (g_('ps'), g_('hT'), g_('carve'), g_('wload'), g_('fm_mm'),
                                                                  g_('tm_mm'), g_('win_d'), g_('tl'), g_('MS'), g_('KB'))
        cosT, sinT, smalls, lamB, snB, to_fm = g_('cosT'), g_('sinT'), g_('smalls'), g_('lamB'), g_('snB'), g_('to_fm')
        G = self.geo(samp)
        t0, tiles, kofs, nkb = G['t0'], G['tiles'], G['kofs'], G['nkb']
        self.pt_i = 0
        lam_init = 0.8 - 0.6 * math.exp(-0.3 * l)
        QT = carve(MS, [96, 3, 1024], BF16)
        KT = carve(MS + 6 * KB, [96, 3, 1280], BF16)
        vaug = carve(MS + 13 * KB + 512, [128, 10, 4, 65], BF16)
        rt = [carve(MS + 19 * KB + i * 2 * KB, [96, 512], F32) for i in range(2)]
        stage = carve(MS + 23 * KB, [128, 512], F32)
        dtmp = [carve(MS + 25 * KB + i * 256, [128, 64], F32) for i in range(3)]
        junk = carve(MS + 25 * KB + 768, [128, 64], F32)
        lt = carve(MS + 26 * KB, [128, 32], F32)
        yd = [carve(MS + 27 * KB + i * 512, [128, 256], BF16) for i in range(4)]

        for i in range(2):
            S.op('dve', lambda e: e.tensor_tensor(out=lt, in0=lamB[:, 64 * i:64 * i + 32],
                                                  in1=lamB[:, 64 * i + 32:64 * i + 64], op=ALU.mult),
                 reads=['lamB'], writes=['ltD'])
            S.op('dve', lambda e: e.reduce_sum(out=smalls[:, 30 + i:31 + i], in_=lt, axis=AX.X), reads=['ltD'], writes=['lsD'])
        S.op('act', lambda e: e.activation(out=smalls[:, 32:34], in_=smalls[:, 30:32], func=AF.Exp), reads=['lsD'], writes=['leD'])
        S.op('dve', lambda e: e.tensor_tensor(out=smalls[:, 34:35], in0=smalls[:, 33:34], in1=smalls[:, 32:33], op=ALU.subtract),
             reads=['leD'], writes=['nlD'])
        S.op('dve', lambda e: e.tensor_scalar(out=smalls[:, 34:35], in0=smalls[:, 34:35], scalar1=float(-lam_init), scalar2=None,
                                              op0=ALU.add), reads=['nlD'], writes=['nlD'])
        S.barrier()
        self.cut('Dlam')
        S.op('pool', lambda e: e.memset(vaug[:, :, :, 64:65], 1.0), writes=['vaugD'])
        if samp:
            S.dma('pool', 'ctxD0', KT[:, :, 0:256], L['cdk_d'][l], writes=['KTD'])
            S.dma('pool', 'ctxD1', vaug[:, 0:2, :, 0:64], L['cdv_d'][:, l], writes=['vaugD'])
        plan = [('D1', 0, QT, 0, 96), ('D1', 256, QT, 1, 96), ('D2', 0, QT, 2, 64),
                ('D2', 256, KT, 0, 96), ('D3', 0, KT, 1, 96), ('D3', 256, KT, 2, 64)]
        cur = None
        for (gn, c0, dst, dc, M) in plan:
            if gn != cur:
                wt, wk = wload(win_d[l, WG_ORDER.index(gn)][:, :, 0:512], 4096)
                w3 = wt[:, 0:4096].rearrange("p (k j) -> p k j", k=8)
                cur = gn
            isK = dst is KT
            dkey = 'KTD' if isK else 'QTD'
            for tt in tiles:
                lc = (tt - tiles[0]) * 512
                b = self.psum('lo')
                fm_mm(ps[b][0:M, :], f'ps{b}', w3, wk, c0, M, lambda k: hT[:, k, tl(tt)], [f'hT{tt}'])
                d0 = (kofs + lc) if isK else lc
                if not samp:
                    S.op('act', lambda e: e.copy(out=dst[0:M, dc, d0:d0 + 512], in_=ps[b][0:M, :]), reads=[f'ps{b}'], writes=[dkey])
                else:
                    b2 = self.psum('lo')
                    fm_mm(ps[b2][0:M, :], f'ps{b2}', w3, wk, c0 + 128, M, lambda k: hT[:, k, tl(tt)], [f'hT{tt}'])
                    S.op('dve', lambda e: e.tensor_tensor(out=rt[0][0:M, :], in0=ps[b][0:M, :], in1=cosT[0:M, lc:lc + 512], op=ALU.mult),
                         reads=[f'ps{b}', 'cosT'], writes=['rtD0'])
                    S.op('dve', lambda e: e.tensor_tensor(out=rt[1][0:M, :], in0=ps[b2][0:M, :], in1=sinT[0:M, lc:lc + 512], op=ALU.mult),
                         reads=[f'ps{b2}', 'sinT'], writes=['rtD1'])
                    S.op('pool', lambda e: e.tensor_tensor(out=dst[0:M, dc, d0:d0 + 512], in0=rt[0][0:M, :], in1=rt[1][0:M, :], op=ALU.add),
                         reads=['rtD0', 'rtD1'], writes=[dkey])
        S.barrier()
        self.cut('Dz')
        wt, wk = wload(win_d[l, WG_ORDER.index('D4')][:, :, 0:512], 4096)
        w3 = wt[:, 0:4096].rearrange("p (k j) -> p k j", k=8)
        for bi in range(G['nt'] // 128):
            blk = t0 // 128 + bi
            N = 256 if samp else 512
            b = self.psum('lo')
            tm_mm(ps[b][:, 0:N], f'ps{b}', w3, wk, 0, N, blk * 128)
            kb = kofs // 128 + bi
            S.op('dve', lambda e: e.tensor_copy(out=vaug[:, kb, :, 0:64], in_=ps[b][:, 0:256].rearrange("p (h e) -> p h e", h=4)),
                 reads=[f'ps{b}'], writes=['vaugD'])
            if not samp:
                S.op('act', lambda e: e.copy(out=stage[:, :], in_=ps[b][:, :]), reads=[f'ps{b}'], writes=['stageD'])
                sq_, bl = blk // 2, blk % 2
                self.store('o_dv', L['o_dv'][sq_, l, bl * 128:(bl + 1) * 128, :], stage[:, 0:256], ['stageD'])
                self.store('o_dk', L['o_dk'][sq_, l, bl * 128:(bl + 1) * 128, :], stage[:, 256:512], ['stageD'])
        S.barrier()
        self.cut('Dtm')
        fin_scale = 8.0 * (1.0 - lam_init)
        ydall = [carve(MS + 27 * KB + i * 2 * KB, [128, 4, 256], BF16) for i in range(2)]
        dA = carve(MS + 31 * KB, [128, 4, 64], F32)
        dB = carve(MS + 32 * KB, [128, 4, 64], F32)
        rr = smalls[:, 36:44].rearrange("p (j b) -> p j b", j=2)
        deferred = []
        qi = 0

        def make_fin(h, accb, nb, yd_):
            def fin():
                av = [ps[accb[j]][:, 0:nb * 65].rearrange("p (b e) -> p b e", e=65) for j in range(2)]
                for j in range(2):
                    S.op('dve', lambda e: e.reciprocal(out=rr[:, j, 0:nb], in_=av[j][:, :, 64]),
                         reads=[f'ps{accb[j]}'], writes=[f'rdD{j}'])
                S.op('dve', lambda e: e.tensor_scalar(out=rr[:, 1, 0:nb], in0=rr[:, 1, 0:nb], scalar1=smalls[:, 34:35], scalar2=None,
                                                      op0=ALU.mult), reads=['rdD1', 'nlD'], writes=['rdD1'])
                S.op('dve', lambda e: e.tensor_tensor(out=dA[:, 0:nb, :], in0=av[0][:, :, 0:64],
                                                      in1=rr[:, 0, 0:nb].unsqueeze(2).to_broadcast([128, nb, 64]), op=ALU.mult),
                     reads=[f'ps{accb[0]}', 'rdD0'], writes=['dA'])
                S.op('dve', lambda e: e.tensor_tensor(out=dB[:, 0:nb, :], in0=av[1][:, :, 0:64],
                                                      in1=rr[:, 1, 0:nb].unsqueeze(2).to_broadcast([128, nb, 64]), op=ALU.mult),
                     reads=[f'ps{accb[1]}', 'rdD1'], writes=['dB'])
                S.op('dve', lambda e: e.tensor_tensor(out=dA[:, 0:nb, :], in0=dA[:, 0:nb, :], in1=dB[:, 0:nb, :], op=ALU.add),
                     reads=['dA', 'dB'], writes=['dA'])
                S.op('dve', lambda e: e.tensor_tensor(out=dB[:, 0:nb, :], in0=dA[:, 0:nb, :], in1=dA[:, 0:nb, :], op=ALU.mult),
                     reads=['dA'], writes=['dB'])
                S.op('dve', lambda e: e.reduce_sum(out=smalls[:, 44:44 + nb], in_=dB[:, 0:nb, :], axis=AX.X), reads=['dB'], writes=['ssD'])
                self.rsqrt(smalls[:, 44:44 + nb], smalls[:, 44:44 + nb], float(64 * EPS), ['ssD'], ['ssD'])
                S.op('dve', lambda e: e.tensor_tensor(out=dA[:, 0:nb, :], in0=dA[:, 0:nb, :],
                                                      in1=smalls[:, 44:44 + nb].unsqueeze(2).to_broadcast([128, nb, 64]), op=ALU.mult),
                     reads=['dA', 'ssD'], writes=['dA'])
                S.op('dve', lambda e: e.scalar_tensor_tensor(out=yd_[:, 0:nb, h * 64:(h + 1) * 64], in0=dA[:, 0:nb, :], scalar=float(fin_scale),
                                                             in1=snB[:, 0:64].unsqueeze(1).to_broadcast([128, nb, 64]),
                                                             op0=ALU.mult, op1=ALU.mult),
                     reads=['dA', 'snB'], writes=[yd_.__dict__.get('k', 'ydD')] if False else [ydkey[id(yd_)]])
            return fin

        ydkey = {id(ydall[0]): 'ydD0', id(ydall[1]): 'ydD1'}
        for (s0, Ls) in G['seqs']:
            kb0 = 0 if samp else s0 // 128
            nkc = (Ls + kofs) // 128
            Nt = min(512, Ls)
            nb = Nt // 128
            for tq in range(Ls // Nt):
                q0 = s0 + tq * Nt
                yd_ = ydall[qi % 2]
                qi += 1
                for h in range(4):
                    accb = [4, 5] if h % 2 == 0 else [6, 7]
                    strs = []
                    for j in range(2):
                        p = 2 * h + j
                        c, pb = p // 3, (p % 3) * 32
                        strs.append(((lambda c=c, pb=pb: (QT[pb:pb + 32, c, q0:q0 + Nt], 'QTD')),
                                     (lambda sc, c=c, pb=pb: (KT[pb:pb + 32, c, (kb0 + sc) * 128:(kb0 + sc + 1) * 128], 'KTD')),
                                     (lambda sc, h=h: (vaug[:, kb0 + sc, h, :], 'vaugD')), accb[j]))
                    self.attn_core(L, strs, nkc, Nt, DIFF_SCALE)
                    self.p0_hook(MS + 40 * KB)
                    for f in deferred:
                        f()
                    deferred = [make_fin(h, accb, nb, yd_)]
                    if h == 3:
                        def tofm(yd_=yd_, nb=nb, q0=q0):
                            for tb in range(nb):
                                to_fm(yd_[:, tb, :], ydkey[id(yd_)], 3, (t0 + q0) // 128 + tb)
                        deferred.append(tofm)
        for f in deferred:
            f()

    def mixer_C(self, l, L, samp):
        S = self.S
        g_ = lambda n: L[n]
        ps, hT, carve, wload, fm_mm, tm_mm, win_d, tl, MS, KB = (g_('ps'), g_('hT'), g_('carve'), g_('wload'), g_('fm_mm'),
                                                                  g_('tm_mm'), g_('win_d'), g_('tl'), g_('MS'), g_('KB'))
        smalls, hnB, gbT, m0T, m0B, C0, sel, identf, mle, mge, to_fm, zero1 = (
            g_('smalls'), g_('hnB'), g_('gbT'), g_('m0T'), g_('m0B'), g_('C0'), g_('sel'), g_('identf'), g_('mle'), g_('mge'),
            g_('to_fm'), g_('zero1'))
        G = self.geo(samp)
        t0, tiles = G['t0'], G['tiles']
        nbp = G['nt'] // 128
        qT = carve(MS, [128, 2, 1024], BF16)
        kT = carve(MS + 4 * KB, [128, 2, 1024], BF16)
        vaug = carve(MS + 8 * KB, [128, 8, 4, 65], BF16)
        sgo = carve(MS + 12 * KB + 512, [128, 8, 256], BF16)
        giT = carve(MS + 16 * KB + 512, [36, 1024], F32)
        lfT = carve(MS + 20 * KB + 512, [36, 1024], F32)
        BT = carve(MS + 24 * KB + 512, [36, 1024], F32)
        GT = carve(MS + 28 * KB + 512, [36, 1024], F32)
        hb = carve(MS + 32 * KB + 512, [128, 8, 256], F32)
        kTM = carve(MS + 40 * KB + 512, [128, 4, 256], BF16)
        negG = [carve(MS + 20 * KB + 512, [128, 1024], F32), carve(MS + 16 * KB + 512, [128, 1024], F32)]
        sq = g_('sq')
        sqb = sq[:, :, :].rearrange("p a b -> p (a b)")
        Wt = [sqb[:, i * 512:(i + 1) * 512] for i in range(4)]
        Dt = [sqb[:, 2048 + i * 1024:2048 + (i + 1) * 1024].bitcast(F32) for i in range(2)]
        rstd = g_('rstd')
        gTM = rstd[0][:, 0:288].rearrange("p (b r) -> p b r", r=36)
        emt = rstd[1][:, 0:288].rearrange("p (b r) -> p b r", r=36)

        S.op('pool', lambda e: e.memset(vaug[:, :, :, 64:65], 1.0), writes=['vaugC'])
        S.op('pool', lambda e: e.memset(BT[:, :], 0.0), writes=['BT'])
        wt, wk = wload(win_d[l, WG_ORDER.index('C1')][:, :, 0:512], 4096)
        w3 = wt[:, 0:4096].rearrange("p (k j) -> p k j", k=8)
        for ci in range(4):
            dst = qT if ci < 2 else kT
            for tt in tiles:
                lc = (tt - tiles[0]) * 512
                b = self.psum('lo')
                fm_mm(ps[b][:, :], f'ps{b}', w3, wk, ci * 128, 128, lambda k: hT[:, k, tl(tt)], [f'hT{tt}'])
                S.op('act', lambda e: e.activation(out=dst[:, ci % 2, lc:lc + 512], in_=ps[b][:, :], func=AF.Copy,
                                                   scale=(1.0 if ci < 2 else 0.125)),
                     reads=[f'ps{b}'], writes=['qTC' if ci < 2 else 'kTC'])
        wt, wk = wload(win_d[l, WG_ORDER.index('C2')][:, :, 0:128], 1024)
        w3 = wt[:, 0:1024].rearrange("p (k j) -> p k j", k=8)
        for tt in tiles:
            lc = (tt - tiles[0]) * 512
            cs = slice(lc, lc + 512)
            b = self.psum('lo')
            fm_mm(ps[b][0:36, :], f'ps{b}', w3, wk, 0, 36, lambda k: hT[:, k, tl(tt)], [f'hT{tt}'])
            S.op('act', lambda e: e.activation(out=giT[:, cs], in_=ps[b][0:36, :], func=AF.Identity, bias=gbT[:, l, 0:1], scale=1.0),
                 reads=[f'ps{b}', 'gbT'], writes=['giT'])
            b = self.psum('lo')
            fm_mm(ps[b][0:36, :], f'ps{b}', w3, wk, 64, 36, lambda k: hT[:, k, tl(tt)], [f'hT{tt}'])
            S.op('dve', lambda e: e.tensor_scalar(out=lfT[:, cs], in0=ps[b][0:36, :], scalar1=gbT[:, l, 1:2], scalar2=-1.0,
                                                  op0=ALU.add, op1=ALU.mult), reads=[f'ps{b}', 'gbT'], writes=['lfT'])
            S.op('act', lambda e: e.activation(out=lfT[:, cs], in_=lfT[:, cs], func=AF.Exp), reads=['lfT'], writes=['lfT'])
            S.op('act', lambda e: e.activation(out=lfT[:, cs], in_=lfT[:, cs], func=AF.Ln, bias=self.cbias(1.0, lfT[:, cs]), scale=1.0),
                 reads=['lfT', 'cbias'], writes=['lfT'])
            S.op('dve', lambda e: e.tensor_scalar(out=lfT[:, cs], in0=lfT[:, cs], scalar1=-1.0, scalar2=None, op0=ALU.mult),
                 reads=['lfT'], writes=['lfT'])
        wt, wk = wload(win_d[l, WG_ORDER.index('C3')][:, :, 0:512], 4096)
        w3 = wt[:, 0:4096].rearrange("p (k j) -> p k j", k=8)
        for bi in range(nbp):
            b = self.psum('lo')
            tm_mm(ps[b][:, :], f'ps{b}', w3, wk, 0, 512, t0 + bi * 128)
            S.op('dve', lambda e: e.tensor_copy(out=vaug[:, bi, :, 0:64], in_=ps[b][:, 0:256].rearrange("p (h e) -> p h e", h=4)),
                 reads=[f'ps{b}'], writes=['vaugC'])
            S.op('act', lambda e: e.activation(out=sgo[:, bi, :], in_=ps[b][:, 256:512], func=AF.Sigmoid),
                 reads=[f'ps{b}'], writes=['sgo'])
        if not samp:
            wt, wk = wload(win_d[l, WG_ORDER.index('C4')][:, :, 0:256], 2048)
            w3 = wt[:, 0:2048].rearrange("p (k j) -> p k j", k=8)
            for bi in range(4):
                b = self.psum('lo')
                tm_mm(ps[b][:, 0:256], f'ps{b}', w3, wk, 0, 256, bi * 128)
                S.op('act', lambda e: e.activation(out=kTM[:, bi, :], in_=ps[b][:, 0:256], func=AF.Copy, scale=0.125),
                     reads=[f'ps{b}'], writes=['kTM'])

        for si, (s0, Ls) in enumerate(G['seqs']):
            nb = Ls // 128
            b0 = s0 // 128
            sl = slice(s0, s0 + Ls)
            S.op('pool', lambda e: e.memset(GT[:, 0:Ls], 1.0), writes=['GT'])
            S.op('dve', lambda e: e.tensor_tensor_scan(out=BT[0:4, 0:Ls], data0=GT[0:4, 0:Ls], data1=lfT[0:4, sl], initial=0.0,
                                                       op0=ALU.mult, op1=ALU.add), reads=['GT', 'lfT'], writes=['BT'])
            S.op('dve', lambda e: e.tensor_tensor_scan(out=BT[32:36, 0:Ls][:, ::-1], data0=GT[32:36, 0:Ls],
                                                       data1=lfT[32:36, sl][:, ::-1], initial=0.0,
                                                       op0=ALU.mult, op1=ALU.add), reads=['GT', 'lfT'], writes=['BT'])
            for r0 in (0, 32):
                S.op('dve', lambda e: e.tensor_tensor(out=giT[r0:r0 + 4, sl], in0=giT[r0:r0 + 4, sl], in1=BT[r0:r0 + 4, 0:Ls],
                                                      op=ALU.subtract), reads=['giT', 'BT'], writes=['giT'])
            init_f = m0T[0:4, l:l + 1] if samp else zero1[0:4, 0:1]
            init_b = m0T[32:36, l:l + 1] if samp else zero1[32:36, 0:1]
            S.op('dve', lambda e: e.tensor_tensor_scan(out=GT[0:4, 0:Ls], data0=giT[0:4, sl], data1=giT[0:4, sl], initial=init_f,
                                                       op0=ALU.max, op1=ALU.max), reads=['giT', 'm0T', 'zero1'], writes=['GT'])
            S.op('dve', lambda e: e.tensor_tensor_scan(out=GT[32:36, 0:Ls][:, ::-1], data0=giT[32:36, sl][:, ::-1],
                                                       data1=giT[32:36, sl][:, ::-1], initial=init_b,
                                                       op0=ALU.max, op1=ALU.max), reads=['giT', 'm0T', 'zero1'], writes=['GT'])
            for r0 in (0, 32):
                S.op('dve', lambda e: e.tensor_tensor(out=BT[r0:r0 + 4, 0:Ls], in0=BT[r0:r0 + 4, 0:Ls], in1=GT[r0:r0 + 4, 0:Ls],
                                                      op=ALU.add), reads=['BT', 'GT'], writes=['BT'])
                S.op('dve', lambda e: e.tensor_scalar(out=GT[r0:r0 + 4, 0:Ls], in0=GT[r0:r0 + 4, 0:Ls], scalar1=-1.0, scalar2=None,
                                                      op0=ALU.mult), reads=['GT'], writes=['GT'])
            if not samp:
                S.dma('sp', 'o_m', L['o_m'][si, l, 0, :].rearrange("(h o) -> h o", o=1), BT[0:4, Ls - 1:Ls], reads=['BT'],
                      writes=['dram_o_m'])
                S.dma('sp', 'o_m', L['o_m'][si, l, 1, :].rearrange("(h o) -> h o", o=1), BT[32:36, 0:1], reads=['BT'],
                      writes=['dram_o_m'])
                if 'o_m' not in self.out_keys:
                    self.out_keys.append('o_m')
            for bi in range(nb):
                b = self.psum('lo')
                S.op('pe', lambda e: e.matmul(ps[b][:, 0:36], lhsT=giT[0:36, s0 + bi * 128:s0 + (bi + 1) * 128], rhs=identf[:, :],
                                              start=True, stop=True), reads=['giT', 'identf'], writes=[f'ps{b}'], inc=False)
                S.op('pe', lambda e: e.matmul(ps[b][:, 64:100], lhsT=BT[0:36, bi * 128:(bi + 1) * 128], rhs=identf[:, :],
                                              start=True, stop=True), reads=['BT', 'identf'], writes=[f'ps{b}'])
                S.op('dve', lambda e: e.tensor_copy(out=gTM[:, bi, :], in_=ps[b][:, 0:36]), reads=[f'ps{b}'], writes=['gTM'])
                S.op('act', lambda e: e.activation(out=emt[:, bi, :], in_=ps[b][:, 64:100], func=AF.Exp, scale=-1.0),
                     reads=[f'ps{b}'], writes=['emt'])
            S.barrier()
            for h in range(4):
                kk, pb = h // 2, (h % 2) * 64
                rf, rb = h, 32 + h
                for d_ in (0, 1):
                    r0 = 0 if d_ == 0 else 32
                    for c0 in range(0, Ls, 512):
                        n = min(512, Ls - c0)
                        b = self.psum('lo')
                        S.op('pe', lambda e: e.matmul(ps[b][:, 0:n], lhsT=sel[r0:r0 + 4, h, :], rhs=GT[r0:r0 + 4, c0:c0 + n],
                                                      start=True, stop=True), reads=['sel', 'GT'], writes=[f'ps{b}'])
                        S.op('act', lambda e: e.copy(out=negG[d_][:, c0:c0 + n], in_=ps[b][:, 0:n]), reads=[f'ps{b}'],
                             writes=[f'negG{d_}'])
                Nt = min(512, Ls)
                ntb = Nt // 128
                for tq in range(Ls // Nt):
                    q0l = tq * Nt
                    tb0 = q0l // 128
                    accf, accb_ = (4, 5) if h % 2 == 0 else (6, 7)
                    af = ps[accf][:, 0:ntb * 65].rearrange("p (b e) -> p b e", e=65)
                    ab = ps[accb_][:, 0:ntb * 65].rearrange("p (b e) -> p b e", e=65)
                    bank_started = {0: False, 1: False}
                    if samp:
                        for d_, acc, an in ((0, af, accf), (1, ab, accb_)):
                            r = d_ * 4 + h
                            wq = Dt[d_][pb:pb + 64, 0:Nt]
                            S.op('act', lambda e: e.activation(out=wq, in_=negG[d_][pb:pb + 64, q0l:q0l + Nt], func=AF.Exp,
                                                               bias=m0B[pb:pb + 64, l * 8 + r:l * 8 + r + 1], scale=1.0),
                                 reads=[f'negG{d_}', 'm0B'], writes=[f'Dt{d_}'])
                            qs = Wt[d_][pb:pb + 64, 0:Nt]
                            S.op('dve', lambda e: e.tensor_tensor(out=qs, in0=qT[pb:pb + 64, kk, s0 + q0l:s0 + q0l + Nt], in1=wq,
                                                                  op=ALU.mult), reads=['qTC', f'Dt{d_}'], writes=[f'Wt{d_}'])
                            for i in range(ntb):
                                S.op('pe', lambda e: e.matmul(acc[:, i, :], lhsT=qs[:, i * 128:(i + 1) * 128],
                                                              rhs=C0[pb:pb + 64, l, d_, kk, :], start=(not bank_started[d_]), stop=False,
                                                              skip_group_check=True),
                                     reads=[f'Wt{d_}', 'C0'], writes=[f'ps{an}'], inc=(i == ntb - 1))
                                bank_started[d_] = True
                    def stage1(j):
                        sb_ = self.psum('lo')
                        S.op('pe', lambda e: e.matmul(ps[sb_][:, 0:Nt], lhsT=kT[pb:pb + 64, kk, s0 + j * 128:s0 + (j + 1) * 128],
                                                      rhs=qT[pb:pb + 64, kk, s0 + q0l:s0 + q0l + Nt], start=True, stop=True),
                             reads=['kTC', 'qTC'], writes=[f'ps{sb_}'])
                        jl = j - tb0
                        lo = max(jl, 0)
                        wf = wb_ = None
                        if lo < ntb:
                            c0_, c1_ = lo * 128, Nt
                            self.wt_i = (getattr(self, 'wt_i', 0) + 1) % 2
                            wf = self.wt_i
                            S.op('act', lambda e: e.activation(out=Dt[0][:, c0_:c1_], in_=negG[0][:, q0l + c0_:q0l + c1_], func=AF.Exp,
                                                               bias=gTM[:, j, rf:rf + 1], scale=1.0),
                                 reads=['negG0', 'gTM'], writes=['Dt0'])
                            S.op('dve', lambda e: e.tensor_tensor(out=Wt[wf][:, c0_:c1_], in0=ps[sb_][:, c0_:c1_], in1=Dt[0][:, c0_:c1_],
                                                                  op=ALU.mult), reads=[f'ps{sb_}', 'Dt0'], writes=[f'Wt{wf}'])
                            if jl >= 0:
                                S.op('pool', lambda e: e.tensor_tensor(out=Wt[wf][:, c0_:c0_ + 128], in0=Wt[wf][:, c0_:c0_ + 128],
                                                                       in1=mle[:, :], op=ALU.mult), reads=[f'Wt{wf}', 'mle'],
                                     writes=[f'Wt{wf}'])
                        hi = min(jl, ntb - 1)
                        if hi >= 0:
                            c0_, c1_ = 0, (hi + 1) * 128
                            self.wt_j = (getattr(self, 'wt_j', 0) + 1) % 2
                            wb_ = 2 + self.wt_j
                            S.op('act', lambda e: e.activation(out=Dt[1][:, c0_:c1_], in_=negG[1][:, q0l + c0_:q0l + c1_], func=AF.Exp,
                                                               bias=gTM[:, j, rb:rb + 1], scale=1.0),
                                 reads=['negG1', 'gTM'], writes=['Dt1'])
                            S.op('dve', lambda e: e.tensor_tensor(out=Wt[wb_][:, c0_:c1_], in0=ps[sb_][:, c0_:c1_], in1=Dt[1][:, c0_:c1_],
                                                                  op=ALU.mult), reads=[f'ps{sb_}', 'Dt1'], writes=[f'Wt{wb_}'])
                            if jl <= ntb - 1:
                                S.op('pool', lambda e: e.tensor_tensor(out=Wt[wb_][:, hi * 128:(hi + 1) * 128],
                                                                       in0=Wt[wb_][:, hi * 128:(hi + 1) * 128], in1=mge[:, :], op=ALU.mult),
                                     reads=[f'Wt{wb_}', 'mge'], writes=[f'Wt{wb_}'])
                        return (lo, wf, hi, wb_)

                    def stage2(j, info):
                        lo, wf, hi, wb_ = info
                        if wf is not None:
                            for i in range(lo, ntb):
                                S.op('pe', lambda e: e.matmul(af[:, i, :], lhsT=Wt[wf][:, i * 128:(i + 1) * 128],
                                                              rhs=vaug[:, b0 + j, h, :], start=(not bank_started[0]), stop=False,
                                                              skip_group_check=True),
                                     reads=[f'Wt{wf}', 'vaugC'], writes=[f'ps{accf}'], inc=(i == ntb - 1))
                                bank_started[0] = True
                        if wb_ is not None:
                            for i in range(0, hi + 1):
                                S.op('pe', lambda e: e.matmul(ab[:, i, :], lhsT=Wt[wb_][:, i * 128:(i + 1) * 128],
                                                              rhs=vaug[:, b0 + j, h, :], start=(not bank_started[1]), stop=False,
                                                              skip_group_check=True),
                                     reads=[f'Wt{wb_}', 'vaugC'], writes=[f'ps{accb_}'], inc=(i == hi))
                                bank_started[1] = True

                    cur = stage1(0)
                    for j in range(nb):
                        nxt = stage1(j + 1) if j + 1 < nb else None
                        stage2(j, cur)
                        cur = nxt
                    dsc = smalls[:, 16:24].rearrange("p (d b) -> p d b", d=2)
                    dtm = carve(MS + 45 * KB, [128, 4, 64], F32)
                    for d_, r, acc, an in ((0, rf, af, accf), (1, rb, ab, accb_)):
                        S.op('dve', lambda e: e.tensor_scalar(out=dsc[:, d_, 0:ntb], in0=acc[:, :, 64], scalar1=-1.0, scalar2=None,
                                                              op0=ALU.mult), reads=[f'ps{an}'], writes=[f'dnC{d_}'])
                        S.op('dve', lambda e: e.tensor_tensor(out=dsc[:, d_, 0:ntb], in0=acc[:, :, 64], in1=dsc[:, d_, 0:ntb], op=ALU.max),
                             reads=[f'ps{an}', f'dnC{d_}'], writes=[f'dnC{d_}'])
                        S.op('dve', lambda e: e.tensor_tensor(out=dsc[:, d_, 0:ntb], in0=dsc[:, d_, 0:ntb], in1=emt[:, tb0:tb0 + ntb, r],
                                                              op=ALU.max), reads=['emt', f'dnC{d_}'], writes=[f'dnC{d_}'])
                        S.op('dve', lambda e: e.reciprocal(out=dsc[:, d_, 0:ntb], in_=dsc[:, d_, 0:ntb]), reads=[f'dnC{d_}'],
                             writes=[f'dnC{d_}'])
                    hdst = hb[:, b0 + tb0:b0 + tb0 + ntb, h * 64:(h + 1) * 64]
                    S.op('dve', lambda e: e.tensor_tensor(out=hdst, in0=af[:, :, 0:64],
                                                          in1=dsc[:, 0, 0:ntb].unsqueeze(2).to_broadcast([128, ntb, 64]), op=ALU.mult),
                         reads=[f'ps{accf}', 'dnC0'], writes=['hbC'])
                    S.op('dve', lambda e: e.tensor_tensor(out=dtm[:, 0:ntb, :], in0=ab[:, :, 0:64],
                                                          in1=dsc[:, 1, 0:ntb].unsqueeze(2).to_broadcast([128, ntb, 64]), op=ALU.mult),
                         reads=[f'ps{accb_}', 'dnC1'], writes=['dtmC'])
                    S.op('dve', lambda e: e.tensor_tensor(out=hdst, in0=hdst, in1=dtm[:, 0:ntb, :], op=ALU.add),
                         reads=['hbC', 'dtmC'], writes=['hbC'])
                self.p0_hook(MS + 46 * KB)
                if not samp:
                    for d_, r, col_last in ((0, rf, Ls - 1), (1, rb, 0)):
                        wS = smalls[:, 56:56 + nb]
                        S.op('act', lambda e: e.activation(out=wS, in_=gTM[:, 0:nb, r], func=AF.Exp,
                                                           bias=negG[d_][:, col_last:col_last + 1], scale=1.0),
                             reads=['gTM', f'negG{d_}'], writes=['wSC'])
                        cb = self.psum('lo')
                        for j in range(nb):
                            kw = Wt[j % 2][:, 0:64]
                            S.op('dve', lambda e: e.tensor_scalar(out=kw, in0=kTM[:, b0 + j, h * 64:(h + 1) * 64], scalar1=wS[:, j:j + 1],
                                                                  scalar2=None, op0=ALU.mult), reads=['kTM', 'wSC'], writes=[f'Wt{j % 2}'])
                            S.op('pe', lambda e: e.matmul(ps[cb][0:64, 0:65], lhsT=kw, rhs=vaug[:, b0 + j, h, :], start=(j == 0),
                                                          stop=(j == nb - 1)), reads=[f'Wt{j % 2}', 'vaugC'], writes=[f'ps{cb}'],
                                 inc=(j == nb - 1))
                        self.cs_i = (getattr(self, 'cs_i', 0) + 1) % 4
                        cst = carve(MS + 42 * KB + 512 + self.cs_i * 512, [64, 65], F32)
                        S.op('act', lambda e: e.copy(out=cst, in_=ps[cb][0:64, 0:65]), reads=[f'ps{cb}'], writes=[f'cst{self.cs_i}'])
                        self.store(f'o_C{self.cs_i}', L['o_C'][si, l, d_, h, :, :], cst[:, 0:64], [f'cst{self.cs_i}'])
                        self.store(f'o_n{self.cs_i}', L['o_n'][si, l, d_, h, :].rearrange("(d o) -> d o", o=1), cst[:, 64:65],
                                   [f'cst{self.cs_i}'])
            for bi in range(nb):
                hsrc = hb[:, b0 + bi, :]
                h3 = hsrc.rearrange("p (h e) -> p h e", h=4)
                ycb = Wt[bi % 2][:, 0:256]
                sq3 = Dt[0][:, 0:256].rearrange("p (h e) -> p h e", h=4)
                S.op('dve', lambda e: e.tensor_tensor(out=sq3, in0=h3, in1=h3, op=ALU.mult), reads=['hbC'], writes=['Dt0'])
                S.op('dve', lambda e: e.reduce_sum(out=smalls[:, 60:64], in_=sq3, axis=AX.X), reads=['Dt0'], writes=['ssC'])
                self.rsqrt(smalls[:, 60:64], smalls[:, 60:64], float(64 * EPS), ['ssC'], ['ssC'])
                S.op('dve', lambda e: e.tensor_tensor(out=h3, in0=h3, in1=smalls[:, 60:64].unsqueeze(2).to_broadcast([128, 4, 64]),
                                                      op=ALU.mult), reads=['hbC', 'ssC'], writes=['hbC'])
                S.op('dve', lambda e: e.scalar_tensor_tensor(out=hsrc, in0=hsrc, scalar=8.0, in1=hnB[:, 0:256], op0=ALU.mult, op1=ALU.mult),
                     reads=['hbC', 'hnB'], writes=['hbC'])
                S.op('dve', lambda e: e.tensor_tensor(out=ycb, in0=hsrc, in1=sgo[:, b0 + bi, :], op=ALU.mult),
                     reads=['hbC', 'sgo'], writes=[f'Wt{bi % 2}'])
                to_fm(ycb, f'Wt{bi % 2}', 2, (t0 + s0) // 128 + bi)
            S.barrier()


def build_program(debug=(), nlayers=DEPTH, stop_after=None):
    b0 = Builder(debug=debug, nlayers=nlayers, stop_after=stop_after)
    b0.build()
    b = Builder(debug=debug, nlayers=nlayers, stop_after=stop_after, wplan=b0.wreq)
    b.build()
    return b


_CACHE = {}


def kernel(**inputs):
    inp = {k: np.asarray(v) for k, v in inputs.items()}
    if 'b' not in _CACHE:
        _CACHE['b'] = build_program()
    b = _CACHE['b']
    sh = prep_shared(inp)
    in_maps = []
    for c in range(8):
        m = dict(sh)
        m.update(prep_core(inp, c))
        in_maps.append({k: np.ascontiguousarray(v, dtype=np.float32) for k, v in m.items()})
    res = run_bass_kernel_spmd(b.nc, in_maps, core_ids=list(range(8)))
    return assemble([r for r in res.results])


def assemble(rs):
    y_prompt = np.zeros((16, 256, D), np.float32)
    y_sample = np.zeros((8, 1024, D), np.float32)
    ckv = np.zeros((16, DEPTH, 256, 128), np.float32)
    kr = np.zeros((16, DEPTH, 256, 32), np.float32)
    dk = np.zeros((16, DEPTH, 256, 4, 2, 32), np.float32)
    dv = np.zeros((16, DEPTH, 256, 4, 64), np.float32)
    Cs = np.zeros((16, DEPTH, 2, 4, 64, 64), np.float32)
    ns = np.zeros((16, DEPTH, 2, 4, 64), np.float32)
    ms = np.zeros((16, DEPTH, 2, 4), np.float32)
    for c, r in enumerate(rs):
        yT = np.asarray(r['yT'])
        tok = yT.transpose(2, 1, 0).reshape(T, D)
        y_prompt[2 * c] = tok[0:256]
        y_prompt[2 * c + 1] = tok[256:512]
        y_sample[c] = tok[512:]
        ckv[2 * c:2 * c + 2] = np.asarray(r['o_ckv'])
        kr[2 * c:2 * c + 2] = np.asarray(r['o_kr'])
        dk[2 * c:2 * c + 2] = np.asarray(r['o_dk']).reshape(2, DEPTH, 256, 4, 2, 32)
        dv[2 * c:2 * c + 2] = np.asarray(r['o_dv']).reshape(2, DEPTH, 256, 4, 64)
        Cs[2 * c:2 * c + 2] = np.asarray(r['o_C'])
        ns[2 * c:2 * c + 2] = np.asarray(r['o_n'])
        ms[2 * c:2 * c + 2] = np.asarray(r['o_m'])
    return (y_prompt, y_sample, ckv, kr, dk, dv, Cs, ns, ms)
```

```python
import math
from contextlib import ExitStack
import numpy as np
import concourse.bass as bass
import concourse.mybir as mybir
from concourse.bass_utils import run_bass_kernel_spmd

F32 = mybir.dt.float32
BF16 = mybir.dt.bfloat16
AF = mybir.ActivationFunctionType
ALU = mybir.AluOpType
AX = mybir.AxisListType

D = 1024
T = 1536
NT = 3
DEPTH = 2
EPS = 1e-6
DFF = 4096
MLA_SCALE = 96 ** -0.5
DIFF_SCALE = 32 ** -0.5
OA, OB, OC, OD = 0, 416, 928, 1968
SEQS = [(0, 256, False), (256, 256, False), (512, 1024, True)]


class Sched:
    def __init__(self, nc, es):
        self.nc = nc
        self.es = es
        self.eng = {'pe': nc.tensor, 'act': nc.scalar, 'dve': nc.vector, 'pool': nc.gpsimd, 'sp': nc.sync}
        self.sem = {e: es.enter_context(nc.semaphore('s_' + e)) for e in self.eng}
        self.cnt = {e: 0 for e in self.eng}
        self.pending = {e: False for e in self.eng}
        self.waited = {e: {} for e in self.eng}
        self.lastw = {}
        self.readers = {}
        self.dsem = {}
        self.ninst = 0

    def _wait(self, e, tok):
        name, sem, val = tok
        if name == e and e == 'pe':
            return
        if self.waited[e].get(name, 0) >= val:
            return
        self.eng[e].wait_ge(sem, val)
        self.waited[e][name] = val

    def _deps(self, e, reads, writes):
        best = {}

        def add(t):
            if t is None:
                return
            b = best.get(t[0])
            if b is None or b[2] < t[2]:
                best[t[0]] = t
        relax = False
        for r in reads:
            add(self.lastw.get(r))
        for w in writes:
            t = self.lastw.get(w)
            if t is not None and not (relax and t[0] == e and w not in reads):
                add(t)
            for t in self.readers.get(w, {}).values():
                if not (relax and t[0] == e):
                    add(t)
        for t in best.values():
            self._wait(e, t)

    def _commit(self, tok, reads, writes):
        for w in writes:
            self.lastw[w] = tok
            self.readers[w] = {}
        for r in reads:
            d = self.readers.setdefault(r, {})
            b = d.get(tok[0])
            if b is None or b[2] < tok[2]:
                d[tok[0]] = tok

    def op(self, e, fn, reads=(), writes=(), inc=True):
        writes = list(writes) + [r for r in reads if r.startswith('ps') and r[2:].isdigit() and r not in writes]
        self._deps(e, reads, writes)
        inst = fn(self.eng[e])
        self.ninst += 1
        if inc:
            self.cnt[e] += 1
            inst.then_inc(self.sem[e], 1)
            tok = (e, self.sem[e], self.cnt[e])
            self.pending[e] = False
        else:
            tok = (e, self.sem[e], self.cnt[e] + 1)
            self.pending[e] = True
        self._commit(tok, reads, writes)
        return tok

    def dma(self, q, key, out, in_, reads=(), writes=()):
        self._deps(q, reads, writes)
        if key not in self.dsem:
            self.dsem[key] = [self.es.enter_context(self.nc.semaphore('d_' + key)), 0]
        d = self.dsem[key]
        d[1] += 16
        self.eng[q].dma_start(out=out, in_=in_).then_inc(d[0], 16)
        self.ninst += 1
        tok = ('d_' + key, d[0], d[1])
        self._commit(tok, reads, writes)
        return tok

    def barrier(self):
        toks = []
        for e in self.eng:
            assert not self.pending[e], e
            if self.cnt[e] > 0:
                toks.append((e, self.sem[e], self.cnt[e]))
        for k, d in self.dsem.items():
            if k[0] == 'w' or k.startswith('c_'):
                continue
            toks.append(('d_' + k, d[0], d[1]))
        for e in self.eng:
            for t in toks:
                if t[0] != e:
                    self._wait(e, t)

    def finish(self, out_keys):
        for k in out_keys:
            d = self.dsem[k]
            self._wait('sp', ('d_' + k, d[0], d[1]))


def ktile(W, ng):
    K, N = W.shape
    assert K % 128 == 0 and N % ng == 0
    return np.ascontiguousarray(W.reshape(K // 128, 128, N // ng, ng).transpose(2, 1, 0, 3))


def fm_vec(v):
    sh = v.shape
    n = sh[-1] // 128
    a = v.reshape(*sh[:-1], n, 128)
    return np.ascontiguousarray(np.moveaxis(a, -1, 0))


def pick_cols(W, cols):
    cols = np.asarray(cols)
    out = np.zeros((W.shape[0], len(cols)), np.float32)
    m = cols >= 0
    out[:, m] = W[:, cols[m]]
    return out


def pad_to(lst, n):
    return list(lst) + [-1] * (n - len(lst))


def win_groups():
    g = {}
    kr = [OA + 384 + i for i in range(32)]
    kr_sw = [OA + 384 + (i + 16) % 32 for i in range(32)]
    g['A1'] = list(range(OA, OA + 384)) + [-1] * 64 + kr
    g['A2'] = [-1] * 64 + kr_sw
    g['A3'] = list(range(OA + 256, OA + 384)) + kr
    g['B1'] = list(range(OB, OB + 256))
    g['B2'] = list(range(OB + 256, OB + 512))
    g['C1'] = list(range(OC, OC + 512))
    gi = pad_to([OC + 1024 + h for h in range(4)], 32) + [OC + 1024 + 8 + h for h in range(4)]
    gf = pad_to([OC + 1024 + 4 + h for h in range(4)], 32) + [OC + 1024 + 12 + h for h in range(4)]
    g['C2'] = pad_to(gi, 64) + pad_to(gf, 64)
    g['C3'] = list(range(OC + 512, OC + 1024))
    g['C4'] = list(range(OC + 256, OC + 512))

    def pairs(base, ps, sw):
        out = []
        for p in ps:
            out += [base + p * 32 + ((d + 16) % 32 if sw else d) for d in range(32)]
        return pad_to(out, 128)
    q, k = OD, OD + 256
    g['D1'] = pairs(q, [0, 1, 2], False) + pairs(q, [0, 1, 2], True) + pairs(q, [3, 4, 5], False) + pairs(q, [3, 4, 5], True)
    g['D2'] = pairs(q, [6, 7], False) + pairs(q, [6, 7], True) + pairs(k, [0, 1, 2], False) + pairs(k, [0, 1, 2], True)
    g['D3'] = pairs(k, [3, 4, 5], False) + pairs(k, [3, 4, 5], True) + pairs(k, [6, 7], False) + pairs(k, [6, 7], True)
    g['D4'] = list(range(OD + 512, OD + 768)) + list(range(OD + 256, OD + 512))
    return g


WG = win_groups()
WG_ORDER = ['A1', 'A2', 'A3', 'B1', 'B2', 'C1', 'C2', 'C3', 'C4', 'D1', 'D2', 'D3', 'D4']


def rope_tables():
    t = np.arange(1024)
    row = (t // 64).astype(np.float32)
    col = (t % 64).astype(np.float32)
    nf = 8
    inv = np.exp(-math.log(10000.0) * np.arange(nf, dtype=np.float32) / nf).astype(np.float32)
    ang = np.concatenate([row[:, None] * inv, col[:, None] * inv], axis=-1).astype(np.float32)
    c, s = np.cos(ang).T, np.sin(ang).T
    cos2 = np.concatenate([c, c], 0)
    sin2 = np.concatenate([-s, s], 0)
    return (np.ascontiguousarray(np.tile(cos2, (4, 1)), dtype=np.float32),
            np.ascontiguousarray(np.tile(sin2, (4, 1)), dtype=np.float32))


def prep_shared(inp):
    sh = {}
    f = lambda a: np.asarray(a, dtype=np.float32)
    w_mod = f(inp['w_mod'])
    sh['wmod'] = np.stack([ktile(w_mod[l], 512) for l in range(DEPTH)])
    sh['bmod'] = fm_vec(f(inp['b_mod']))
    sh['ng'] = fm_vec(f(inp['norm_g']))
    w_in = f(inp['w_in'])
    wg = np.zeros((DEPTH, len(WG_ORDER), 128, 8, 512), np.float32)
    for l in range(DEPTH):
        for gi, name in enumerate(WG_ORDER):
            cols = WG[name]
            wsel = pick_cols(w_in[l], cols)
            wg[l, gi, :, :, :len(cols)] = wsel.reshape(8, 128, len(cols)).transpose(1, 0, 2)
    sh['win'] = wg
    w_merge = f(inp['w_merge'])
    wm = w_merge.reshape(DEPTH, D, 4, 8, 128).transpose(0, 1, 3, 2, 4).reshape(DEPTH, D, 4096)
    sh['wmerge'] = np.stack([ktile(wm[l], 512) for l in range(DEPTH)])
    bm = f(inp['b_merge']).reshape(DEPTH, 4, 8, 128)
    sh['bmerge'] = np.ascontiguousarray(bm.transpose(3, 0, 2, 1))
    wb = f(inp['w_branch']).reshape(DEPTH, 4, 2, 128, 8, 128)
    sh['wbranch'] = np.ascontiguousarray(wb.transpose(0, 4, 3, 1, 2, 5))
    w_out = f(inp['w_out'])
    sh['wout'] = np.stack([ktile(w_out[l], 512) for l in range(DEPTH)])
    w_ff1 = f(inp['w_ff1'])
    sh['wff1'] = np.stack([ktile(w_ff1[l], 512) for l in range(DEPTH)])
    w_ff2 = f(inp['w_ff2'])
    sh['wff2'] = np.ascontiguousarray(w_ff2.reshape(DEPTH, 8, 4, 128, 1024).transpose(0, 1, 3, 2, 4))
    w_uq = f(inp['w_uq'])
    cols, cols_sw = [], []
    for h in range(4):
        nope = [h * 96 + i for i in range(64)]
        cols += nope + [h * 96 + 64 + i for i in range(32)]
        cols_sw += nope + [h * 96 + 64 + (i + 16) % 32 for i in range(32)]
    wuq = np.stack([np.concatenate([w_uq[l][:, cols], w_uq[l][:, cols_sw]], 1) for l in range(DEPTH)])
    sh['wuq'] = np.ascontiguousarray(wuq.reshape(DEPTH, 2, 128, 768).transpose(2, 0, 1, 3))
    w_ukv = f(inp['w_ukv']).reshape(DEPTH, 128, 4, 128)
    kn = w_ukv[:, :, :, :64].reshape(DEPTH, 128, 256)
    vv = w_ukv[:, :, :, 64:].reshape(DEPTH, 128, 256)
    sh['wukv'] = np.ascontiguousarray(np.concatenate([kn, vv], -1).transpose(1, 0, 2))
    sh['qn'] = fm_vec(f(inp['mla_q_norm']))
    sh['kvn'] = fm_vec(f(inp['mla_kv_norm']))
    sh['kvn_row'] = f(inp['mla_kv_norm']).reshape(1, DEPTH * 128)
    sh['gvn'] = fm_vec(f(inp['gmlp_v_norm']))
    bs = f(inp['gmlp_b_s'])
    bsB = np.zeros((128, DEPTH, 2, 128), np.float32)
    for kk in range(2):
        bsB[0:64, :, kk, :] = bs[:, 2 * kk, :][None]
        bsB[64:128, :, kk, :] = bs[:, 2 * kk + 1, :][None]
    sh['bsB'] = bsB
    sh['wsT'] = np.ascontiguousarray(f(inp['gmlp_w_s']).transpose(3, 0, 1, 2))
    gb = f(inp['mlstm_gate_bias'])
    gbT = np.zeros((36, DEPTH, 2), np.float32)
    for l in range(DEPTH):
        for gate in range(2):
            gbT[0:4, l, gate] = gb[l, 0, gate]
            gbT[32:36, l, gate] = gb[l, 1, gate]
    sh['gbT'] = gbT
    sh['hn_row'] = f(inp['mlstm_head_norm']).reshape(1, DEPTH * 256)
    sh['lam_row'] = f(inp['diff_lambda']).reshape(1, DEPTH * 128)
    sh['sn_row'] = f(inp['diff_sub_norm']).reshape(1, DEPTH * 64)
    cos2, sin2 = rope_tables()
    sh['cosT'] = cos2
    sh['sinT'] = sin2
    ii = np.arange(128)
    sh['ident'] = np.eye(128, dtype=np.float32)
    sh['mask_le'] = (ii[:, None] <= ii[None, :]).astype(np.float32)
    sh['mask_ge'] = (ii[:, None] >= ii[None, :]).astype(np.float32)
    sel = np.zeros((36, 4, 128), np.float32)
    for r in range(4):
        sel[r, r, :] = 1.0
        sel[32 + r, r, :] = 1.0
    sh['sel'] = sel
    return sh


def prep_core(inp, c):
    f = lambda a: np.asarray(a, dtype=np.float32)
    xp = f(inp['x_prompt'])
    xs = f(inp['x_sample'])
    xtok = np.concatenate([xp[2 * c], xp[2 * c + 1], xs[c]], axis=0)
    m = {}
    m['xT'] = np.ascontiguousarray(xtok.reshape(T, 8, 128).transpose(2, 1, 0))
    cond = np.stack([f(inp['c_ctx']), f(inp['c'])[c]], axis=-1)
    m['condT'] = np.ascontiguousarray(cond.reshape(8, 128, 2).transpose(1, 0, 2))
    m['c_ckvT'] = np.ascontiguousarray(f(inp['cache_mla_ckv'])[c].transpose(0, 2, 1))
    m['c_krT'] = np.ascontiguousarray(f(inp['cache_mla_krope'])[c].transpose(0, 2, 1))
    dk = f(inp['cache_diff_k'])[c].reshape(DEPTH, 256, 8, 32)
    dkT = np.zeros((DEPTH, 96, 3, 256), np.float32)
    for p in range(8):
        dkT[:, (p % 3) * 32:(p % 3) * 32 + 32, p // 3, :] = dk[:, :, p, :].transpose(0, 2, 1)
    m['c_dkT'] = dkT
    m['c_dv'] = np.ascontiguousarray(f(inp['cache_diff_v'])[c].reshape(DEPTH, 2, 128, 4, 64).transpose(2, 0, 1, 3, 4))
    sC = f(inp['state_mlstm_C'])[c]
    sn = f(inp['state_mlstm_n'])[c]
    c0 = np.zeros((128, DEPTH, 2, 2, 65), np.float32)
    for h in range(4):
        c0[(h % 2) * 64:(h % 2) * 64 + 64, :, :, h // 2, 0:64] = sC[:, :, h].transpose(2, 0, 1, 3)
        c0[(h % 2) * 64:(h % 2) * 64 + 64, :, :, h // 2, 64] = sn[:, :, h].transpose(2, 0, 1)
    m['c_C0'] = c0
    sm = f(inp['state_mlstm_m'])[c]
    m0T = np.zeros((36, DEPTH), np.float32)
    m0T[0:4] = sm[:, 0].T
    m0T[32:36] = sm[:, 1].T
    m['c_m0T'] = m0T
    m['c_m0row'] = np.ascontiguousarray(sm.reshape(1, DEPTH * 8))
    return m


class StopBuild(Exception):
    pass


class Builder:
    def cut(self, name):
        if self.stop_after == name:
            raise StopBuild()

    def __init__(self, debug=(), nlayers=DEPTH, stop_after=None, wplan=None):
        self.stop_after = stop_after
        self.wplan = wplan
        self.debug = set(debug)
        self.nlayers = nlayers
        self.nc = bass.Bass("TRN2", target_bir_lowering=False)
        self.es = ExitStack()
        self.S = Sched(self.nc, self.es)
        self.ins = {}
        self.outs = {}
        self.out_keys = []
        self.pools = {'all': list(range(8)), 'lo': list(range(4))}
        self.rr = {'all': 0, 'lo': 0}

    def din(self, name, shape):
        ap = self.nc.dram_tensor(name, list(shape), F32, kind="ExternalInput").ap()
        self.ins[name] = ap
        return ap

    def dout(self, name, shape):
        ap = self.nc.dram_tensor(name, list(shape), F32, kind="ExternalOutput").ap()
        self.outs[name] = ap
        return ap

    def sb(self, name, shape, dt):
        return self.es.enter_context(self.nc.sbuf_tensor(name, list(shape), dt))

    def psum(self, pool='all'):
        lst = self.pools[pool]
        b = lst[self.rr[pool] % len(lst)]
        self.rr[pool] += 1
        return b

    def rsqrt(self, out_ap, in_ap, c, reads, writes):
        cb = self.cbias(c, out_ap)
        if getattr(self, 'rsqrt_lnexp', False):
            self.S.op('act', lambda e: e.activation(out=out_ap, in_=in_ap, func=AF.Ln, bias=cb, scale=1.0),
                      reads=list(reads) + ['cbias'], writes=writes)
            self.S.op('act', lambda e: e.activation(out=out_ap, in_=out_ap, func=AF.Exp, scale=-0.5), reads=writes, writes=writes)
            return
        self.S.op('act', lambda e: e.activation(out=out_ap, in_=in_ap, func=AF.Sqrt, bias=cb, scale=1.0),
                  reads=list(reads) + ['cbias'], writes=writes)
        self.S.op('dve', lambda e: e.reciprocal(out=out_ap, in_=out_ap), reads=writes, writes=writes)

    def cbias(self, c, like=None):
        a = self._cb[round(float(c), 12)]
        if like is None:
            return a
        bp, n = like.base_partition(), like.shape[0]
        return a[bp:bp + n, :]

    def store(self, key, out_ap, in_ap, reads):
        if key not in self.out_keys:
            self.out_keys.append(key)
        self.S.dma('sp', key, out_ap, in_ap, reads=reads, writes=['dram_' + key])

    def build(self):
        nc, S, es = self.nc, self.S, self.es
        NG = len(WG_ORDER)
        xT_d = self.din('xT', [128, 8, T])
        condT_d = self.din('condT', [128, 8, 2])
        wmod_d = self.din('wmod', [DEPTH, 12, 128, 8, 512])
        bmod_d = self.din('bmod', [128, DEPTH, 48])
        ng_d = self.din('ng', [128, DEPTH, 4, 8])
        win_d = self.din('win', [DEPTH, NG, 128, 8, 512])
        wmerge_d = self.din('wmerge', [DEPTH, 8, 128, 8, 512])
        bmerge_d = self.din('bmerge', [128, DEPTH, 8, 4])
        wbranch_d = self.din('wbranch', [DEPTH, 8, 128, 4, 2, 128])
        wout_d = self.din('wout', [DEPTH, 2, 128, 8, 512])
        wff1_d = self.din('wff1', [DEPTH, 8, 128, 8, 512])
        wff2_d = self.din('wff2', [DEPTH, 8, 128, 4, 1024])
        wuq_d = self.din('wuq', [128, DEPTH, 2, 768])
        wukv_d = self.din('wukv', [128, DEPTH, 512])
        qn_d = self.din('qn', [128, DEPTH, 2])
        kvn_d = self.din('kvn', [128, DEPTH, 1])
        kvnrow_d = self.din('kvn_row', [1, DEPTH * 128])
        gvn_d = self.din('gvn', [128, DEPTH, 2])
        bsB_d = self.din('bsB', [128, DEPTH, 2, 128])
        wsT_d = self.din('wsT', [128, DEPTH, 4, 128])
        gbT_d = self.din('gbT', [36, DEPTH, 2])
        hnrow_d = self.din('hn_row', [1, DEPTH * 256])
        lamrow_d = self.din('lam_row', [1, DEPTH * 128])
        snrow_d = self.din('sn_row', [1, DEPTH * 64])
        cos_d = self.din('cosT', [128, 1024])
        sin_d = self.din('sinT', [128, 1024])
        ident_d = self.din('ident', [128, 128])
        mle_d = self.din('mask_le', [128, 128])
        mge_d = self.din('mask_ge', [128, 128])
        sel_d = self.din('sel', [36, 4, 128])
        cckv_d = self.din('c_ckvT', [DEPTH, 128, 256])
        ckr_d = self.din('c_krT', [DEPTH, 32, 256])
        cdk_d = self.din('c_dkT', [DEPTH, 96, 3, 256])
        cdv_d = self.din('c_dv', [128, DEPTH, 2, 4, 64])
        cC0_d = self.din('c_C0', [128, DEPTH, 2, 2, 65])
        cm0T_d = self.din('c_m0T', [36, DEPTH])
        cm0row_d = self.din('c_m0row', [1, DEPTH * 8])
        yT_d = self.dout('yT', [128, 8, T])
        o_ckv = self.dout('o_ckv', [2, DEPTH, 256, 128])
        o_kr = self.dout('o_kr', [2, DEPTH, 256, 32])
        o_dk = self.dout('o_dk', [2, DEPTH, 256, 256])
        o_dv = self.dout('o_dv', [2, DEPTH, 256, 256])
        o_C = self.dout('o_C', [2, DEPTH, 2, 4, 64, 64])
        o_n = self.dout('o_n', [2, DEPTH, 2, 4, 64])
        o_m = self.dout('o_m', [2, DEPTH, 2, 4])

        xT = self.sb('xT_sb', [128, 8, T], F32)
        hT = self.sb('hT_sb', [128, 8, T], BF16)
        AR_N = 43008
        arena = self.sb('arena', [128, AR_N], BF16)
        WSLOT = 3
        wbuf = [self.sb(f'wbuf{i}', [128, 4096], BF16) for i in range(WSLOT)]
        wbr = [self.sb(f'wbr{i}', [128, 1024], BF16) for i in range(2)]
        ones = self.sb('ones', [128, 128], BF16)
        identb = self.sb('identb', [128, 128], BF16)
        identf = self.sb('identf', [36, 36], F32)
        mle = self.sb('mle', [128, 128], BF16)
        mge = self.sb('mge', [128, 128], BF16)
        sel = self.sb('sel_sb', [36, 4, 128], F32)
        cosT = self.sb('cos_sb', [128, 1024], F32)
        sinT = self.sb('sin_sb', [128, 1024], F32)
        condT = self.sb('condT_sb', [128, 8, 2], F32)
        scond = self.sb('scond', [128, 8, 2], BF16)
        bmod = self.sb('bmod_sb', [128, DEPTH, 48], F32)
        ng = self.sb('ng_sb', [128, DEPTH, 4, 8], F32)
        bmerge = self.sb('bmerge_sb', [128, DEPTH, 8, 4], F32)
        modT = self.sb('modT', [128, 48, 2], F32)
        msc = self.sb('msc', [128, 6, 8, 2], F32)
        wuq = self.sb('wuq_sb', [128, 1, 2, 768], BF16)
        wukv = self.sb('wukv_sb', [128, 1, 512], BF16)
        qn = self.sb('qn_sb', [128, DEPTH, 2], F32)
        kvn = self.sb('kvn_sb', [128, DEPTH, 1], F32)
        kvnB = self.sb('kvnB', [128, 128], F32)
        gvn = self.sb('gvn_sb', [128, DEPTH, 2], F32)
        bsB = self.sb('bsB_sb', [128, 1, 2, 128], F32)
        wsT = self.sb('wsT_sb', [128, 1, 4, 128], BF16)
        gbT = self.sb('gbT_sb', [36, DEPTH, 2], F32)
        hnB = self.sb('hnB', [128, 256], F32)
        lamB = self.sb('lamB', [128, 128], F32)
        snB = self.sb('snB', [128, 64], F32)
        m0T = self.sb('m0T', [36, DEPTH], F32)
        m0B = self.sb('m0B', [128, DEPTH * 8], F32)
        C0 = self.sb('C0_sb', [128, DEPTH, 2, 2, 65], BF16)
        smalls = self.sb('smalls', [128, 64], F32)
        zero1 = self.sb('zero1', [128, 1], F32)
        ps = [es.enter_context(nc.psum_tensor(f'ps{i}', [128, 512], F32)) for i in range(8)]

        def carve(off_b, shape, dt):
            esz = 2 if dt == BF16 else 4
            n = int(np.prod(shape[1:]))
            a = arena[0:shape[0], off_b // 2: off_b // 2 + n * esz // 2]
            if dt != BF16:
                a = a.bitcast(dt)
            if len(shape) == 3:
                a = a.rearrange("p (a b) -> p a b", a=shape[1])
            elif len(shape) == 4:
                a = a.rearrange("p (a b c) -> p a b c", a=shape[1], b=shape[2])
            return a
        KB = 1024
        assert AR_N * 2 == 84 * KB
        ybr = carve(0, [128, 4, 2, T], BF16)
        mrg = carve(24 * KB, [128, 8, T], BF16)
        ffo = carve(0, [128, 8, T], F32)
        f1g = [carve(48 * KB + i * 12 * KB, [128, 4, T], BF16) for i in range(2)]
        yf = [carve(48 * KB + i * 12 * KB, [128, 8, 384], F32)[:, :, 0:384] for i in range(2)]
        sq = carve(72 * KB, [128, 8, 512], BF16)
        rstd = [carve(80 * KB + i * 2 * KB, [128, 512], F32) for i in range(2)]
        MS = 24 * KB

        self.wslot = 0

        self.wreq = []
        self.wissued = 0

        def w_issue(j):
            name, off, apl, ncols = self.wplan[j]
            src = bass.AP(self.ins[name].tensor, off, [list(x) for x in apl])
            i = j % WSLOT
            S.dma('pool', f'w{i}', wbuf[i][:, 0:ncols], src, writes=[f'wbuf{i}'])

        def wload(src3, ncols_total):
            j = len(self.wreq)
            desc = (src3.name, int(src3.offset), tuple(tuple(x) for x in src3.ap), int(ncols_total))
            self.wreq.append(desc)
            i = j % WSLOT
            if self.wplan is None:
                S.dma('pool', f'w{i}', wbuf[i][:, 0:ncols_total], src3, writes=[f'wbuf{i}'])
            else:
                assert self.wplan[j] == desc, (j, self.wplan[j], desc)
                while self.wissued <= min(j + 1, len(self.wplan) - 1):
                    w_issue(self.wissued)
                    self.wissued += 1
            return wbuf[i], f'wbuf{i}'

        def tl(tt):
            return slice(tt * 512, (tt + 1) * 512)

        self.rs_i = 0

        def rms_rstd(src3, src_key, nk, dtot, n=512):
            i = self.rs_i
            self.rs_i ^= 1
            b = self.psum()
            kf = src_key if callable(src_key) else (lambda k: src_key)
            for k in range(nk):
                S.op('act', lambda e: e.activation(out=sq[:, k, 0:n], in_=src3[:, k, :], func=AF.Square),
                     reads=[kf(k)], writes=[f'sq{k}'])
                S.op('pe', lambda e: e.matmul(ps[b][:, 0:n], lhsT=ones[:, :], rhs=sq[:, k, 0:n],
                                              start=(k == 0), stop=(k == nk - 1)),
                     reads=[f'sq{k}', 'ones'], writes=[f'ps{b}'], inc=(k == nk - 1))
            self.rsqrt(rstd[i][:, 0:n], ps[b][:, 0:n], float(dtot * EPS), [f'ps{b}'], [f'rstd{i}'])
            return rstd[i], f'rstd{i}'

        def fm_mm(out_ap, okey, w3, wk, c0, M, rhs_fn, rkeys, nk=8):
            for k in range(nk):
                rk_ = [(x + f'_{k}') if x.startswith('hT') else x for x in rkeys]
                S.op('pe', lambda e: e.matmul(out_ap, lhsT=w3[:, k, c0:c0 + M], rhs=rhs_fn(k),
                                              start=(k == 0), stop=(k == nk - 1)),
                     reads=[wk] + rk_, writes=[okey], inc=(k == nk - 1))

        def tm_mm(out_ap, okey, w3, wk, c0, N, tok0, nk=8):
            tt = tok0 // 512
            for k in range(nk):
                S.op('pe', lambda e: e.matmul(out_ap, lhsT=hT[:, k, tok0:tok0 + 128], rhs=w3[:, k, c0:c0 + N],
                                              start=(k == 0), stop=(k == nk - 1)),
                     reads=[wk, f'hT{tt}_{k}'], writes=[okey], inc=(k == nk - 1))

        def to_fm(src_bf, skey, br, blk):
            b = self.psum('lo')
            pb_ = ps[b][:, :].bitcast(BF16)
            for kk in range(2):
                S.op('pe', lambda e: e.transpose(out=pb_[:, kk * 128:(kk + 1) * 128], in_=src_bf[:, kk * 128:(kk + 1) * 128],
                                                 identity=identb[:, :]),
                     reads=[skey, 'identb'], writes=[f'ps{b}'], inc=(kk == 1))
            S.op('act', lambda e: e.copy(out=ybr[:, br, :, blk * 128:(blk + 1) * 128],
                                         in_=pb_[:, 0:256].rearrange("p (k t) -> p k t", k=2)),
                 reads=[f'ps{b}'], writes=['ybr'])

        S.op('pool', lambda e: e.memset(ones[:], 1.0), writes=['ones'])
        S.op('pool', lambda e: e.memset(zero1[:], 0.0), writes=['zero1'])
        cbt = self.sb('cbt', [128, 8], F32)
        self._cb = {}
        for i, c in enumerate([D * EPS, 256 * EPS, 128 * EPS, 64 * EPS, 1.0]):
            S.op('pool', lambda e: e.memset(cbt[:, i:i + 1], float(c)), writes=['cbias'])
            self._cb[round(float(c), 12)] = cbt[:, i:i + 1]
        S.dma('sp', 'x', xT[:], xT_d, writes=[f'xT{a}_{b}' for a in range(NT) for b in range(8)])
        cl = [(condT, condT_d, 'condT'), (bmod, bmod_d, 'bmod'), (ng, ng_d, 'ng'), (bmerge, bmerge_d, 'bmerge'),
              (qn, qn_d, 'qn'), (kvn, kvn_d, 'kvn'), (gvn, gvn_d, 'gvn'), (gbT, gbT_d, 'gbT'),
              (cosT, cos_d, 'cosT'), (sinT, sin_d, 'sinT'), (sel, sel_d, 'sel'), (m0T, cm0T_d, 'm0T'),
              (identf, ident_d[0:36, 0:36], 'identf')]
        for (dst, src, key) in cl:
            S.dma('sp', 'c_' + key, dst[:], src, writes=[key])
        for (dst, src, key, n) in [(m0B, cm0row_d, 'm0B', DEPTH * 8)]:
            S.dma('sp', 'c_' + key, dst[:], src.partition_broadcast(128), writes=[key])
        for (dst, src, key) in [(identb, ident_d, 'identb'), (mle, mle_d, 'mle'), (mge, mge_d, 'mge'),
                                (C0, cC0_d, 'C0')]:
            S.dma('pool', 'c_' + key, dst[:], src, writes=[key])
        S.op('act', lambda e: e.activation(out=scond[:], in_=condT[:], func=AF.Silu),
             reads=['condT'], writes=['scond'])

        def norm_rms(tt):
            return rms_rstd(xT[:, :, tl(tt)], (lambda k, tt=tt: f'xT{tt}_{k}'), 8, D)

        def norm_apply(l, which, tt, rr_):
            ia, ib = (0, 1) if which == 0 else (3, 4)
            c = 0 if tt == 0 else 1
            r, rk = rr_
            for k in range(8):
                t_ = carve(18 * KB + (k % 2) * 2 * KB, [128, 512], F32)
                tk = f'nm_tmp{k % 2}'
                S.op('dve', lambda e: e.scalar_tensor_tensor(
                    out=t_, in0=xT[:, k, tl(tt)], scalar=msc[:, ia, k, c:c + 1], in1=r[:, :],
                    op0=ALU.mult, op1=ALU.mult), reads=[f'xT{tt}_{k}', 'msc', rk], writes=[tk])
                S.op('act', lambda e: e.activation(out=hT[:, k, tl(tt)], in_=t_, func=AF.Identity,
                                                   bias=msc[:, ib, k, c:c + 1], scale=1.0),
                     reads=[tk, 'msc'], writes=[f'hT{tt}_{k}'])

        def norm_mod_tile(l, which, tt):
            norm_apply(l, which, tt, norm_rms(tt))

        def norm_mod(l, which):
            for tt in range(NT):
                norm_mod_tile(l, which, tt)

        def resid_apply(src3, skey, tt, ig, tmp_off, rr_):
            c = 0 if tt == 0 else 1
            kf = skey if callable(skey) else (lambda k: skey)
            r, rk = rr_
            for k in range(8):
                t_ = carve(tmp_off + (k % 2) * 2 * KB, [128, 512], F32)
                tk = f'rs_tmp{k % 2}'
                S.op('dve', lambda e: e.scalar_tensor_tensor(
                    out=t_, in0=src3[:, k, :], scalar=msc[:, ig, k, c:c + 1], in1=r[:, :],
                    op0=ALU.mult, op1=ALU.mult), reads=[kf(k), 'msc', rk], writes=[tk])
                S.op('dve', lambda e: e.tensor_tensor(out=xT[:, k, tl(tt)], in0=xT[:, k, tl(tt)], in1=t_,
                                                      op=ALU.add), reads=[tk, f'xT{tt}_{k}'], writes=[f'xT{tt}_{k}'])

        def resid(src3, skey, tt, ig, tmp_off):
            resid_apply(src3, skey, tt, ig, tmp_off, rms_rstd(src3, skey, 8, D))

        modN = self.sb('modN', [128, DEPTH, 48, 2], F32)

        def p0_step(lay, g, scratch_off):
            wt, wk = wload(wmod_d[lay, g].rearrange("p k j -> p (k j)"), 4096)
            w3 = wt[:, 0:4096].rearrange("p (k j) -> p k j", k=8)
            bg = self.psum('lo')
            for k in range(8):
                S.op('pe', lambda e: e.matmul(ps[bg][0:2, :], lhsT=scond[:, k, :], rhs=w3[:, k, :], start=(k == 0), stop=(k == 7)),
                     reads=[wk, 'scond'], writes=[f'ps{bg}'], inc=(k == 7))
            mt = carve(scratch_off, [2, 512], F32)
            S.op('act', lambda e: e.copy(out=mt, in_=ps[bg][0:2, :]), reads=[f'ps{bg}'], writes=['mtmp'])
            bt = self.psum('lo')
            for jj in range(4):
                S.op('pe', lambda e: e.matmul(ps[bt][:, 2 * jj:2 * jj + 2], lhsT=mt[0:2, jj * 128:(jj + 1) * 128], rhs=identf[0:2, 0:2],
                                              start=True, stop=True), reads=['mtmp', 'identf'], writes=[f'ps{bt}'], inc=(jj == 3))
            pm = ps[bt][:, 0:8].rearrange("p (j c) -> p j c", c=2)
            S.op('dve', lambda e: e.tensor_tensor(out=modN[:, lay, 4 * g:4 * g + 4, :], in0=pm,
                                                  in1=bmod[:, lay, 4 * g:4 * g + 4].unsqueeze(2).to_broadcast([128, 4, 2]), op=ALU.add),
                 reads=[f'ps{bt}', 'bmod'], writes=['modN'])

        self.p0_queue = [(lay, g) for lay in range(self.nlayers) for g in range(12)]

        def p0_flush(lay, gmax, scratch_off):
            while self.p0_queue and (self.p0_queue[0][0] < lay or (self.p0_queue[0][0] == lay and self.p0_queue[0][1] <= gmax)):
                la, g = self.p0_queue.pop(0)
                p0_step(la, g, scratch_off)

        def p0_hook(scratch_off):
            if self.p0_queue:
                la, g = self.p0_queue.pop(0)
                p0_step(la, g, scratch_off)
        self.p0_hook = p0_hook

        def msc_derive(l, which):
            for c in range(2):
                for (dst, isc, ign) in (((0, 1, 0),) if which == 0 else ((3, 4, 2),)):
                    S.op('dve', lambda e: e.scalar_tensor_tensor(
                        out=msc[:, dst, :, c], in0=modN[:, l, isc * 8:(isc + 1) * 8, c], scalar=1.0,
                        in1=ng[:, l, ign, :], op0=ALU.add, op1=ALU.mult), reads=['modN', 'ng'], writes=['msc'])
                    S.op('dve', lambda e: e.tensor_scalar(
                        out=msc[:, dst, :, c], in0=msc[:, dst, :, c], scalar1=32.0, scalar2=None, op0=ALU.mult),
                        reads=['msc'], writes=['msc'])
                for (dst, ish) in (((1, 0),) if which == 0 else ((4, 3),)):
                    S.op('dve', lambda e: e.tensor_copy(out=msc[:, dst, :, c], in_=modN[:, l, ish * 8:(ish + 1) * 8, c]),
                         reads=['modN'], writes=['msc'])
                if which == 1:
                    for (dst, ig, ign) in ((2, 2, 1), (5, 5, 3)):
                        S.op('dve', lambda e: e.scalar_tensor_tensor(
                            out=msc[:, dst, :, c], in0=modN[:, l, ig * 8:(ig + 1) * 8, c], scalar=32.0,
                            in1=ng[:, l, ign, :], op0=ALU.mult, op1=ALU.mult), reads=['modN', 'ng'], writes=['msc'])

        try:
          self.cut('init')
          for l in range(self.nlayers):
              S.dma('sp', 'c_bsB', bsB[:, 0], bsB_d[:, l], writes=['bsB'])
              S.dma('sp', 'c_kvnB', kvnB[:], kvnrow_d[:, l * 128:(l + 1) * 128].partition_broadcast(128), writes=['kvnB'])
              S.dma('sp', 'c_hnB', hnB[:], hnrow_d[:, l * 256:(l + 1) * 256].partition_broadcast(128), writes=['hnB'])
              S.dma('sp', 'c_lamB', lamB[:], lamrow_d[:, l * 128:(l + 1) * 128].partition_broadcast(128), writes=['lamB'])
              S.dma('sp', 'c_snB', snB[:], snrow_d[:, l * 64:(l + 1) * 64].partition_broadcast(128), writes=['snB'])
              S.dma('pool', 'c_wuq', wuq[:, 0], wuq_d[:, l], writes=['wuq'])
              S.dma('pool', 'c_wukv', wukv[:, 0], wukv_d[:, l], writes=['wukv'])
              S.dma('pool', 'c_wsT', wsT[:, 0], wsT_d[:, l], writes=['wsT'])
              p0_flush(l, 3, MS)
              msc_derive(l, 0)

              nrm = {0: norm_rms(0), 1: norm_rms(1)}

              def p1_tile(tt):
                  norm_apply(l, 0, tt, nrm[tt])
                  if tt == 0:
                      nrm[2] = norm_rms(2)
              self.mixer_B(l, locals(), pre_tile=p1_tile)
              S.barrier()
              self.cut(f'B_{l}')
              for samp in (False, True):
                  self.rsqrt_lnexp = True
                  self.mixer_A(l, locals(), samp)
                  S.barrier()
                  self.cut(f'A{int(samp)}_{l}')
                  self.mixer_D(l, locals(), samp)
                  S.barrier()
                  self.cut(f'D{int(samp)}_{l}')
                  self.mixer_C(l, locals(), samp)
                  S.barrier()
                  self.cut(f'C{int(samp)}_{l}')
              self.rsqrt_lnexp = False

              if l == 0 and 'ybr' in self.debug:
                  o = self.dout('dbg_ybr', [128, 8 * T])
                  S.dma('pool', 'dbg_ybr', o, arena[:, 0:8 * T], reads=['ybr'], writes=['dbgo1'])
                  self.out_keys.append('dbg_ybr')
              gsb_all = [carve(MS + 24 * KB + i * 2 * KB, [128, 512], F32) for i in range(8)]
              self.gs_i = 0
              for n in range(8):
                  wt, wk = wload(wmerge_d[l, n].rearrange("p k j -> p (k j)"), 4096)
                  wm3 = wt[:, 0:4096].rearrange("p (k j) -> p k j", k=8)
                  ib = n % 2
                  if n == 0:
                      S.dma('pool', 'wbr0', wbr[0][:, :], wbranch_d[l, 0].rearrange("p b k j -> p (b k j)"), writes=['wbr0'])
                  if n + 1 < 8:
                      S.dma('pool', f'wbr{1 - ib}', wbr[1 - ib][:, :], wbranch_d[l, n + 1].rearrange("p b k j -> p (b k j)"),
                            writes=[f'wbr{1 - ib}'])
                  wb4 = wbr[ib][:, :].rearrange("p (b k j) -> p b k j", b=4, k=2)
                  for tt in range(NT):
                      self.gs_i ^= 1
                      gsb = gsb_all[4 * self.gs_i:4 * self.gs_i + 4]
                      go = 4 * self.gs_i
                      for br in range(4):
                          bg = self.psum()
                          fm_mm(ps[bg][:, :], f'ps{bg}', wm3, wk, br * 128, 128, lambda k: hT[:, k, tl(tt)], [f'hT{tt}'])
                          S.op('act', lambda e: e.activation(
                              out=gsb[br], in_=ps[bg][:, :], func=AF.Sigmoid, bias=bmerge[:, l, n, br:br + 1], scale=1.0),
                              reads=[f'ps{bg}', 'bmerge'], writes=[f'gsb{go + br}'])
                          bb = self.psum()
                          for kk in range(2):
                              S.op('pe', lambda e: e.matmul(ps[bb][:, :], lhsT=wb4[:, br, kk, :], rhs=ybr[:, br, kk, tl(tt)],
                                                            start=(kk == 0), stop=(kk == 1)),
                                   reads=[f'wbr{ib}', 'ybr'], writes=[f'ps{bb}'], inc=(kk == 1))
                          S.op('dve', lambda e: e.tensor_tensor(out=gsb[br], in0=ps[bb][:, :], in1=gsb[br], op=ALU.mult),
                               reads=[f'ps{bb}', f'gsb{go + br}'], writes=[f'gsb{go + br}'])
                      S.op('pool', lambda e: e.tensor_tensor(out=gsb[0], in0=gsb[0], in1=gsb[1], op=ALU.add),
                           reads=[f'gsb{go}', f'gsb{go + 1}'], writes=[f'gsb{go}'])
                      S.op('pool', lambda e: e.tensor_tensor(out=gsb[2], in0=gsb[2], in1=gsb[3], op=ALU.add),
                           reads=[f'gsb{go + 2}', f'gsb{go + 3}'], writes=[f'gsb{go + 2}'])
                      S.op('pool', lambda e: e.tensor_tensor(out=mrg[:, n, tl(tt)], in0=gsb[0], in1=gsb[2], op=ALU.add),
                           reads=[f'gsb{go}', f'gsb{go + 2}'], writes=['mrg'])
              S.barrier()

              if l == 0 and 'mrg' in self.debug:
                  o = self.dout('dbg_mrg', [128, 8 * T])
                  S.dma('pool', 'dbg_mrg', o, arena[:, 8 * T:16 * T], reads=['mrg'], writes=['dbgo2'])
                  self.out_keys.append('dbg_mrg')
              self.cut(f'P4_{l}')
              wo = []
              for g in range(2):
                  wt, wk = wload(wout_d[l, g].rearrange("p k j -> p (k j)"), 4096)
                  wo.append((wt[:, 0:4096].rearrange("p (k j) -> p k j", k=8), wk))
              p0_flush(l, 11, 64 * KB + 4 * KB)
              msc_derive(l, 1)
              yfs = [carve(0, [128, 8, 512], F32), carve(48 * KB, [128, 8, 512], F32)]

              def p5_mm(tt):
                  yfull = yfs[tt % 2]
                  yk = f'yfull{tt % 2}'
                  for n in range(8):
                      w3, wk = wo[n // 4]
                      b = self.psum()
                      fm_mm(ps[b][:, :], f'ps{b}', w3, wk, (n % 4) * 128, 128, lambda k: mrg[:, k, tl(tt)], ['mrg'])
                      S.op('act', lambda e: e.copy(out=yfull[:, n, :], in_=ps[b][:, :]), reads=[f'ps{b}'], writes=[f'{yk}_{n}'])

              def p5_rms(tt):
                  return rms_rstd(yfs[tt % 2], (lambda k, tt=tt: f'yfull{tt % 2}_{k}'), 8, D)

              def p5_app(tt, rr_):
                  resid_apply(yfs[tt % 2], (lambda k, tt=tt: f'yfull{tt % 2}_{k}'), tt, 2, 64 * KB, rr_)
                  if l == 0 and 'x1' in self.debug and tt == NT - 1:
                      o = self.dout('dbg_x1', [128, 8, T])
                      S.dma('sp', 'dbg_x1', o, xT[:], reads=[f'xT{a}_{b}' for a in range(NT) for b in range(8)], writes=['dbgo3'])
                      self.out_keys.append('dbg_x1')

              p5_mm(0)
              p5_mm(1)
              p5_app(0, p5_rms(0))
              p5_mm(2)
              n0 = norm_rms(0)
              r1 = p5_rms(1)
              norm_apply(l, 1, 0, n0)
              p5_app(1, r1)
              n1 = norm_rms(1)
              r2 = p5_rms(2)
              norm_apply(l, 1, 1, n1)
              p5_app(2, r2)
              norm_apply(l, 1, 2, norm_rms(2))
              self.cut(f'P5_{l}')
              S.barrier()

              for g in range(8):
                  wt, wk = wload(wff1_d[l, g].rearrange("p k j -> p (k j)"), 4096)
                  w3 = wt[:, 0:4096].rearrange("p (k j) -> p k j", k=8)
                  fi = g % 2
                  for jj in range(4):
                      for tt in range(NT):
                          b = self.psum()
                          fm_mm(ps[b][:, :], f'ps{b}', w3, wk, jj * 128, 128, lambda k: hT[:, k, tl(tt)], [f'hT{tt}'])
                          self.ft_i = (getattr(self, 'ft_i', 0) + 1) % 2
                          ftmp = carve(72 * KB + self.ft_i * 2 * KB, [128, 512], F32)
                          S.op('act', lambda e: e.activation(out=ftmp, in_=ps[b][:, :], func=AF.Relu),
                               reads=[f'ps{b}'], writes=[f'sq{2 * self.ft_i}', f'sq{2 * self.ft_i + 1}'])
                          S.op('act', lambda e: e.activation(out=f1g[fi][:, jj, tl(tt)], in_=ftmp, func=AF.Square),
                               reads=[f'sq{2 * self.ft_i}', f'sq{2 * self.ft_i + 1}'], writes=[f'f1g{fi}'])
                  wt2, wk2 = wload(wff2_d[l, g].rearrange("p k j -> p (k j)"), 4096)
                  w23 = wt2[:, 0:4096].rearrange("p (k j) -> p k j", k=4)
                  for n in range(8):
                      for tt in range(NT):
                          b = self.psum()
                          fm_mm(ps[b][:, :], f'ps{b}', w23, wk2, n * 128, 128, lambda k: f1g[fi][:, k, tl(tt)], [f'f1g{fi}'], nk=4)
                          if g == 0:
                              S.op('act', lambda e: e.copy(out=ffo[:, n, tl(tt)], in_=ps[b][:, :]),
                                   reads=[f'ps{b}'], writes=[f'ffo{tt}'])
                          else:
                              S.op('dve', lambda e: e.tensor_tensor(out=ffo[:, n, tl(tt)], in0=ps[b][:, :], in1=ffo[:, n, tl(tt)],
                                                                    op=ALU.add),
                                   reads=[f'ps{b}', f'ffo{tt}'], writes=[f'ffo{tt}'])
              S.barrier()
              for tt in range(NT):
                  resid(ffo[:, :, tl(tt)], f'ffo{tt}', tt, 5, 48 * KB)
              S.barrier()

        except StopBuild:
            S.barrier()
        self.store('out_y', yT_d, xT[:], [f'xT{a}_{b}' for a in range(NT) for b in range(8)])
        S.finish(self.out_keys)
        return nc

    def mixer_B(self, l, L, pre_tile=None):
        S = self.S
        g_ = lambda n: L[n]
        ps, hT, ybr, carve, wload, fm_mm, tm_mm, win_d = (g_('ps'), g_('hT'), g_('ybr'), g_('carve'), g_('wload'),
                                                             g_('fm_mm'), g_('tm_mm'), g_('win_d'))
        gvn, bsB, wsT, smalls, tl, MS, KB = g_('gvn'), g_('bsB'), g_('wsT'), g_('smalls'), g_('tl'), g_('MS'), g_('KB')
        gB1, gB2 = WG_ORDER.index('B1'), WG_ORDER.index('B2')
        wt1, wk1 = wload(win_d[l, gB1][:, :, 0:256], 8 * 256)
        w31 = wt1[:, 0:2048].rearrange("p (k j) -> p k j", k=8)
        wt2, wk2 = wload(win_d[l, gB2][:, :, 0:256], 8 * 256)
        w32 = wt2[:, 0:2048].rearrange("p (k j) -> p k j", k=8)
        vr = [carve(MS + i * 512, [128, 256], BF16) for i in range(2)]
        junks = [carve(MS + 2 * KB + i * KB, [128, 256], F32) for i in range(2)]
        tmpms = [carve(MS + 4 * KB + i * KB, [128, 2, 128], F32) for i in range(2)]
        for tt in range(NT):
            if pre_tile is not None:
                pre_tile(tt)
            ub = [5 + (2 * tt) % 3, 5 + (2 * tt + 1) % 3]
            for kk in range(2):
                fm_mm(ps[ub[kk]][:, :], f'ps{ub[kk]}', w31, wk1, kk * 128, 128, lambda k: hT[:, k, tl(tt)], [f'hT{tt}'])
            for bi in range(4):
                blk = tt * 4 + bi
                vb = self.psum('lo')
                tm_mm(ps[vb][:, 0:256], f'ps{vb}', w32, wk2, 0, 256, blk * 128)
                pq = bi % 2
                junk, tmpm = junks[pq], tmpms[pq]
                ss = smalls[:, 2 * pq:2 * pq + 1]
                rs_ = smalls[:, 2 * pq + 1:2 * pq + 2]
                S.op('act', lambda e: e.activation(out=junk, in_=ps[vb][:, 0:256], func=AF.Square, accum_out=ss),
                     reads=[f'ps{vb}'], writes=[f'junkB{pq}', f'ssB{pq}'])
                self.rsqrt(rs_, ss, float(256 * EPS), [f'ssB{pq}'], [f'rsB{pq}'])
                v_ = vr[bi % 2]
                S.op('dve', lambda e: e.tensor_scalar(out=v_, in0=ps[vb][:, 0:256], scalar1=rs_, scalar2=16.0,
                                                      op0=ALU.mult, op1=ALU.mult),
                     reads=[f'ps{vb}', f'rsB{pq}'], writes=[f'vrB{bi % 2}'])
                mb = self.psum('lo')
                for g in range(4):
                    S.op('pe', lambda e: e.matmul(ps[mb][(g % 2) * 64:(g % 2) * 64 + 64, (g // 2) * 128:(g // 2) * 128 + 128],
                                                  lhsT=v_[:, g * 64:(g + 1) * 64], rhs=wsT[:, 0, g, :], start=True, stop=True),
                         reads=[f'vrB{bi % 2}', 'wsT'], writes=[f'ps{mb}'], inc=(g == 3))
                for kk in range(2):
                    S.op('dve', lambda e: e.scalar_tensor_tensor(
                        out=tmpm[:, kk, :], in0=ps[mb][:, kk * 128:(kk + 1) * 128], scalar=gvn[:, l, kk:kk + 1],
                        in1=bsB[:, 0, kk, :], op0=ALU.mult, op1=ALU.add),
                        reads=[f'ps{mb}', 'gvn', 'bsB'], writes=[f'tmpmB{pq}'])
                    S.op('dve', lambda e: e.tensor_tensor(
                        out=ybr[:, 1, kk, blk * 128:(blk + 1) * 128], in0=ps[ub[kk]][:, bi * 128:(bi + 1) * 128],
                        in1=tmpm[:, kk, :], op=ALU.mult),
                        reads=[f'ps{ub[kk]}', f'tmpmB{pq}'], writes=['ybr'])

    def attn_core(self, L, streams, nkc, Nt, scale):
        S = self.S
        ps, carve, MS, KB = L['ps'], L['carve'], L['MS'], L['KB']
        nb = Nt // 128
        qs = [st[0]() for st in streams]

        def stage1(si, sc):
            qa, qk = qs[si]
            ka, kk_ = streams[si][1](sc)
            sb_ = self.psum('lo')
            S.op('pe', lambda e: e.matmul(ps[sb_][:, 0:Nt], lhsT=ka, rhs=qa, start=True, stop=True),
                 reads=[kk_, qk], writes=[f'ps{sb_}'])
            pi = self.pt_i
            self.pt_i = (self.pt_i + 1) % 6
            pT = carve(MS + 42 * KB + pi * KB, [128, 512], BF16)
            S.op('act', lambda e: e.activation(out=pT[:, 0:Nt], in_=ps[sb_][:, 0:Nt], func=AF.Exp, scale=float(scale)),
                 reads=[f'ps{sb_}'], writes=[f'pT{pi}'])
            return pT, pi

        cur = [stage1(si, 0) for si in range(len(streams))]
        for sc in range(nkc):
            nxt = [stage1(si, sc + 1) if sc + 1 < nkc else None for si in range(len(streams))]
            for si, st in enumerate(streams):
                pT, pi = cur[si]
                va, vk = st[2](sc)
                accb = st[3]
                for tb in range(nb):
                    S.op('pe', lambda e: e.matmul(ps[accb][:, tb * 65:(tb + 1) * 65], lhsT=pT[:, tb * 128:(tb + 1) * 128], rhs=va,
                                                  start=(sc == 0 and tb == 0), stop=(sc == nkc - 1 and tb == nb - 1),
                                                  skip_group_check=True),
                         reads=[f'pT{pi}', vk], writes=[f'ps{accb}'], inc=(tb == nb - 1))
            cur = nxt

    @staticmethod
    def geo(samp):
        if samp:
            return dict(t0=512, nt=1024, tiles=[1, 2], seqs=[(0, 1024)], kofs=256, nkb=10)
        return dict(t0=0, nt=512, tiles=[0], seqs=[(0, 256), (256, 256)], kofs=0, nkb=4)

    def mixer_A(self, l, L, samp):
        S = self.S
        g_ = lambda n: L[n]
        ps, hT, carve, wload, fm_mm, tm_mm, win_d, tl, MS, KB = (g_('ps'), g_('hT'), g_('carve'), g_('wload'), g_('fm_mm'),
                                                                  g_('tm_mm'), g_('win_d'), g_('tl'), g_('MS'), g_('KB'))
        wuq, wukv, qn, kvn, kvnB, cosT, sinT, smalls = (g_('wuq'), g_('wukv'), g_('qn'), g_('kvn'), g_('kvnB'), g_('cosT'),
                                                         g_('sinT'), g_('smalls'))
        rms_rstd, to_fm = g_('rms_rstd'), g_('to_fm')
        G = self.geo(samp)
        t0, tiles, kofs, nkb = G['t0'], G['tiles'], G['kofs'], G['nkb']
        nk = nkb * 128
        self.pt_i = 0
        qT = carve(MS, [96, 4, 1024], BF16)
        ckvnT = carve(MS + 8 * KB, [128, 1280], BF16)
        krT = carve(MS + 8 * KB + 2560, [96, 1280], BF16)
        KT = carve(MS + 13 * KB, [96, 4, 1280], BF16)
        cqf = carve(MS + 13 * KB, [128, 2, 512], F32)
        cqn = carve(MS + 17 * KB, [128, 2, 512], BF16)
        ckf = carve(MS + 19 * KB, [128, 1, 512], F32)
        vaug = carve(MS + 23 * KB, [128, 10, 4, 65], BF16)
        krf = carve(MS + 28 * KB + 512, [96, 1024], F32)
        rt = [carve(MS + 32 * KB + 512 + i * KB, [96, 256], F32) for i in range(2)]
        ya = [carve(MS + 34 * KB + 512 + i * 512, [128, 256], BF16) for i in range(4)]
        stage = carve(MS + 36 * KB + 512, [128, 160], F32)

        if samp:
            S.dma('pool', 'ctxA0', ckvnT[:, 0:256], L['cckv_d'][l], writes=['ckvnT'])
            S.dma('pool', 'ctxA1', krT[64:96, 0:256], L['ckr_d'][l], writes=['krT'])
        gA1, gA2, gA3 = WG_ORDER.index('A1'), WG_ORDER.index('A2'), WG_ORDER.index('A3')
        wt, wk = wload(win_d[l, gA1][:, :, 0:480], 8 * 480)
        w3 = wt[:, 0:3840].rearrange("p (k j) -> p k j", k=8)
        for kk in range(2):
            S.op('dve', lambda e: e.tensor_scalar(out=smalls[:, 2 + kk:3 + kk], in0=qn[:, l, kk:kk + 1], scalar1=16.0,
                                                  scalar2=None, op0=ALU.mult), reads=['qn'], writes=['qn16'])
        S.op('dve', lambda e: e.tensor_scalar(out=smalls[:, 4:5], in0=kvn[:, l, 0:1], scalar1=float(math.sqrt(128.0)),
                                              scalar2=None, op0=ALU.mult), reads=['kvn'], writes=['kvn11'])
        for tt in tiles:
            lc = (tt - tiles[0]) * 512
            kc = kofs + lc
            rh = lambda k: hT[:, k, tl(tt)]
            for kk in range(2):
                b = self.psum('lo')
                fm_mm(ps[b][:, :], f'ps{b}', w3, wk, kk * 128, 128, rh, [f'hT{tt}'])
                S.op('act', lambda e: e.copy(out=cqf[:, kk, :], in_=ps[b][:, :]), reads=[f'ps{b}'], writes=['cqf'])
            r, rk = rms_rstd(cqf[:, :, :], 'cqf', 2, 256)
            for kk in range(2):
                S.op('dve', lambda e: e.scalar_tensor_tensor(out=cqn[:, kk, :], in0=cqf[:, kk, :], scalar=smalls[:, 2 + kk:3 + kk],
                                                             in1=r[:, :], op0=ALU.mult, op1=ALU.mult),
                     reads=['cqf', 'qn16', rk], writes=['cqn'])
            for h in range(4):
                bq = self.psum('lo')
                for kk in range(2):
                    S.op('pe', lambda e: e.matmul(ps[bq][0:96, :], lhsT=wuq[:, 0, kk, h * 96:(h + 1) * 96], rhs=cqn[:, kk, :],
                                                  start=(kk == 0), stop=(kk == 1)),
                         reads=['wuq', 'cqn'], writes=[f'ps{bq}'], inc=(kk == 1))
                S.op('act', lambda e: e.copy(out=qT[0:64, h, lc:lc + 512], in_=ps[bq][0:64, :]), reads=[f'ps{bq}'], writes=['qTA'])
                if not samp:
                    S.op('act', lambda e: e.copy(out=qT[64:96, h, lc:lc + 512], in_=ps[bq][64:96, :]), reads=[f'ps{bq}'], writes=['qTA'])
                else:
                    bs_ = self.psum('lo')
                    for kk in range(2):
                        S.op('pe', lambda e: e.matmul(ps[bs_][0:96, :], lhsT=wuq[:, 0, kk, 384 + h * 96:384 + (h + 1) * 96],
                                                      rhs=cqn[:, kk, :], start=(kk == 0), stop=(kk == 1)),
                             reads=['wuq', 'cqn'], writes=[f'ps{bs_}'], inc=(kk == 1))
                    for hh in range(2):
                        cs = slice(hh * 256, (hh + 1) * 256)
                        p0 = lc + hh * 256
                        S.op('dve', lambda e: e.tensor_tensor(out=rt[0][64:96, :], in0=ps[bq][64:96, cs], in1=cosT[64:96, p0:p0 + 256],
                                                              op=ALU.mult), reads=[f'ps{bq}', 'cosT'], writes=['rt0'])
                        S.op('dve', lambda e: e.tensor_tensor(out=rt[1][64:96, :], in0=ps[bs_][64:96, cs], in1=sinT[64:96, p0:p0 + 256],
                                                              op=ALU.mult), reads=[f'ps{bs_}', 'sinT'], writes=['rt1'])
                        S.op('pool', lambda e: e.tensor_tensor(out=qT[64:96, h, p0:p0 + 256], in0=rt[0][64:96, :], in1=rt[1][64:96, :],
                                                               op=ALU.add), reads=['rt0', 'rt1'], writes=['qTA'])
            b = self.psum('lo')
            fm_mm(ps[b][:, :], f'ps{b}', w3, wk, 256, 128, rh, [f'hT{tt}'])
            S.op('act', lambda e: e.copy(out=ckf[:, 0, :], in_=ps[b][:, :]), reads=[f'ps{b}'], writes=['ckf'])
            r, rk = rms_rstd(ckf[:, :, :], 'ckf', 1, 128)
            S.op('dve', lambda e: e.scalar_tensor_tensor(out=ckvnT[:, kc:kc + 512], in0=ckf[:, 0, :], scalar=smalls[:, 4:5],
                                                         in1=r[:, :], op0=ALU.mult, op1=ALU.mult),
                 reads=['ckf', 'kvn11', rk], writes=['ckvnT'])
            b = self.psum('lo')
            fm_mm(ps[b][0:96, :], f'ps{b}', w3, wk, 384, 96, rh, [f'hT{tt}'])
            if not samp:
                S.op('act', lambda e: e.copy(out=krT[64:96, kc:kc + 512], in_=ps[b][64:96, :]), reads=[f'ps{b}'], writes=['krT'])
            else:
                S.op('act', lambda e: e.copy(out=krf[64:96, lc:lc + 512], in_=ps[b][64:96, :]), reads=[f'ps{b}'], writes=['krf'])
        if samp:
            wt, wk = wload(win_d[l, gA2][:, :, 0:96], 8 * 96)
            w3 = wt[:, 0:768].rearrange("p (k j) -> p k j", k=8)
            for tt in tiles:
                lc = (tt - tiles[0]) * 512
                b = self.psum('lo')
                fm_mm(ps[b][0:96, :], f'ps{b}', w3, wk, 0, 96, lambda k: hT[:, k, tl(tt)], [f'hT{tt}'])
                for hh in range(2):
                    p0 = lc + hh * 256
                    S.op('pool', lambda e: e.tensor_tensor(out=rt[0][64:96, :], in0=krf[64:96, p0:p0 + 256], in1=cosT[64:96, p0:p0 + 256],
                                                           op=ALU.mult), reads=['krf', 'cosT'], writes=['rt0'])
                    S.op('dve', lambda e: e.tensor_tensor(out=rt[1][64:96, :], in0=ps[b][64:96, hh * 256:(hh + 1) * 256],
                                                          in1=sinT[64:96, p0:p0 + 256], op=ALU.mult),
                         reads=[f'ps{b}', 'sinT'], writes=['rt1'])
                    S.op('pool', lambda e: e.tensor_tensor(out=krT[64:96, 256 + p0:256 + p0 + 256], in0=rt[0][64:96, :],
                                                           in1=rt[1][64:96, :], op=ALU.add), reads=['rt0', 'rt1'], writes=['krT'])
        else:
            wt, wk = wload(win_d[l, gA3][:, :, 0:160], 8 * 160)
            w3 = wt[:, 0:1280].rearrange("p (k j) -> p k j", k=8)
            for blk in range(4):
                b = self.psum('lo')
                tm_mm(ps[b][:, 0:160], f'ps{b}', w3, wk, 0, 160, blk * 128)
                S.op('act', lambda e: e.activation(out=stage[:, 0:128], in_=ps[b][:, 0:128], func=AF.Square, accum_out=smalls[:, 5:6]),
                     reads=[f'ps{b}'], writes=['stageA', 'ssA'])
                self.rsqrt(smalls[:, 6:7], smalls[:, 5:6], float(128 * EPS), ['ssA'], ['rsA'])
                S.op('dve', lambda e: e.tensor_scalar(out=stage[:, 0:128], in0=ps[b][:, 0:128], scalar1=smalls[:, 6:7],
                                                      scalar2=float(math.sqrt(128.0)), op0=ALU.mult, op1=ALU.mult),
                     reads=[f'ps{b}', 'rsA', 'stageA'], writes=['stageA'])
                S.op('dve', lambda e: e.tensor_tensor(out=stage[:, 0:128], in0=stage[:, 0:128], in1=kvnB[:, 0:128],
                                                      op=ALU.mult), reads=['stageA', 'kvnB'], writes=['stageA'])
                S.op('act', lambda e: e.copy(out=stage[:, 128:160], in_=ps[b][:, 128:160]), reads=[f'ps{b}', 'stageA'], writes=['stageA'])
                sq_, bl = blk // 2, blk % 2
                self.store('o_ckv', L['o_ckv'][sq_, l, bl * 128:(bl + 1) * 128, :], stage[:, 0:128], ['stageA'])
                self.store('o_kr', L['o_kr'][sq_, l, bl * 128:(bl + 1) * 128, :], stage[:, 128:160], ['stageA'])
        S.barrier()
        S.op('pool', lambda e: e.memset(vaug[:, :, :, 64:65], 1.0), writes=['vaugA'])
        for c0 in range(0, nk, 512):
            c1 = min(nk, c0 + 512)
            n = c1 - c0
            for h in range(4):
                b = self.psum('lo')
                S.op('pe', lambda e: e.matmul(ps[b][0:64, 0:n], lhsT=wukv[:, 0, h * 64:(h + 1) * 64], rhs=ckvnT[:, c0:c1],
                                              start=True, stop=True), reads=['wukv', 'ckvnT'], writes=[f'ps{b}'])
                S.op('act', lambda e: e.copy(out=KT[0:64, h, c0:c1], in_=ps[b][0:64, 0:n]), reads=[f'ps{b}'], writes=['KTA'])
                S.op('pool', lambda e: e.tensor_copy(out=KT[64:96, h, c0:c1], in_=krT[64:96, c0:c1]), reads=['krT'], writes=['KTA'])
        for kb in range(nkb):
            b = self.psum('lo')
            S.op('pe', lambda e: e.matmul(ps[b][:, 0:256], lhsT=ckvnT[:, kb * 128:(kb + 1) * 128], rhs=wukv[:, 0, 256:512],
                                          start=True, stop=True), reads=['wukv', 'ckvnT'], writes=[f'ps{b}'])
            S.op('dve', lambda e: e.tensor_copy(out=vaug[:, kb, :, 0:64], in_=ps[b][:, 0:256].rearrange("p (h e) -> p h e", h=4)),
                 reads=[f'ps{b}'], writes=['vaugA'])
        yaall = [carve(MS + 34 * KB + 512 + i * 2 * KB, [128, 4, 256], BF16) for i in range(2)]
        yakey = {id(yaall[0]): 'yaA0', id(yaall[1]): 'yaA1'}
        deferred = []
        qi = 0

        def make_fin(hs, banks, nb, ya_):
            def fin():
                for i, h in enumerate(hs):
                    accb = banks[i]
                    av = ps[accb][:, 0:nb * 65].rearrange("p (b e) -> p b e", e=65)
                    S.op('dve', lambda e: e.reciprocal(out=smalls[:, 8 + 4 * i:8 + 4 * i + nb], in_=av[:, :, 64]),
                         reads=[f'ps{accb}'], writes=[f'rdA{i}'])
                    S.op('dve', lambda e: e.tensor_tensor(out=ya_[:, 0:nb, h * 64:(h + 1) * 64], in0=av[:, :, 0:64],
                                                          in1=smalls[:, 8 + 4 * i:8 + 4 * i + nb].unsqueeze(2).to_broadcast([128, nb, 64]),
                                                          op=ALU.mult),
                         reads=[f'ps{accb}', f'rdA{i}'], writes=[yakey[id(ya_)]])
            return fin

        for (s0, Ls) in G['seqs']:
            kb0 = 0 if samp else s0 // 128
            nkc = (Ls + kofs) // 128
            Nt = min(512, Ls)
            nb = Nt // 128
            for tq in range(Ls // Nt):
                q0 = s0 + tq * Nt
                ya_ = yaall[qi % 2]
                qi += 1
                for hp in range(2):
                    banks = (4, 5) if hp == 0 else (6, 7)
                    hs = (2 * hp, 2 * hp + 1)
                    self.attn_core(L, [((lambda h=h: (qT[0:96, h, q0:q0 + Nt], 'qTA')),
                                        (lambda sc, h=h: (KT[0:96, h, (kb0 + sc) * 128:(kb0 + sc + 1) * 128], 'KTA')),
                                        (lambda sc, h=h: (vaug[:, kb0 + sc, h, :], 'vaugA')), banks[i]) for i, h in enumerate(hs)],
                                   nkc, Nt, MLA_SCALE)
                    self.p0_hook(MS + 40 * KB)
                    for f in deferred:
                        f()
                    deferred = [make_fin(hs, banks, nb, ya_)]
                    if hp == 1:
                        def tofm(ya_=ya_, nb=nb, q0=q0):
                            for tb in range(nb):
                                to_fm(ya_[:, tb, :], yakey[id(ya_)], 0, (t0 + q0) // 128 + tb)
                        deferred.append(tofm)
        for f in deferred:
            f()

    def mixer_D(self, l, L, samp):
        S = self.S
        g_ = lambda n: L[n]
        ps, hT, carve, wload, fm_mm, tm_mm, win_d, tl, MS, KB = (g_('ps'), g_('hT'), g_('carve'), g_('wload'), g_('fm_mm'),
                                                                  g_('tm_mm'), g_('win_d'), g_('tl'), g_('MS'), g_('KB'))
        cosT, sinT, smalls, lamB, snB, to_fm = g_('cosT'), g_('sinT'), g_('smalls'), g_('lamB'), g_('snB'), g_('to_fm')
        G = self.geo(samp)
        t0, tiles, kofs, nkb = G['t0'], G['tiles'], G['kofs'], G['nkb']
        self.pt_i = 0
        lam_init = 0.8 - 0.6 * math.exp(-0.3 * l)
        QT = carve(MS, [96, 3, 1024], BF16)
        KT = carve(MS + 6 * KB, [96, 3, 1280], BF16)
        vaug = carve(MS + 13 * KB + 512, [128, 10, 4, 65], BF16)
        rt = [carve(MS + 19 * KB + i * 2 * KB, [96, 512], F32) for i in range(2)]
        stage = carve(MS + 23 * KB, [128, 512], F32)
        dtmp = [carve(MS + 25 * KB + i * 256, [128, 64], F32) for i in range(3)]
        junk = carve(MS + 25 * KB + 768, [128, 64], F32)
        lt = carve(MS + 26 * KB, [128, 32], F32)
        yd = [carve(MS + 27 * KB + i * 512, [128, 256], BF16) for i in range(4)]

        for i in range(2):
            S.op('dve', lambda e: e.tensor_tensor(out=lt, in0=lamB[:, 64 * i:64 * i + 32],
                                                  in1=lamB[:, 64 * i + 32:64 * i + 64], op=ALU.mult),
                 reads=['lamB'], writes=['ltD'])
            S.op('dve', lambda e: e.reduce_sum(out=smalls[:, 30 + i:31 + i], in_=lt, axis=AX.X), reads=['ltD'], writes=['lsD'])
        S.op('act', lambda e: e.activation(out=smalls[:, 32:34], in_=smalls[:, 30:32], func=AF.Exp), reads=['lsD'], writes=['leD'])
        S.op('dve', lambda e: e.tensor_tensor(out=smalls[:, 34:35], in0=smalls[:, 33:34], in1=smalls[:, 32:33], op=ALU.subtract),
             reads=['leD'], writes=['nlD'])
        S.op('dve', lambda e: e.tensor_scalar(out=smalls[:, 34:35], in0=smalls[:, 34:35], scalar1=float(-lam_init), scalar2=None,
                                              op0=ALU.add), reads=['nlD'], writes=['nlD'])
        S.barrier()
        self.cut('Dlam')
        S.op('pool', lambda e: e.memset(vaug[:, :, :, 64:65], 1.0), writes=['vaugD'])
        if samp:
            S.dma('pool', 'ctxD0', KT[:, :, 0:256], L['cdk_d'][l], writes=['KTD'])
            S.dma('pool', 'ctxD1', vaug[:, 0:2, :, 0:64], L['cdv_d'][:, l], writes=['vaugD'])
        plan = [('D1', 0, QT, 0, 96), ('D1', 256, QT, 1, 96), ('D2', 0, QT, 2, 64),
                ('D2', 256, KT, 0, 96), ('D3', 0, KT, 1, 96), ('D3', 256, KT, 2, 64)]
        cur = None
        for (gn, c0, dst, dc, M) in plan:
            if gn != cur:
                wt, wk = wload(win_d[l, WG_ORDER.index(gn)][:, :, 0:512], 4096)
                w3 = wt[:, 0:4096].rearrange("p (k j) -> p k j", k=8)
                cur = gn
            isK = dst is KT
            dkey = 'KTD' if isK else 'QTD'
            for tt in tiles:
                lc = (tt - tiles[0]) * 512
                b = self.psum('lo')
                fm_mm(ps[b][0:M, :], f'ps{b}', w3, wk, c0, M, lambda k: hT[:, k, tl(tt)], [f'hT{tt}'])
                d0 = (kofs + lc) if isK else lc
                if not samp:
                    S.op('act', lambda e: e.copy(out=dst[0:M, dc, d0:d0 + 512], in_=ps[b][0:M, :]), reads=[f'ps{b}'], writes=[dkey])
                else:
                    b2 = self.psum('lo')
                    fm_mm(ps[b2][0:M, :], f'ps{b2}', w3, wk, c0 + 128, M, lambda k: hT[:, k, tl(tt)], [f'hT{tt}'])
                    S.op('dve', lambda e: e.tensor_tensor(out=rt[0][0:M, :], in0=ps[b][0:M, :], in1=cosT[0:M, lc:lc + 512], op=ALU.mult),
                         reads=[f'ps{b}', 'cosT'], writes=['rtD0'])
                    S.op('dve', lambda e: e.tensor_tensor(out=rt[1][0:M, :], in0=ps[b2][0:M, :], in1=sinT[0:M, lc:lc + 512], op=ALU.mult),
                         reads=[f'ps{b2}', 'sinT'], writes=['rtD1'])
                    S.op('pool', lambda e: e.tensor_tensor(out=dst[0:M, dc, d0:d0 + 512], in0=rt[0][0:M, :], in1=rt[1][0:M, :], op=ALU.add),
                         reads=['rtD0', 'rtD1'], writes=[dkey])
        S.barrier()
        self.cut('Dz')
        wt, wk = wload(win_d[l, WG_ORDER.index('D4')][:, :, 0:512], 4096)
        w3 = wt[:, 0:4096].rearrange("p (k j) -> p k j", k=8)
        for bi in range(G['nt'] // 128):
            blk = t0 // 128 + bi
            N = 256 if samp else 512
            b = self.psum('lo')
            tm_mm(ps[b][:, 0:N], f'ps{b}', w3, wk, 0, N, blk * 128)
            kb = kofs // 128 + bi
            S.op('dve', lambda e: e.tensor_copy(out=vaug[:, kb, :, 0:64], in_=ps[b][:, 0:256].rearrange("p (h e) -> p h e", h=4)),
                 reads=[f'ps{b}'], writes=['vaugD'])
            if not samp:
                S.op('act', lambda e: e.copy(out=stage[:, :], in_=ps[b][:, :]), reads=[f'ps{b}'], writes=['stageD'])
                sq_, bl = blk // 2, blk % 2
                self.store('o_dv', L['o_dv'][sq_, l, bl * 128:(bl + 1) * 128, :], stage[:, 0:256], ['stageD'])
                self.store('o_dk', L['o_dk'][sq_, l, bl * 128:(bl + 1) * 128, :], stage[:, 256:512], ['stageD'])
        S.barrier()
        self.cut('Dtm')
        fin_scale = 8.0 * (1.0 - lam_init)
        ydall = [carve(MS + 27 * KB + i * 2 * KB, [128, 4, 256], BF16) for i in range(2)]
        dA = carve(MS + 31 * KB, [128, 4, 64], F32)
        dB = carve(MS + 32 * KB, [128, 4, 64], F32)
        rr = smalls[:, 36:44].rearrange("p (j b) -> p j b", j=2)
        deferred = []
        qi = 0

        def make_fin(h, accb, nb, yd_):
            def fin():
                av = [ps[accb[j]][:, 0:nb * 65].rearrange("p (b e) -> p b e", e=65) for j in range(2)]
                for j in range(2):
                    S.op('dve', lambda e: e.reciprocal(out=rr[:, j, 0:nb], in_=av[j][:, :, 64]),
                         reads=[f'ps{accb[j]}'], writes=[f'rdD{j}'])
                S.op('dve', lambda e: e.tensor_scalar(out=rr[:, 1, 0:nb], in0=rr[:, 1, 0:nb], scalar1=smalls[:, 34:35], scalar2=None,
                                                      op0=ALU.mult), reads=['rdD1', 'nlD'], writes=['rdD1'])
                S.op('dve', lambda e: e.tensor_tensor(out=dA[:, 0:nb, :], in0=av[0][:, :, 0:64],
                                                      in1=rr[:, 0, 0:nb].unsqueeze(2).to_broadcast([128, nb, 64]), op=ALU.mult),
                     reads=[f'ps{accb[0]}', 'rdD0'], writes=['dA'])
                S.op('dve', lambda e: e.tensor_tensor(out=dB[:, 0:nb, :], in0=av[1][:, :, 0:64],
                                                      in1=rr[:, 1, 0:nb].unsqueeze(2).to_broadcast([128, nb, 64]), op=ALU.mult),
                     reads=[f'ps{accb[1]}', 'rdD1'], writes=['dB'])
                S.op('dve', lambda e: e.tensor_tensor(out=dA[:, 0:nb, :], in0=dA[:, 0:nb, :], in1=dB[:, 0:nb, :], op=ALU.add),
                     reads=['dA', 'dB'], writes=['dA'])
                S.op('dve', lambda e: e.tensor_tensor(out=dB[:, 0:nb, :], in0=dA[:, 0:nb, :], in1=dA[:, 0:nb, :], op=ALU.mult),
                     reads=['dA'], writes=['dB'])
                S.op('dve', lambda e: e.reduce_sum(out=smalls[:, 44:44 + nb], in_=dB[:, 0:nb, :], axis=AX.X), reads=['dB'], writes=['ssD'])
                self.rsqrt(smalls[:, 44:44 + nb], smalls[:, 44:44 + nb], float(64 * EPS), ['ssD'], ['ssD'])
                S.op('dve', lambda e: e.tensor_tensor(out=dA[:, 0:nb, :], in0=dA[:, 0:nb, :],
                                                      in1=smalls[:, 44:44 + nb].unsqueeze(2).to_broadcast([128, nb, 64]), op=ALU.mult),
                     reads=['dA', 'ssD'], writes=['dA'])
                S.op('dve', lambda e: e.scalar_tensor_tensor(out=yd_[:, 0:nb, h * 64:(h + 1) * 64], in0=dA[:, 0:nb, :], scalar=float(fin_scale),
                                                             in1=snB[:, 0:64].unsqueeze(1).to_broadcast([128, nb, 64]),
                                                             op0=ALU.mult, op1=ALU.mult),
                     reads=['dA', 'snB'], writes=[yd_.__dict__.get('k', 'ydD')] if False else [ydkey[id(yd_)]])
            return fin

        ydkey = {id(ydall[0]): 'ydD0', id(ydall[1]): 'ydD1'}
        for (s0, Ls) in G['seqs']:
            kb0 = 0 if samp else s0 // 128
            nkc = (Ls + kofs) // 128
            Nt = min(512, Ls)
            nb = Nt // 128
            for tq in range(Ls // Nt):
                q0 = s0 + tq * Nt
                yd_ = ydall[qi % 2]
                qi += 1
                for h in range(4):
                    accb = [4, 5] if h % 2 == 0 else [6, 7]
                    strs = []
                    for j in range(2):
                        p = 2 * h + j
                        c, pb = p // 3, (p % 3) * 32
                        strs.append(((lambda c=c, pb=pb: (QT[pb:pb + 32, c, q0:q0 + Nt], 'QTD')),
                                     (lambda sc, c=c, pb=pb: (KT[pb:pb + 32, c, (kb0 + sc) * 128:(kb0 + sc + 1) * 128], 'KTD')),
                                     (lambda sc, h=h: (vaug[:, kb0 + sc, h, :], 'vaugD')), accb[j]))
                    self.attn_core(L, strs, nkc, Nt, DIFF_SCALE)
                    self.p0_hook(MS + 40 * KB)
                    for f in deferred:
                        f()
                    deferred = [make_fin(h, accb, nb, yd_)]
                    if h == 3:
                        def tofm(yd_=yd_, nb=nb, q0=q0):
                            for tb in range(nb):
                                to_fm(yd_[:, tb, :], ydkey[id(yd_)], 3, (t0 + q0) // 128 + tb)
                        deferred.append(tofm)
        for f in deferred:
            f()

    def mixer_C(self, l, L, samp):
        S = self.S
        g_ = lambda n: L[n]
        ps, hT, carve, wload, fm_mm, tm_mm, win_d, tl, MS, KB = (g_('ps'), g_('hT'), g_('carve'), g_('wload'), g_('fm_mm'),
                                                                  g_('tm_mm'), g_('win_d'), g_('tl'), g_('MS'), g_('KB'))
        smalls, hnB, gbT, m0T, m0B, C0, sel, identf, mle, mge, to_fm, zero1 = (
            g_('smalls'), g_('hnB'), g_('gbT'), g_('m0T'), g_('m0B'), g_('C0'), g_('sel'), g_('identf'), g_('mle'), g_('mge'),
            g_('to_fm'), g_('zero1'))
        G = self.geo(samp)
        t0, tiles = G['t0'], G['tiles']
        nbp = G['nt'] // 128
        qT = carve(MS, [128, 2, 1024], BF16)
        kT = carve(MS + 4 * KB, [128, 2, 1024], BF16)
        vaug = carve(MS + 8 * KB, [128, 8, 4, 65], BF16)
        sgo = carve(MS + 12 * KB + 512, [128, 8, 256], BF16)
        giT = carve(MS + 16 * KB + 512, [36, 1024], F32)
        lfT = carve(MS + 20 * KB + 512, [36, 1024], F32)
        BT = carve(MS + 24 * KB + 512, [36, 1024], F32)
        GT = carve(MS + 28 * KB + 512, [36, 1024], F32)
        hb = carve(MS + 32 * KB + 512, [128, 8, 256], F32)
        kTM = carve(MS + 40 * KB + 512, [128, 4, 256], BF16)
        negG = [carve(MS + 20 * KB + 512, [128, 1024], F32), carve(MS + 16 * KB + 512, [128, 1024], F32)]
        sq = g_('sq')
        sqb = sq[:, :, :].rearrange("p a b -> p (a b)")
        Wt = [sqb[:, i * 512:(i + 1) * 512] for i in range(4)]
        Dt = [sqb[:, 2048 + i * 1024:2048 + (i + 1) * 1024].bitcast(F32) for i in range(2)]
        rstd = g_('rstd')
        gTM = rstd[0][:, 0:288].rearrange("p (b r) -> p b r", r=36)
        emt = rstd[1][:, 0:288].rearrange("p (b r) -> p b r", r=36)

        S.op('pool', lambda e: e.memset(vaug[:, :, :, 64:65], 1.0), writes=['vaugC'])
        S.op('pool', lambda e: e.memset(BT[:, :], 0.0), writes=['BT'])
        wt, wk = wload(win_d[l, WG_ORDER.index('C1')][:, :, 0:512], 4096)
        w3 = wt[:, 0:4096].rearrange("p (k j) -> p k j", k=8)
        for ci in range(4):
            dst = qT if ci < 2 else kT
            for tt in tiles:
                lc = (tt - tiles[0]) * 512
                b = self.psum('lo')
                fm_mm(ps[b][:, :], f'ps{b}', w3, wk, ci * 128, 128, lambda k: hT[:, k, tl(tt)], [f'hT{tt}'])
                S.op('act', lambda e: e.activation(out=dst[:, ci % 2, lc:lc + 512], in_=ps[b][:, :], func=AF.Copy,
                                                   scale=(1.0 if ci < 2 else 0.125)),
                     reads=[f'ps{b}'], writes=['qTC' if ci < 2 else 'kTC'])
        wt, wk = wload(win_d[l, WG_ORDER.index('C2')][:, :, 0:128], 1024)
        w3 = wt[:, 0:1024].rearrange("p (k j) -> p k j", k=8)
        for tt in tiles:
            lc = (tt - tiles[0]) * 512
            cs = slice(lc, lc + 512)
            b = self.psum('lo')
            fm_mm(ps[b][0:36, :], f'ps{b}', w3, wk, 0, 36, lambda k: hT[:, k, tl(tt)], [f'hT{tt}'])
            S.op('act', lambda e: e.activation(out=giT[:, cs], in_=ps[b][0:36, :], func=AF.Identity, bias=gbT[:, l, 0:1], scale=1.0),
                 reads=[f'ps{b}', 'gbT'], writes=['giT'])
            b = self.psum('lo')
            fm_mm(ps[b][0:36, :], f'ps{b}', w3, wk, 64, 36, lambda k: hT[:, k, tl(tt)], [f'hT{tt}'])
            S.op('dve', lambda e: e.tensor_scalar(out=lfT[:, cs], in0=ps[b][0:36, :], scalar1=gbT[:, l, 1:2], scalar2=-1.0,
                                                  op0=ALU.add, op1=ALU.mult), reads=[f'ps{b}', 'gbT'], writes=['lfT'])
            S.op('act', lambda e: e.activation(out=lfT[:, cs], in_=lfT[:, cs], func=AF.Exp), reads=['lfT'], writes=['lfT'])
            S.op('act', lambda e: e.activation(out=lfT[:, cs], in_=lfT[:, cs], func=AF.Ln, bias=self.cbias(1.0, lfT[:, cs]), scale=1.0),
                 reads=['lfT', 'cbias'], writes=['lfT'])
            S.op('dve', lambda e: e.tensor_scalar(out=lfT[:, cs], in0=lfT[:, cs], scalar1=-1.0, scalar2=None, op0=ALU.mult),
                 reads=['lfT'], writes=['lfT'])
        wt, wk = wload(win_d[l, WG_ORDER.index('C3')][:, :, 0:512], 4096)
        w3 = wt[:, 0:4096].rearrange("p (k j) -> p k j", k=8)
        for bi in range(nbp):
            b = self.psum('lo')
            tm_mm(ps[b][:, :], f'ps{b}', w3, wk, 0, 512, t0 + bi * 128)
            S.op('dve', lambda e: e.tensor_copy(out=vaug[:, bi, :, 0:64], in_=ps[b][:, 0:256].rearrange("p (h e) -> p h e", h=4)),
                 reads=[f'ps{b}'], writes=['vaugC'])
            S.op('act', lambda e: e.activation(out=sgo[:, bi, :], in_=ps[b][:, 256:512], func=AF.Sigmoid),
                 reads=[f'ps{b}'], writes=['sgo'])
        if not samp:
            wt, wk = wload(win_d[l, WG_ORDER.index('C4')][:, :, 0:256], 2048)
            w3 = wt[:, 0:2048].rearrange("p (k j) -> p k j", k=8)
            for bi in range(4):
                b = self.psum('lo')
                tm_mm(ps[b][:, 0:256], f'ps{b}', w3, wk, 0, 256, bi * 128)
                S.op('act', lambda e: e.activation(out=kTM[:, bi, :], in_=ps[b][:, 0:256], func=AF.Copy, scale=0.125),
                     reads=[f'ps{b}'], writes=['kTM'])

        for si, (s0, Ls) in enumerate(G['seqs']):
            nb = Ls // 128
            b0 = s0 // 128
            sl = slice(s0, s0 + Ls)
            S.op('pool', lambda e: e.memset(GT[:, 0:Ls], 1.0), writes=['GT'])
            S.op('dve', lambda e: e.tensor_tensor_scan(out=BT[0:4, 0:Ls], data0=GT[0:4, 0:Ls], data1=lfT[0:4, sl], initial=0.0,
                                                       op0=ALU.mult, op1=ALU.add), reads=['GT', 'lfT'], writes=['BT'])
            S.op('dve', lambda e: e.tensor_tensor_scan(out=BT[32:36, 0:Ls][:, ::-1], data0=GT[32:36, 0:Ls],
                                                       data1=lfT[32:36, sl][:, ::-1], initial=0.0,
                                                       op0=ALU.mult, op1=ALU.add), reads=['GT', 'lfT'], writes=['BT'])
            for r0 in (0, 32):
                S.op('dve', lambda e: e.tensor_tensor(out=giT[r0:r0 + 4, sl], in0=giT[r0:r0 + 4, sl], in1=BT[r0:r0 + 4, 0:Ls],
                                                      op=ALU.subtract), reads=['giT', 'BT'], writes=['giT'])
            init_f = m0T[0:4, l:l + 1] if samp else zero1[0:4, 0:1]
            init_b = m0T[32:36, l:l + 1] if samp else zero1[32:36, 0:1]
            S.op('dve', lambda e: e.tensor_tensor_scan(out=GT[0:4, 0:Ls], data0=giT[0:4, sl], data1=giT[0:4, sl], initial=init_f,
                                                       op0=ALU.max, op1=ALU.max), reads=['giT', 'm0T', 'zero1'], writes=['GT'])
            S.op('dve', lambda e: e.tensor_tensor_scan(out=GT[32:36, 0:Ls][:, ::-1], data0=giT[32:36, sl][:, ::-1],
                                                       data1=giT[32:36, sl][:, ::-1], initial=init_b,
                                                       op0=ALU.max, op1=ALU.max), reads=['giT', 'm0T', 'zero1'], writes=['GT'])
            for r0 in (0, 32):
                S.op('dve', lambda e: e.tensor_tensor(out=BT[r0:r0 + 4, 0:Ls], in0=BT[r0:r0 + 4, 0:Ls], in1=GT[r0:r0 + 4, 0:Ls],
                                                      op=ALU.add), reads=['BT', 'GT'], writes=['BT'])
                S.op('dve', lambda e: e.tensor_scalar(out=GT[r0:r0 + 4, 0:Ls], in0=GT[r0:r0 + 4, 0:Ls], scalar1=-1.0, scalar2=None,
                                                      op0=ALU.mult), reads=['GT'], writes=['GT'])
            if not samp:
                S.dma('sp', 'o_m', L['o_m'][si, l, 0, :].rearrange("(h o) -> h o", o=1), BT[0:4, Ls - 1:Ls], reads=['BT'],
                      writes=['dram_o_m'])
                S.dma('sp', 'o_m', L['o_m'][si, l, 1, :].rearrange("(h o) -> h o", o=1), BT[32:36, 0:1], reads=['BT'],
                      writes=['dram_o_m'])
                if 'o_m' not in self.out_keys:
                    self.out_keys.append('o_m')
            for bi in range(nb):
                b = self.psum('lo')
                S.op('pe', lambda e: e.matmul(ps[b][:, 0:36], lhsT=giT[0:36, s0 + bi * 128:s0 + (bi + 1) * 128], rhs=identf[:, :],
                                              start=True, stop=True), reads=['giT', 'identf'], writes=[f'ps{b}'], inc=False)
                S.op('pe', lambda e: e.matmul(ps[b][:, 64:100], lhsT=BT[0:36, bi * 128:(bi + 1) * 128], rhs=identf[:, :],
                                              start=True, stop=True), reads=['BT', 'identf'], writes=[f'ps{b}'])
                S.op('dve', lambda e: e.tensor_copy(out=gTM[:, bi, :], in_=ps[b][:, 0:36]), reads=[f'ps{b}'], writes=['gTM'])
                S.op('act', lambda e: e.activation(out=emt[:, bi, :], in_=ps[b][:, 64:100], func=AF.Exp, scale=-1.0),
                     reads=[f'ps{b}'], writes=['emt'])
            S.barrier()
            for h in range(4):
                kk, pb = h // 2, (h % 2) * 64
                rf, rb = h, 32 + h
                for d_ in (0, 1):
                    r0 = 0 if d_ == 0 else 32
                    for c0 in range(0, Ls, 512):
                        n = min(512, Ls - c0)
                        b = self.psum('lo')
                        S.op('pe', lambda e: e.matmul(ps[b][:, 0:n], lhsT=sel[r0:r0 + 4, h, :], rhs=GT[r0:r0 + 4, c0:c0 + n],
                                                      start=True, stop=True), reads=['sel', 'GT'], writes=[f'ps{b}'])
                        S.op('act', lambda e: e.copy(out=negG[d_][:, c0:c0 + n], in_=ps[b][:, 0:n]), reads=[f'ps{b}'],
                             writes=[f'negG{d_}'])
                Nt = min(512, Ls)
                ntb = Nt // 128
                for tq in range(Ls // Nt):
                    q0l = tq * Nt
                    tb0 = q0l // 128
                    accf, accb_ = (4, 5) if h % 2 == 0 else (6, 7)
                    af = ps[accf][:, 0:ntb * 65].rearrange("p (b e) -> p b e", e=65)
                    ab = ps[accb_][:, 0:ntb * 65].rearrange("p (b e) -> p b e", e=65)
                    bank_started = {0: False, 1: False}
                    if samp:
                        for d_, acc, an in ((0, af, accf), (1, ab, accb_)):
                            r = d_ * 4 + h
                            wq = Dt[d_][pb:pb + 64, 0:Nt]
                            S.op('act', lambda e: e.activation(out=wq, in_=negG[d_][pb:pb + 64, q0l:q0l + Nt], func=AF.Exp,
                                                               bias=m0B[pb:pb + 64, l * 8 + r:l * 8 + r + 1], scale=1.0),
                                 reads=[f'negG{d_}', 'm0B'], writes=[f'Dt{d_}'])
                            qs = Wt[d_][pb:pb + 64, 0:Nt]
                            S.op('dve', lambda e: e.tensor_tensor(out=qs, in0=qT[pb:pb + 64, kk, s0 + q0l:s0 + q0l + Nt], in1=wq,
                                                                  op=ALU.mult), reads=['qTC', f'Dt{d_}'], writes=[f'Wt{d_}'])
                            for i in range(ntb):
                                S.op('pe', lambda e: e.matmul(acc[:, i, :], lhsT=qs[:, i * 128:(i + 1) * 128],
                                                              rhs=C0[pb:pb + 64, l, d_, kk, :], start=(not bank_started[d_]), stop=False,
                                                              skip_group_check=True),
                                     reads=[f'Wt{d_}', 'C0'], writes=[f'ps{an}'], inc=(i == ntb - 1))
                                bank_started[d_] = True
                    def stage1(j):
                        sb_ = self.psum('lo')
                        S.op('pe', lambda e: e.matmul(ps[sb_][:, 0:Nt], lhsT=kT[pb:pb + 64, kk, s0 + j * 128:s0 + (j + 1) * 128],
                                                      rhs=qT[pb:pb + 64, kk, s0 + q0l:s0 + q0l + Nt], start=True, stop=True),
                             reads=['kTC', 'qTC'], writes=[f'ps{sb_}'])
                        jl = j - tb0
                        lo = max(jl, 0)
                        wf = wb_ = None
                        if lo < ntb:
                            c0_, c1_ = lo * 128, Nt
                            self.wt_i = (getattr(self, 'wt_i', 0) + 1) % 2
                            wf = self.wt_i
                            S.op('act', lambda e: e.activation(out=Dt[0][:, c0_:c1_], in_=negG[0][:, q0l + c0_:q0l + c1_], func=AF.Exp,
                                                               bias=gTM[:, j, rf:rf + 1], scale=1.0),
                                 reads=['negG0', 'gTM'], writes=['Dt0'])
                            S.op('dve', lambda e: e.tensor_tensor(out=Wt[wf][:, c0_:c1_], in0=ps[sb_][:, c0_:c1_], in1=Dt[0][:, c0_:c1_],
                                                                  op=ALU.mult), reads=[f'ps{sb_}', 'Dt0'], writes=[f'Wt{wf}'])
                            if jl >= 0:
                                S.op('pool', lambda e: e.tensor_tensor(out=Wt[wf][:, c0_:c0_ + 128], in0=Wt[wf][:, c0_:c0_ + 128],
                                                                       in1=mle[:, :], op=ALU.mult), reads=[f'Wt{wf}', 'mle'],
                                     writes=[f'Wt{wf}'])
                        hi = min(jl, ntb - 1)
                        if hi >= 0:
                            c0_, c1_ = 0, (hi + 1) * 128
                            self.wt_j = (getattr(self, 'wt_j', 0) + 1) % 2
                            wb_ = 2 + self.wt_j
                            S.op('act', lambda e: e.activation(out=Dt[1][:, c0_:c1_], in_=negG[1][:, q0l + c0_:q0l + c1_], func=AF.Exp,
                                                               bias=gTM[:, j, rb:rb + 1], scale=1.0),
                                 reads=['negG1', 'gTM'], writes=['Dt1'])
                            S.op('dve', lambda e: e.tensor_tensor(out=Wt[wb_][:, c0_:c1_], in0=ps[sb_][:, c0_:c1_], in1=Dt[1][:, c0_:c1_],
                                                                  op=ALU.mult), reads=[f'ps{sb_}', 'Dt1'], writes=[f'Wt{wb_}'])
                            if jl <= ntb - 1:
                                S.op('pool', lambda e: e.tensor_tensor(out=Wt[wb_][:, hi * 128:(hi + 1) * 128],
                                                                       in0=Wt[wb_][:, hi * 128:(hi + 1) * 128], in1=mge[:, :], op=ALU.mult),
                                     reads=[f'Wt{wb_}', 'mge'], writes=[f'Wt{wb_}'])
                        return (lo, wf, hi, wb_)

                    def stage2(j, info):
                        lo, wf, hi, wb_ = info
                        if wf is not None:
                            for i in range(lo, ntb):
                                S.op('pe', lambda e: e.matmul(af[:, i, :], lhsT=Wt[wf][:, i * 128:(i + 1) * 128],
                                                              rhs=vaug[:, b0 + j, h, :], start=(not bank_started[0]), stop=False,
                                                              skip_group_check=True),
                                     reads=[f'Wt{wf}', 'vaugC'], writes=[f'ps{accf}'], inc=(i == ntb - 1))
                                bank_started[0] = True
                        if wb_ is not None:
                            for i in range(0, hi + 1):
                                S.op('pe', lambda e: e.matmul(ab[:, i, :], lhsT=Wt[wb_][:, i * 128:(i + 1) * 128],
                                                              rhs=vaug[:, b0 + j, h, :], start=(not bank_started[1]), stop=False,
                                                              skip_group_check=True),
                                     reads=[f'Wt{wb_}', 'vaugC'], writes=[f'ps{accb_}'], inc=(i == hi))
                                bank_started[1] = True

                    cur = stage1(0)
                    for j in range(nb):
                        nxt = stage1(j + 1) if j + 1 < nb else None
                        stage2(j, cur)
                        cur = nxt
                    dsc = smalls[:, 16:24].rearrange("p (d b) -> p d b", d=2)
                    dtm = carve(MS + 45 * KB, [128, 4, 64], F32)
                    for d_, r, acc, an in ((0, rf, af, accf), (1, rb, ab, accb_)):
                        S.op('dve', lambda e: e.tensor_scalar(out=dsc[:, d_, 0:ntb], in0=acc[:, :, 64], scalar1=-1.0, scalar2=None,
                                                              op0=ALU.mult), reads=[f'ps{an}'], writes=[f'dnC{d_}'])
                        S.op('dve', lambda e: e.tensor_tensor(out=dsc[:, d_, 0:ntb], in0=acc[:, :, 64], in1=dsc[:, d_, 0:ntb], op=ALU.max),
                             reads=[f'ps{an}', f'dnC{d_}'], writes=[f'dnC{d_}'])
                        S.op('dve', lambda e: e.tensor_tensor(out=dsc[:, d_, 0:ntb], in0=dsc[:, d_, 0:ntb], in1=emt[:, tb0:tb0 + ntb, r],
                                                              op=ALU.max), reads=['emt', f'dnC{d_}'], writes=[f'dnC{d_}'])
                        S.op('dve', lambda e: e.reciprocal(out=dsc[:, d_, 0:ntb], in_=dsc[:, d_, 0:ntb]), reads=[f'dnC{d_}'],
                             writes=[f'dnC{d_}'])
                    hdst = hb[:, b0 + tb0:b0 + tb0 + ntb, h * 64:(h + 1) * 64]
                    S.op('dve', lambda e: e.tensor_tensor(out=hdst, in0=af[:, :, 0:64],
                                                          in1=dsc[:, 0, 0:ntb].unsqueeze(2).to_broadcast([128, ntb, 64]), op=ALU.mult),
                         reads=[f'ps{accf}', 'dnC0'], writes=['hbC'])
                    S.op('dve', lambda e: e.tensor_tensor(out=dtm[:, 0:ntb, :], in0=ab[:, :, 0:64],
                                                          in1=dsc[:, 1, 0:ntb].unsqueeze(2).to_broadcast([128, ntb, 64]), op=ALU.mult),
                         reads=[f'ps{accb_}', 'dnC1'], writes=['dtmC'])
                    S.op('dve', lambda e: e.tensor_tensor(out=hdst, in0=hdst, in1=dtm[:, 0:ntb, :], op=ALU.add),
                         reads=['hbC', 'dtmC'], writes=['hbC'])
                self.p0_hook(MS + 46 * KB)
                if not samp:
                    for d_, r, col_last in ((0, rf, Ls - 1), (1, rb, 0)):
                        wS = smalls[:, 56:56 + nb]
                        S.op('act', lambda e: e.activation(out=wS, in_=gTM[:, 0:nb, r], func=AF.Exp,
                                                           bias=negG[d_][:, col_last:col_last + 1], scale=1.0),
                             reads=['gTM', f'negG{d_}'], writes=['wSC'])
                        cb = self.psum('lo')
                        for j in range(nb):
                            kw = Wt[j % 2][:, 0:64]
                            S.op('dve', lambda e: e.tensor_scalar(out=kw, in0=kTM[:, b0 + j, h * 64:(h + 1) * 64], scalar1=wS[:, j:j + 1],
                                                                  scalar2=None, op0=ALU.mult), reads=['kTM', 'wSC'], writes=[f'Wt{j % 2}'])
                            S.op('pe', lambda e: e.matmul(ps[cb][0:64, 0:65], lhsT=kw, rhs=vaug[:, b0 + j, h, :], start=(j == 0),
                                                          stop=(j == nb - 1)), reads=[f'Wt{j % 2}', 'vaugC'], writes=[f'ps{cb}'],
                                 inc=(j == nb - 1))
                        self.cs_i = (getattr(self, 'cs_i', 0) + 1) % 4
                        cst = carve(MS + 42 * KB + 512 + self.cs_i * 512, [64, 65], F32)
                        S.op('act', lambda e: e.copy(out=cst, in_=ps[cb][0:64, 0:65]), reads=[f'ps{cb}'], writes=[f'cst{self.cs_i}'])
                        self.store(f'o_C{self.cs_i}', L['o_C'][si, l, d_, h, :, :], cst[:, 0:64], [f'cst{self.cs_i}'])
                        self.store(f'o_n{self.cs_i}', L['o_n'][si, l, d_, h, :].rearrange("(d o) -> d o", o=1), cst[:, 64:65],
                                   [f'cst{self.cs_i}'])
            for bi in range(nb):
                hsrc = hb[:, b0 + bi, :]
                h3 = hsrc.rearrange("p (h e) -> p h e", h=4)
                ycb = Wt[bi % 2][:, 0:256]
                sq3 = Dt[0][:, 0:256].rearrange("p (h e) -> p h e", h=4)
                S.op('dve', lambda e: e.tensor_tensor(out=sq3, in0=h3, in1=h3, op=ALU.mult), reads=['hbC'], writes=['Dt0'])
                S.op('dve', lambda e: e.reduce_sum(out=smalls[:, 60:64], in_=sq3, axis=AX.X), reads=['Dt0'], writes=['ssC'])
                self.rsqrt(smalls[:, 60:64], smalls[:, 60:64], float(64 * EPS), ['ssC'], ['ssC'])
                S.op('dve', lambda e: e.tensor_tensor(out=h3, in0=h3, in1=smalls[:, 60:64].unsqueeze(2).to_broadcast([128, 4, 64]),
                                                      op=ALU.mult), reads=['hbC', 'ssC'], writes=['hbC'])
                S.op('dve', lambda e: e.scalar_tensor_tensor(out=hsrc, in0=hsrc, scalar=8.0, in1=hnB[:, 0:256], op0=ALU.mult, op1=ALU.mult),
                     reads=['hbC', 'hnB'], writes=['hbC'])
                S.op('dve', lambda e: e.tensor_tensor(out=ycb, in0=hsrc, in1=sgo[:, b0 + bi, :], op=ALU.mult),
                     reads=['hbC', 'sgo'], writes=[f'Wt{bi % 2}'])
                to_fm(ycb, f'Wt{bi % 2}', 2, (t0 + s0) // 128 + bi)


def build_program(debug=(), nlayers=DEPTH, stop_after=None):
    b0 = Builder(debug=debug, nlayers=nlayers, stop_after=stop_after)
    b0.build()
    b = Builder(debug=debug, nlayers=nlayers, stop_after=stop_after, wplan=b0.wreq)
    b.build()
    return b


_CACHE = {}


def kernel(**inputs):
    inp = {k: np.asarray(v) for k, v in inputs.items()}
    if 'b' not in _CACHE:
        _CACHE['b'] = build_program()
    b = _CACHE['b']
    sh = prep_shared(inp)
    in_maps = []
    for c in range(8):
        m = dict(sh)
        m.update(prep_core(inp, c))
        in_maps.append({k: np.ascontiguousarray(v, dtype=np.float32) for k, v in m.items()})
    res = run_bass_kernel_spmd(b.nc, in_maps, core_ids=list(range(8)))
    return assemble([r for r in res.results])


def assemble(rs):
    y_prompt = np.zeros((16, 256, D), np.float32)
    y_sample = np.zeros((8, 1024, D), np.float32)
    ckv = np.zeros((16, DEPTH, 256, 128), np.float32)
    kr = np.zeros((16, DEPTH, 256, 32), np.float32)
    dk = np.zeros((16, DEPTH, 256, 4, 2, 32), np.float32)
    dv = np.zeros((16, DEPTH, 256, 4, 64), np.float32)
    Cs = np.zeros((16, DEPTH, 2, 4, 64, 64), np.float32)
    ns = np.zeros((16, DEPTH, 2, 4, 64), np.float32)
    ms = np.zeros((16, DEPTH, 2, 4), np.float32)
    for c, r in enumerate(rs):
        yT = np.asarray(r['yT'])
        tok = yT.transpose(2, 1, 0).reshape(T, D)
        y_prompt[2 * c] = tok[0:256]
        y_prompt[2 * c + 1] = tok[256:512]
        y_sample[c] = tok[512:]
        ckv[2 * c:2 * c + 2] = np.asarray(r['o_ckv'])
        kr[2 * c:2 * c + 2] = np.asarray(r['o_kr'])
        dk[2 * c:2 * c + 2] = np.asarray(r['o_dk']).reshape(2, DEPTH, 256, 4, 2, 32)
        dv[2 * c:2 * c + 2] = np.asarray(r['o_dv']).reshape(2, DEPTH, 256, 4, 64)
        Cs[2 * c:2 * c + 2] = np.asarray(r['o_C'])
        ns[2 * c:2 * c + 2] = np.asarray(r['o_n'])
        ms[2 * c:2 * c + 2] = np.asarray(r['o_m'])
    return (y_prompt, y_sample, ckv, kr, dk, dv, Cs, ns, ms)
```

```python
import math
from contextlib import ExitStack
import numpy as np
import concourse.bass as bass
import concourse.mybir as mybir
from concourse.bass_utils import run_bass_kernel_spmd

F32 = mybir.dt.float32
BF16 = mybir.dt.bfloat16
AF = mybir.ActivationFunctionType
ALU = mybir.AluOpType
AX = mybir.AxisListType

D = 1024
T = 1536
NT = 3
DEPTH = 2
EPS = 1e-6
DFF = 4096
MLA_SCALE = 96 ** -0.5
DIFF_SCALE = 32 ** -0.5
OA, OB, OC, OD = 0, 416, 928, 1968
SEQS = [(0, 256, False), (256, 256, False), (512, 1024, True)]


class Sched:
    def __init__(self, nc, es):
        self.nc = nc
        self.es = es
        self.eng = {'pe': nc.tensor, 'act': nc.scalar, 'dve': nc.vector, 'pool': nc.gpsimd, 'sp': nc.sync}
        self.sem = {e: es.enter_context(nc.semaphore('s_' + e)) for e in self.eng}
        self.cnt = {e: 0 for e in self.eng}
        self.pending = {e: False for e in self.eng}
        self.waited = {e: {} for e in self.eng}
        self.lastw = {}
        self.readers = {}
        self.dsem = {}
        self.ninst = 0

    def _wait(self, e, tok):
        name, sem, val = tok
        if name == e and e == 'pe':
            return
        if self.waited[e].get(name, 0) >= val:
            return
        self.eng[e].wait_ge(sem, val)
        self.waited[e][name] = val

    def _deps(self, e, reads, writes):
        best = {}

        def add(t):
            if t is None:
                return
            b = best.get(t[0])
            if b is None or b[2] < t[2]:
                best[t[0]] = t
        relax = False
        for r in reads:
            add(self.lastw.get(r))
        for w in writes:
            t = self.lastw.get(w)
            if t is not None and not (relax and t[0] == e and w not in reads):
                add(t)
            for t in self.readers.get(w, {}).values():
                if not (relax and t[0] == e):
                    add(t)
        for t in best.values():
            self._wait(e, t)

    def _commit(self, tok, reads, writes):
        for w in writes:
            self.lastw[w] = tok
            self.readers[w] = {}
        for r in reads:
            d = self.readers.setdefault(r, {})
            b = d.get(tok[0])
            if b is None or b[2] < tok[2]:
                d[tok[0]] = tok

    def op(self, e, fn, reads=(), writes=(), inc=True):
        writes = list(writes) + [r for r in reads if r.startswith('ps') and r[2:].isdigit() and r not in writes]
        self._deps(e, reads, writes)
        inst = fn(self.eng[e])
        self.ninst += 1
        if inc:
            self.cnt[e] += 1
            inst.then_inc(self.sem[e], 1)
            tok = (e, self.sem[e], self.cnt[e])
            self.pending[e] = False
        else:
            tok = (e, self.sem[e], self.cnt[e] + 1)
            self.pending[e] = True
        self._commit(tok, reads, writes)
        return tok

    def dma(self, q, key, out, in_, reads=(), writes=()):
        self._deps(q, reads, writes)
        if key not in self.dsem:
            self.dsem[key] = [self.es.enter_context(self.nc.semaphore('d_' + key)), 0]
        d = self.dsem[key]
        d[1] += 16
        self.eng[q].dma_start(out=out, in_=in_).then_inc(d[0], 16)
        self.ninst += 1
        tok = ('d_' + key, d[0], d[1])
        self._commit(tok, reads, writes)
        return tok

    def barrier(self):
        toks = []
        for e in self.eng:
            assert not self.pending[e], e
            if self.cnt[e] > 0:
                toks.append((e, self.sem[e], self.cnt[e]))
        for k, d in self.dsem.items():
            if k[0] == 'w' or k.startswith('c_'):
                continue
            toks.append(('d_' + k, d[0], d[1]))
        for e in self.eng:
            for t in toks:
                if t[0] != e:
                    self._wait(e, t)

    def finish(self, out_keys):
        for k in out_keys:
            d = self.dsem[k]
            self._wait('sp', ('d_' + k, d[0], d[1]))


def ktile(W, ng):
    K, N = W.shape
    assert K % 128 == 0 and N % ng == 0
    return np.ascontiguousarray(W.reshape(K // 128, 128, N // ng, ng).transpose(2, 1, 0, 3))


def fm_vec(v):
    sh = v.shape
    n = sh[-1] // 128
    a = v.reshape(*sh[:-1], n, 128)
    return np.ascontiguousarray(np.moveaxis(a, -1, 0))


def pick_cols(W, cols):
    cols = np.asarray(cols)
    out = np.zeros((W.shape[0], len(cols)), np.float32)
    m = cols >= 0
    out[:, m] = W[:, cols[m]]
    return out


def pad_to(lst, n):
    return list(lst) + [-1] * (n - len(lst))


def win_groups():
    g = {}
    kr = [OA + 384 + i for i in range(32)]
    kr_sw = [OA + 384 + (i + 16) % 32 for i in range(32)]
    g['A1'] = list(range(OA, OA + 384)) + [-1] * 64 + kr
    g['A2'] = [-1] * 64 + kr_sw
    g['A3'] = list(range(OA + 256, OA + 384)) + kr
    g['B1'] = list(range(OB, OB + 256))
    g['B2'] = list(range(OB + 256, OB + 512))
    g['C1'] = list(range(OC, OC + 512))
    gi = pad_to([OC + 1024 + h for h in range(4)], 32) + [OC + 1024 + 8 + h for h in range(4)]
    gf = pad_to([OC + 1024 + 4 + h for h in range(4)], 32) + [OC + 1024 + 12 + h for h in range(4)]
    g['C2'] = pad_to(gi, 64) + pad_to(gf, 64)
    g['C3'] = list(range(OC + 512, OC + 1024))
    g['C4'] = list(range(OC + 256, OC + 512))

    def pairs(base, ps, sw):
        out = []
        for p in ps:
            out += [base + p * 32 + ((d + 16) % 32 if sw else d) for d in range(32)]
        return pad_to(out, 128)
    q, k = OD, OD + 256
    g['D1'] = pairs(q, [0, 1, 2], False) + pairs(q, [0, 1, 2], True) + pairs(q, [3, 4, 5], False) + pairs(q, [3, 4, 5], True)
    g['D2'] = pairs(q, [6, 7], False) + pairs(q, [6, 7], True) + pairs(k, [0, 1, 2], False) + pairs(k, [0, 1, 2], True)
    g['D3'] = pairs(k, [3, 4, 5], False) + pairs(k, [3, 4, 5], True) + pairs(k, [6, 7], False) + pairs(k, [6, 7], True)
    g['D4'] = list(range(OD + 512, OD + 768)) + list(range(OD + 256, OD + 512))
    return g


WG = win_groups()
WG_ORDER = ['A1', 'A2', 'A3', 'B1', 'B2', 'C1', 'C2', 'C3', 'C4', 'D1', 'D2', 'D3', 'D4']


def rope_tables():
    t = np.arange(1024)
    row = (t // 64).astype(np.float32)
    col = (t % 64).astype(np.float32)
    nf = 8
    inv = np.exp(-math.log(10000.0) * np.arange(nf, dtype=np.float32) / nf).astype(np.float32)
    ang = np.concatenate([row[:, None] * inv, col[:, None] * inv], axis=-1).astype(np.float32)
    c, s = np.cos(ang).T, np.sin(ang).T
    cos2 = np.concatenate([c, c], 0)
    sin2 = np.concatenate([-s, s], 0)
    return (np.ascontiguousarray(np.tile(cos2, (4, 1)), dtype=np.float32),
            np.ascontiguousarray(np.tile(sin2, (4, 1)), dtype=np.float32))


def prep_shared(inp):
    sh = {}
    f = lambda a: np.asarray(a, dtype=np.float32)
    w_mod = f(inp['w_mod'])
    sh['wmod'] = np.stack([ktile(w_mod[l], 512) for l in range(DEPTH)])
    sh['bmod'] = fm_vec(f(inp['b_mod']))
    sh['ng'] = fm_vec(f(inp['norm_g']))
    w_in = f(inp['w_in'])
    wg = np.zeros((DEPTH, len(WG_ORDER), 128, 8, 512), np.float32)
    for l in range(DEPTH):
        for gi, name in enumerate(WG_ORDER):
            cols = WG[name]
            wsel = pick_cols(w_in[l], cols)
            wg[l, gi, :, :, :len(cols)] = wsel.reshape(8, 128, len(cols)).transpose(1, 0, 2)
    sh['win'] = wg
    w_merge = f(inp['w_merge'])
    wm = w_merge.reshape(DEPTH, D, 4, 8, 128).transpose(0, 1, 3, 2, 4).reshape(DEPTH, D, 4096)
    sh['wmerge'] = np.stack([ktile(wm[l], 512) for l in range(DEPTH)])
    bm = f(inp['b_merge']).reshape(DEPTH, 4, 8, 128)
    sh['bmerge'] = np.ascontiguousarray(bm.transpose(3, 0, 2, 1))
    wb = f(inp['w_branch']).reshape(DEPTH, 4, 2, 128, 8, 128)
    sh['wbranch'] = np.ascontiguousarray(wb.transpose(0, 4, 3, 1, 2, 5))
    w_out = f(inp['w_out'])
    sh['wout'] = np.stack([ktile(w_out[l], 512) for l in range(DEPTH)])
    w_ff1 = f(inp['w_ff1'])
    sh['wff1'] = np.stack([ktile(w_ff1[l], 512) for l in range(DEPTH)])
    w_ff2 = f(inp['w_ff2'])
    sh['wff2'] = np.ascontiguousarray(w_ff2.reshape(DEPTH, 8, 4, 128, 1024).transpose(0, 1, 3, 2, 4))
    w_uq = f(inp['w_uq'])
    cols, cols_sw = [], []
    for h in range(4):
        nope = [h * 96 + i for i in range(64)]
        cols += nope + [h * 96 + 64 + i for i in range(32)]
        cols_sw += nope + [h * 96 + 64 + (i + 16) % 32 for i in range(32)]
    wuq = np.stack([np.concatenate([w_uq[l][:, cols], w_uq[l][:, cols_sw]], 1) for l in range(DEPTH)])
    sh['wuq'] = np.ascontiguousarray(wuq.reshape(DEPTH, 2, 128, 768).transpose(2, 0, 1, 3))
    w_ukv = f(inp['w_ukv']).reshape(DEPTH, 128, 4, 128)
    kn = w_ukv[:, :, :, :64].reshape(DEPTH, 128, 256)
    vv = w_ukv[:, :, :, 64:].reshape(DEPTH, 128, 256)
    sh['wukv'] = np.ascontiguousarray(np.concatenate([kn, vv], -1).transpose(1, 0, 2))
    sh['qn'] = fm_vec(f(inp['mla_q_norm']))
    sh['kvn'] = fm_vec(f(inp['mla_kv_norm']))
    sh['kvn_row'] = f(inp['mla_kv_norm']).reshape(1, DEPTH * 128)
    sh['gvn'] = fm_vec(f(inp['gmlp_v_norm']))
    bs = f(inp['gmlp_b_s'])
    bsB = np.zeros((128, DEPTH, 2, 128), np.float32)
    for kk in range(2):
        bsB[0:64, :, kk, :] = bs[:, 2 * kk, :][None]
        bsB[64:128, :, kk, :] = bs[:, 2 * kk + 1, :][None]
    sh['bsB'] = bsB
    sh['wsT'] = np.ascontiguousarray(f(inp['gmlp_w_s']).transpose(3, 0, 1, 2))
    gb = f(inp['mlstm_gate_bias'])
    gbT = np.zeros((36, DEPTH, 2), np.float32)
    for l in range(DEPTH):
        for gate in range(2):
            gbT[0:4, l, gate] = gb[l, 0, gate]
            gbT[32:36, l, gate] = gb[l, 1, gate]
    sh['gbT'] = gbT
    sh['hn_row'] = f(inp['mlstm_head_norm']).reshape(1, DEPTH * 256)
    sh['lam_row'] = f(inp['diff_lambda']).reshape(1, DEPTH * 128)
    sh['sn_row'] = f(inp['diff_sub_norm']).reshape(1, DEPTH * 64)
    cos2, sin2 = rope_tables()
    sh['cosT'] = cos2
    sh['sinT'] = sin2
    ii = np.arange(128)
    sh['ident'] = np.eye(128, dtype=np.float32)
    sh['mask_le'] = (ii[:, None] <= ii[None, :]).astype(np.float32)
    sh['mask_ge'] = (ii[:, None] >= ii[None, :]).astype(np.float32)
    sel = np.zeros((36, 4, 128), np.float32)
    for r in range(4):
        sel[r, r, :] = 1.0
        sel[32 + r, r, :] = 1.0
    sh['sel'] = sel
    return sh


def prep_core(inp, c):
    f = lambda a: np.asarray(a, dtype=np.float32)
    xp = f(inp['x_prompt'])
    xs = f(inp['x_sample'])
    xtok = np.concatenate([xp[2 * c], xp[2 * c + 1], xs[c]], axis=0)
    m = {}
    m['xT'] = np.ascontiguousarray(xtok.reshape(T, 8, 128).transpose(2, 1, 0))
    cond = np.stack([f(inp['c_ctx']), f(inp['c'])[c]], axis=-1)
    m['condT'] = np.ascontiguousarray(cond.reshape(8, 128, 2).transpose(1, 0, 2))
    m['c_ckvT'] = np.ascontiguousarray(f(inp['cache_mla_ckv'])[c].transpose(0, 2, 1))
    m['c_krT'] = np.ascontiguousarray(f(inp['cache_mla_krope'])[c].transpose(0, 2, 1))
    dk = f(inp['cache_diff_k'])[c].reshape(DEPTH, 256, 8, 32)
    dkT = np.zeros((DEPTH, 96, 3, 256), np.float32)
    for p in range(8):
        dkT[:, (p % 3) * 32:(p % 3) * 32 + 32, p // 3, :] = dk[:, :, p, :].transpose(0, 2, 1)
    m['c_dkT'] = dkT
    m['c_dv'] = np.ascontiguousarray(f(inp['cache_diff_v'])[c].reshape(DEPTH, 2, 128, 4, 64).transpose(2, 0, 1, 3, 4))
    sC = f(inp['state_mlstm_C'])[c]
    sn = f(inp['state_mlstm_n'])[c]
    c0 = np.zeros((128, DEPTH, 2, 2, 65), np.float32)
    for h in range(4):
        c0[(h % 2) * 64:(h % 2) * 64 + 64, :, :, h // 2, 0:64] = sC[:, :, h].transpose(2, 0, 1, 3)
        c0[(h % 2) * 64:(h % 2) * 64 + 64, :, :, h // 2, 64] = sn[:, :, h].transpose(2, 0, 1)
    m['c_C0'] = c0
    sm = f(inp['state_mlstm_m'])[c]
    m0T = np.zeros((36, DEPTH), np.float32)
    m0T[0:4] = sm[:, 0].T
    m0T[32:36] = sm[:, 1].T
    m['c_m0T'] = m0T
    m['c_m0row'] = np.ascontiguousarray(sm.reshape(1, DEPTH * 8))
    return m


class StopBuild(Exception):
    pass


class Builder:
    def cut(self, name):
        if self.stop_after == name:
            raise StopBuild()

    def __init__(self, debug=(), nlayers=DEPTH, stop_after=None, wplan=None):
        self.stop_after = stop_after
        self.wplan = wplan
        self.debug = set(debug)
        self.nlayers = nlayers
        self.nc = bass.Bass("TRN2", target_bir_lowering=False)
        self.es = ExitStack()
        self.S = Sched(self.nc, self.es)
        self.ins = {}
        self.outs = {}
        self.out_keys = []
        self.pools = {'all': list(range(8)), 'lo': list(range(4))}
        self.rr = {'all': 0, 'lo': 0}

    def din(self, name, shape):
        ap = self.nc.dram_tensor(name, list(shape), F32, kind="ExternalInput").ap()
        self.ins[name] = ap
        return ap

    def dout(self, name, shape):
        ap = self.nc.dram_tensor(name, list(shape), F32, kind="ExternalOutput").ap()
        self.outs[name] = ap
        return ap

    def sb(self, name, shape, dt):
        return self.es.enter_context(self.nc.sbuf_tensor(name, list(shape), dt))

    def psum(self, pool='all'):
        lst = self.pools[pool]
        b = lst[self.rr[pool] % len(lst)]
        self.rr[pool] += 1
        return b

    def rsqrt(self, out_ap, in_ap, c, reads, writes):
        cb = self.cbias(c, out_ap)
        if True:
            self.S.op('act', lambda e: e.activation(out=out_ap, in_=in_ap, func=AF.Ln, bias=cb, scale=1.0),
                      reads=list(reads) + ['cbias'], writes=writes)
            self.S.op('act', lambda e: e.activation(out=out_ap, in_=out_ap, func=AF.Exp, scale=-0.5), reads=writes, writes=writes)
            return
        self.S.op('act', lambda e: e.activation(out=out_ap, in_=in_ap, func=AF.Sqrt, bias=cb, scale=1.0),
                  reads=list(reads) + ['cbias'], writes=writes)
        self.S.op('dve', lambda e: e.reciprocal(out=out_ap, in_=out_ap), reads=writes, writes=writes)

    def cbias(self, c, like=None):
        a = self._cb[round(float(c), 12)]
        if like is None:
            return a
        bp, n = like.base_partition(), like.shape[0]
        return a[bp:bp + n, :]

    def store(self, key, out_ap, in_ap, reads):
        if key not in self.out_keys:
            self.out_keys.append(key)
        self.S.dma('sp', key, out_ap, in_ap, reads=reads, writes=['dram_' + key])

    def build(self):
        nc, S, es = self.nc, self.S, self.es
        NG = len(WG_ORDER)
        xT_d = self.din('xT', [128, 8, T])
        condT_d = self.din('condT', [128, 8, 2])
        wmod_d = self.din('wmod', [DEPTH, 12, 128, 8, 512])
        bmod_d = self.din('bmod', [128, DEPTH, 48])
        ng_d = self.din('ng', [128, DEPTH, 4, 8])
        win_d = self.din('win', [DEPTH, NG, 128, 8, 512])
        wmerge_d = self.din('wmerge', [DEPTH, 8, 128, 8, 512])
        bmerge_d = self.din('bmerge', [128, DEPTH, 8, 4])
        wbranch_d = self.din('wbranch', [DEPTH, 8, 128, 4, 2, 128])
        wout_d = self.din('wout', [DEPTH, 2, 128, 8, 512])
        wff1_d = self.din('wff1', [DEPTH, 8, 128, 8, 512])
        wff2_d = self.din('wff2', [DEPTH, 8, 128, 4, 1024])
        wuq_d = self.din('wuq', [128, DEPTH, 2, 768])
        wukv_d = self.din('wukv', [128, DEPTH, 512])
        qn_d = self.din('qn', [128, DEPTH, 2])
        kvn_d = self.din('kvn', [128, DEPTH, 1])
        kvnrow_d = self.din('kvn_row', [1, DEPTH * 128])
        gvn_d = self.din('gvn', [128, DEPTH, 2])
        bsB_d = self.din('bsB', [128, DEPTH, 2, 128])
        wsT_d = self.din('wsT', [128, DEPTH, 4, 128])
        gbT_d = self.din('gbT', [36, DEPTH, 2])
        hnrow_d = self.din('hn_row', [1, DEPTH * 256])
        lamrow_d = self.din('lam_row', [1, DEPTH * 128])
        snrow_d = self.din('sn_row', [1, DEPTH * 64])
        cos_d = self.din('cosT', [128, 1024])
        sin_d = self.din('sinT', [128, 1024])
        ident_d = self.din('ident', [128, 128])
        mle_d = self.din('mask_le', [128, 128])
        mge_d = self.din('mask_ge', [128, 128])
        sel_d = self.din('sel', [36, 4, 128])
        cckv_d = self.din('c_ckvT', [DEPTH, 128, 256])
        ckr_d = self.din('c_krT', [DEPTH, 32, 256])
        cdk_d = self.din('c_dkT', [DEPTH, 96, 3, 256])
        cdv_d = self.din('c_dv', [128, DEPTH, 2, 4, 64])
        cC0_d = self.din('c_C0', [128, DEPTH, 2, 2, 65])
        cm0T_d = self.din('c_m0T', [36, DEPTH])
        cm0row_d = self.din('c_m0row', [1, DEPTH * 8])
        yT_d = self.dout('yT', [128, 8, T])
        o_ckv = self.dout('o_ckv', [2, DEPTH, 256, 128])
        o_kr = self.dout('o_kr', [2, DEPTH, 256, 32])
        o_dk = self.dout('o_dk', [2, DEPTH, 256, 256])
        o_dv = self.dout('o_dv', [2, DEPTH, 256, 256])
        o_C = self.dout('o_C', [2, DEPTH, 2, 4, 64, 64])
        o_n = self.dout('o_n', [2, DEPTH, 2, 4, 64])
        o_m = self.dout('o_m', [2, DEPTH, 2, 4])

        xT = self.sb('xT_sb', [128, 8, T], F32)
        hT = self.sb('hT_sb', [128, 8, T], BF16)
        AR_N = 43008
        arena = self.sb('arena', [128, AR_N], BF16)
        WSLOT = 3
        wbuf = [self.sb(f'wbuf{i}', [128, 4096], BF16) for i in range(WSLOT)]
        wbr = [self.sb(f'wbr{i}', [128, 1024], BF16) for i in range(2)]
        ones = self.sb('ones', [128, 128], BF16)
        identb = self.sb('identb', [128, 128], BF16)
        identf = self.sb('identf', [36, 36], F32)
        mle = self.sb('mle', [128, 128], BF16)
        mge = self.sb('mge', [128, 128], BF16)
        sel = self.sb('sel_sb', [36, 4, 128], F32)
        cosT = self.sb('cos_sb', [128, 1024], F32)
        sinT = self.sb('sin_sb', [128, 1024], F32)
        condT = self.sb('condT_sb', [128, 8, 2], F32)
        scond = self.sb('scond', [128, 8, 2], BF16)
        bmod = self.sb('bmod_sb', [128, DEPTH, 48], F32)
        ng = self.sb('ng_sb', [128, DEPTH, 4, 8], F32)
        bmerge = self.sb('bmerge_sb', [128, DEPTH, 8, 4], F32)
        modT = self.sb('modT', [128, 48, 2], F32)
        msc = self.sb('msc', [128, 6, 8, 2], F32)
        wuq = self.sb('wuq_sb', [128, 1, 2, 768], BF16)
        wukv = self.sb('wukv_sb', [128, 1, 512], BF16)
        qn = self.sb('qn_sb', [128, DEPTH, 2], F32)
        kvn = self.sb('kvn_sb', [128, DEPTH, 1], F32)
        kvnB = self.sb('kvnB', [128, 128], F32)
        gvn = self.sb('gvn_sb', [128, DEPTH, 2], F32)
        bsB = self.sb('bsB_sb', [128, 1, 2, 128], F32)
        wsT = self.sb('wsT_sb', [128, 1, 4, 128], BF16)
        gbT = self.sb('gbT_sb', [36, DEPTH, 2], F32)
        hnB = self.sb('hnB', [128, 256], F32)
        lamB = self.sb('lamB', [128, 128], F32)
        snB = self.sb('snB', [128, 64], F32)
        m0T = self.sb('m0T', [36, DEPTH], F32)
        m0B = self.sb('m0B', [128, DEPTH * 8], F32)
        C0 = self.sb('C0_sb', [128, DEPTH, 2, 2, 65], BF16)
        smalls = self.sb('smalls', [128, 64], F32)
        zero1 = self.sb('zero1', [128, 1], F32)
        ps = [es.enter_context(nc.psum_tensor(f'ps{i}', [128, 512], F32)) for i in range(8)]

        def carve(off_b, shape, dt):
            esz = 2 if dt == BF16 else 4
            n = int(np.prod(shape[1:]))
            a = arena[0:shape[0], off_b // 2: off_b // 2 + n * esz // 2]
            if dt != BF16:
                a = a.bitcast(dt)
            if len(shape) == 3:
                a = a.rearrange("p (a b) -> p a b", a=shape[1])
            elif len(shape) == 4:
                a = a.rearrange("p (a b c) -> p a b c", a=shape[1], b=shape[2])
            return a
        KB = 1024
        assert AR_N * 2 == 84 * KB
        ybr = carve(0, [128, 4, 2, T], BF16)
        mrg = carve(24 * KB, [128, 8, T], BF16)
        ffo = carve(0, [128, 8, T], F32)
        f1g = [carve(48 * KB + i * 12 * KB, [128, 4, T], BF16) for i in range(2)]
        yf = [carve(48 * KB + i * 12 * KB, [128, 8, 384], F32)[:, :, 0:384] for i in range(2)]
        sq = carve(72 * KB, [128, 8, 512], BF16)
        rstd = [carve(80 * KB + i * 2 * KB, [128, 512], F32) for i in range(2)]
        MS = 24 * KB

        self.wslot = 0

        self.wreq = []
        self.wissued = 0

        def w_issue(j):
            name, off, apl, ncols = self.wplan[j]
            src = bass.AP(self.ins[name].tensor, off, [list(x) for x in apl])
            i = j % WSLOT
            S.dma('pool', f'w{i}', wbuf[i][:, 0:ncols], src, writes=[f'wbuf{i}'])

        def wload(src3, ncols_total):
            j = len(self.wreq)
            desc = (src3.name, int(src3.offset), tuple(tuple(x) for x in src3.ap), int(ncols_total))
            self.wreq.append(desc)
            i = j % WSLOT
            if self.wplan is None:
                S.dma('pool', f'w{i}', wbuf[i][:, 0:ncols_total], src3, writes=[f'wbuf{i}'])
            else:
                assert self.wplan[j] == desc, (j, self.wplan[j], desc)
                while self.wissued <= min(j + 1, len(self.wplan) - 1):
                    w_issue(self.wissued)
                    self.wissued += 1
            return wbuf[i], f'wbuf{i}'

        def tl(tt):
            return slice(tt * 512, (tt + 1) * 512)

        self.rs_i = 0

        def rms_rstd(src3, src_key, nk, dtot, n=512):
            i = self.rs_i
            self.rs_i ^= 1
            b = self.psum()
            kf = src_key if callable(src_key) else (lambda k: src_key)
            for k in range(nk):
                S.op('act', lambda e: e.activation(out=sq[:, k, 0:n], in_=src3[:, k, :], func=AF.Square),
                     reads=[kf(k)], writes=[f'sq{k}'])
                S.op('pe', lambda e: e.matmul(ps[b][:, 0:n], lhsT=ones[:, :], rhs=sq[:, k, 0:n],
                                              start=(k == 0), stop=(k == nk - 1)),
                     reads=[f'sq{k}', 'ones'], writes=[f'ps{b}'], inc=(k == nk - 1))
            self.rsqrt(rstd[i][:, 0:n], ps[b][:, 0:n], float(dtot * EPS), [f'ps{b}'], [f'rstd{i}'])
            return rstd[i], f'rstd{i}'

        def fm_mm(out_ap, okey, w3, wk, c0, M, rhs_fn, rkeys, nk=8):
            for k in range(nk):
                rk_ = [(x + f'_{k}') if x.startswith('hT') else x for x in rkeys]
                S.op('pe', lambda e: e.matmul(out_ap, lhsT=w3[:, k, c0:c0 + M], rhs=rhs_fn(k),
                                              start=(k == 0), stop=(k == nk - 1)),
                     reads=[wk] + rk_, writes=[okey], inc=(k == nk - 1))

        def tm_mm(out_ap, okey, w3, wk, c0, N, tok0, nk=8):
            tt = tok0 // 512
            for k in range(nk):
                S.op('pe', lambda e: e.matmul(out_ap, lhsT=hT[:, k, tok0:tok0 + 128], rhs=w3[:, k, c0:c0 + N],
                                              start=(k == 0), stop=(k == nk - 1)),
                     reads=[wk, f'hT{tt}_{k}'], writes=[okey], inc=(k == nk - 1))

        def to_fm(src_bf, skey, br, blk):
            b = self.psum('lo')
            pb_ = ps[b][:, :].bitcast(BF16)
            for kk in range(2):
                S.op('pe', lambda e: e.transpose(out=pb_[:, kk * 128:(kk + 1) * 128], in_=src_bf[:, kk * 128:(kk + 1) * 128],
                                                 identity=identb[:, :]),
                     reads=[skey, 'identb'], writes=[f'ps{b}'], inc=(kk == 1))
            S.op('act', lambda e: e.copy(out=ybr[:, br, :, blk * 128:(blk + 1) * 128],
                                         in_=pb_[:, 0:256].rearrange("p (k t) -> p k t", k=2)),
                 reads=[f'ps{b}'], writes=['ybr'])

        S.op('pool', lambda e: e.memset(ones[:], 1.0), writes=['ones'])
        S.op('pool', lambda e: e.memset(zero1[:], 0.0), writes=['zero1'])
        cbt = self.sb('cbt', [128, 8], F32)
        self._cb = {}
        for i, c in enumerate([D * EPS, 256 * EPS, 128 * EPS, 64 * EPS, 1.0]):
            S.op('pool', lambda e: e.memset(cbt[:, i:i + 1], float(c)), writes=['cbias'])
            self._cb[round(float(c), 12)] = cbt[:, i:i + 1]
        S.dma('sp', 'x', xT[:], xT_d, writes=[f'xT{a}_{b}' for a in range(NT) for b in range(8)])
        cl = [(condT, condT_d, 'condT'), (bmod, bmod_d, 'bmod'), (ng, ng_d, 'ng'), (bmerge, bmerge_d, 'bmerge'),
              (qn, qn_d, 'qn'), (kvn, kvn_d, 'kvn'), (gvn, gvn_d, 'gvn'), (gbT, gbT_d, 'gbT'),
              (cosT, cos_d, 'cosT'), (sinT, sin_d, 'sinT'), (sel, sel_d, 'sel'), (m0T, cm0T_d, 'm0T'),
              (identf, ident_d[0:36, 0:36], 'identf')]
        for (dst, src, key) in cl:
            S.dma('sp', 'c_' + key, dst[:], src, writes=[key])
        for (dst, src, key, n) in [(m0B, cm0row_d, 'm0B', DEPTH * 8)]:
            S.dma('sp', 'c_' + key, dst[:], src.partition_broadcast(128), writes=[key])
        for (dst, src, key) in [(identb, ident_d, 'identb'), (mle, mle_d, 'mle'), (mge, mge_d, 'mge'),
                                (C0, cC0_d, 'C0')]:
            S.dma('pool', 'c_' + key, dst[:], src, writes=[key])
        S.op('act', lambda e: e.activation(out=scond[:], in_=condT[:], func=AF.Silu),
             reads=['condT'], writes=['scond'])

        def norm_rms(tt):
            return rms_rstd(xT[:, :, tl(tt)], (lambda k, tt=tt: f'xT{tt}_{k}'), 8, D)

        def norm_apply(l, which, tt, rr_):
            ia, ib = (0, 1) if which == 0 else (3, 4)
            c = 0 if tt == 0 else 1
            r, rk = rr_
            for k in range(8):
                t_ = carve(18 * KB + (k % 2) * 2 * KB, [128, 512], F32)
                tk = f'nm_tmp{k % 2}'
                S.op('dve', lambda e: e.scalar_tensor_tensor(
                    out=t_, in0=xT[:, k, tl(tt)], scalar=msc[:, ia, k, c:c + 1], in1=r[:, :],
                    op0=ALU.mult, op1=ALU.mult), reads=[f'xT{tt}_{k}', 'msc', rk], writes=[tk])
                S.op('act', lambda e: e.activation(out=hT[:, k, tl(tt)], in_=t_, func=AF.Identity,
                                                   bias=msc[:, ib, k, c:c + 1], scale=1.0),
                     reads=[tk, 'msc'], writes=[f'hT{tt}_{k}'])

        def norm_mod_tile(l, which, tt):
            norm_apply(l, which, tt, norm_rms(tt))

        def norm_mod(l, which):
            for tt in range(NT):
                norm_mod_tile(l, which, tt)

        def resid_apply(src3, skey, tt, ig, tmp_off, rr_):
            c = 0 if tt == 0 else 1
            kf = skey if callable(skey) else (lambda k: skey)
            r, rk = rr_
            for k in range(8):
                t_ = carve(tmp_off + (k % 2) * 2 * KB, [128, 512], F32)
                tk = f'rs_tmp{k % 2}'
                S.op('dve', lambda e: e.scalar_tensor_tensor(
                    out=t_, in0=src3[:, k, :], scalar=msc[:, ig, k, c:c + 1], in1=r[:, :],
                    op0=ALU.mult, op1=ALU.mult), reads=[kf(k), 'msc', rk], writes=[tk])
                S.op('dve', lambda e: e.tensor_tensor(out=xT[:, k, tl(tt)], in0=xT[:, k, tl(tt)], in1=t_,
                                                      op=ALU.add), reads=[tk, f'xT{tt}_{k}'], writes=[f'xT{tt}_{k}'])

        def resid(src3, skey, tt, ig, tmp_off):
            resid_apply(src3, skey, tt, ig, tmp_off, rms_rstd(src3, skey, 8, D))

        modN = self.sb('modN', [128, DEPTH, 48, 2], F32)

        def p0_step(lay, g, scratch_off):
            wt, wk = wload(wmod_d[lay, g].rearrange("p k j -> p (k j)"), 4096)
            w3 = wt[:, 0:4096].rearrange("p (k j) -> p k j", k=8)
            bg = self.psum('lo')
            for k in range(8):
                S.op('pe', lambda e: e.matmul(ps[bg][0:2, :], lhsT=scond[:, k, :], rhs=w3[:, k, :], start=(k == 0), stop=(k == 7)),
                     reads=[wk, 'scond'], writes=[f'ps{bg}'], inc=(k == 7))
            mt = carve(scratch_off, [2, 512], F32)
            S.op('act', lambda e: e.copy(out=mt, in_=ps[bg][0:2, :]), reads=[f'ps{bg}'], writes=['mtmp'])
            bt = self.psum('lo')
            for jj in range(4):
                S.op('pe', lambda e: e.matmul(ps[bt][:, 2 * jj:2 * jj + 2], lhsT=mt[0:2, jj * 128:(jj + 1) * 128], rhs=identf[0:2, 0:2],
                                              start=True, stop=True), reads=['mtmp', 'identf'], writes=[f'ps{bt}'], inc=(jj == 3))
            pm = ps[bt][:, 0:8].rearrange("p (j c) -> p j c", c=2)
            S.op('dve', lambda e: e.tensor_tensor(out=modN[:, lay, 4 * g:4 * g + 4, :], in0=pm,
                                                  in1=bmod[:, lay, 4 * g:4 * g + 4].unsqueeze(2).to_broadcast([128, 4, 2]), op=ALU.add),
                 reads=[f'ps{bt}', 'bmod'], writes=['modN'])

        self.p0_queue = [(lay, g) for lay in range(self.nlayers) for g in range(12)]

        def p0_flush(lay, gmax, scratch_off):
            while self.p0_queue and (self.p0_queue[0][0] < lay or (self.p0_queue[0][0] == lay and self.p0_queue[0][1] <= gmax)):
                la, g = self.p0_queue.pop(0)
                p0_step(la, g, scratch_off)

        def p0_hook(scratch_off):
            if self.p0_queue:
                la, g = self.p0_queue.pop(0)
                p0_step(la, g, scratch_off)
        self.p0_hook = p0_hook

        def msc_derive(l, which):
            for c in range(2):
                for (dst, isc, ign) in (((0, 1, 0),) if which == 0 else ((3, 4, 2),)):
                    S.op('dve', lambda e: e.scalar_tensor_tensor(
                        out=msc[:, dst, :, c], in0=modN[:, l, isc * 8:(isc + 1) * 8, c], scalar=1.0,
                        in1=ng[:, l, ign, :], op0=ALU.add, op1=ALU.mult), reads=['modN', 'ng'], writes=['msc'])
                    S.op('dve', lambda e: e.tensor_scalar(
                        out=msc[:, dst, :, c], in0=msc[:, dst, :, c], scalar1=32.0, scalar2=None, op0=ALU.mult),
                        reads=['msc'], writes=['msc'])
                for (dst, ish) in (((1, 0),) if which == 0 else ((4, 3),)):
                    S.op('dve', lambda e: e.tensor_copy(out=msc[:, dst, :, c], in_=modN[:, l, ish * 8:(ish + 1) * 8, c]),
                         reads=['modN'], writes=['msc'])
                if which == 1:
                    for (dst, ig, ign) in ((2, 2, 1), (5, 5, 3)):
                        S.op('dve', lambda e: e.scalar_tensor_tensor(
                            out=msc[:, dst, :, c], in0=modN[:, l, ig * 8:(ig + 1) * 8, c], scalar=32.0,
                            in1=ng[:, l, ign, :], op0=ALU.mult, op1=ALU.mult), reads=['modN', 'ng'], writes=['msc'])

        try:
          self.cut('init')
          for l in range(self.nlayers):
              S.dma('sp', 'c_bsB', bsB[:, 0], bsB_d[:, l], writes=['bsB'])
              S.dma('sp', 'c_kvnB', kvnB[:], kvnrow_d[:, l * 128:(l + 1) * 128].partition_broadcast(128), writes=['kvnB'])
              S.dma('sp', 'c_hnB', hnB[:], hnrow_d[:, l * 256:(l + 1) * 256].partition_broadcast(128), writes=['hnB'])
              S.dma('sp', 'c_lamB', lamB[:], lamrow_d[:, l * 128:(l + 1) * 128].partition_broadcast(128), writes=['lamB'])
              S.dma('sp', 'c_snB', snB[:], snrow_d[:, l * 64:(l + 1) * 64].partition_broadcast(128), writes=['snB'])
              S.dma('pool', 'c_wuq', wuq[:, 0], wuq_d[:, l], writes=['wuq'])
              S.dma('pool', 'c_wukv', wukv[:, 0], wukv_d[:, l], writes=['wukv'])
              S.dma('pool', 'c_wsT', wsT[:, 0], wsT_d[:, l], writes=['wsT'])
              p0_flush(l, 3, MS)
              msc_derive(l, 0)

              nrm = {0: norm_rms(0), 1: norm_rms(1)}

              def p1_tile(tt):
                  norm_apply(l, 0, tt, nrm[tt])
                  if tt == 0:
                      nrm[2] = norm_rms(2)
              self.mixer_B(l, locals(), pre_tile=p1_tile)
              S.barrier()
              self.cut(f'B_{l}')
              for samp in (False, True):
                  self.rsqrt_lnexp = True
                  self.mixer_A(l, locals(), samp)
                  S.barrier()
                  self.cut(f'A{int(samp)}_{l}')
                  self.mixer_D(l, locals(), samp)
                  S.barrier()
                  self.cut(f'D{int(samp)}_{l}')
                  self.mixer_C(l, locals(), samp)
                  S.barrier()
                  self.cut(f'C{int(samp)}_{l}')
              self.rsqrt_lnexp = False

              if l == 0 and 'ybr' in self.debug:
                  o = self.dout('dbg_ybr', [128, 8 * T])
                  S.dma('pool', 'dbg_ybr', o, arena[:, 0:8 * T], reads=['ybr'], writes=['dbgo1'])
                  self.out_keys.append('dbg_ybr')
              gsb_all = [carve(MS + 24 * KB + i * 2 * KB, [128, 512], F32) for i in range(8)]
              self.gs_i = 0
              for n in range(8):
                  wt, wk = wload(wmerge_d[l, n].rearrange("p k j -> p (k j)"), 4096)
                  wm3 = wt[:, 0:4096].rearrange("p (k j) -> p k j", k=8)
                  ib = n % 2
                  if n == 0:
                      S.dma('pool', 'wbr0', wbr[0][:, :], wbranch_d[l, 0].rearrange("p b k j -> p (b k j)"), writes=['wbr0'])
                  if n + 1 < 8:
                      S.dma('pool', f'wbr{1 - ib}', wbr[1 - ib][:, :], wbranch_d[l, n + 1].rearrange("p b k j -> p (b k j)"),
                            writes=[f'wbr{1 - ib}'])
                  wb4 = wbr[ib][:, :].rearrange("p (b k j) -> p b k j", b=4, k=2)
                  for tt in range(NT):
                      self.gs_i ^= 1
                      gsb = gsb_all[4 * self.gs_i:4 * self.gs_i + 4]
                      go = 4 * self.gs_i
                      for br in range(4):
                          bg = self.psum()
                          fm_mm(ps[bg][:, :], f'ps{bg}', wm3, wk, br * 128, 128, lambda k: hT[:, k, tl(tt)], [f'hT{tt}'])
                          S.op('act', lambda e: e.activation(
                              out=gsb[br], in_=ps[bg][:, :], func=AF.Sigmoid, bias=bmerge[:, l, n, br:br + 1], scale=1.0),
                              reads=[f'ps{bg}', 'bmerge'], writes=[f'gsb{go + br}'])
                          bb = self.psum()
                          for kk in range(2):
                              S.op('pe', lambda e: e.matmul(ps[bb][:, :], lhsT=wb4[:, br, kk, :], rhs=ybr[:, br, kk, tl(tt)],
                                                            start=(kk == 0), stop=(kk == 1)),
                                   reads=[f'wbr{ib}', 'ybr'], writes=[f'ps{bb}'], inc=(kk == 1))
                          S.op('dve', lambda e: e.tensor_tensor(out=gsb[br], in0=ps[bb][:, :], in1=gsb[br], op=ALU.mult),
                               reads=[f'ps{bb}', f'gsb{go + br}'], writes=[f'gsb{go + br}'])
                      S.op('pool', lambda e: e.tensor_tensor(out=gsb[0], in0=gsb[0], in1=gsb[1], op=ALU.add),
                           reads=[f'gsb{go}', f'gsb{go + 1}'], writes=[f'gsb{go}'])
                      S.op('pool', lambda e: e.tensor_tensor(out=gsb[2], in0=gsb[2], in1=gsb[3], op=ALU.add),
                           reads=[f'gsb{go + 2}', f'gsb{go + 3}'], writes=[f'gsb{go + 2}'])
                      S.op('pool', lambda e: e.tensor_tensor(out=mrg[:, n, tl(tt)], in0=gsb[0], in1=gsb[2], op=ALU.add),
                           reads=[f'gsb{go}', f'gsb{go + 2}'], writes=['mrg'])
              S.barrier()

              if l == 0 and 'mrg' in self.debug:
                  o = self.dout('dbg_mrg', [128, 8 * T])
                  S.dma('pool', 'dbg_mrg', o, arena[:, 8 * T:16 * T], reads=['mrg'], writes=['dbgo2'])
                  self.out_keys.append('dbg_mrg')
              self.cut(f'P4_{l}')
              wo = []
              for g in range(2):
                  wt, wk = wload(wout_d[l, g].rearrange("p k j -> p (k j)"), 4096)
                  wo.append((wt[:, 0:4096].rearrange("p (k j) -> p k j", k=8), wk))
              p0_flush(l, 11, 64 * KB + 4 * KB)
              msc_derive(l, 1)
              yfs = [carve(0, [128, 8, 512], F32), carve(48 * KB, [128, 8, 512], F32)]

              def p5_mm(tt):
                  yfull = yfs[tt % 2]
                  yk = f'yfull{tt % 2}'
                  for n in range(8):
                      w3, wk = wo[n // 4]
                      b = self.psum()
                      fm_mm(ps[b][:, :], f'ps{b}', w3, wk, (n % 4) * 128, 128, lambda k: mrg[:, k, tl(tt)], ['mrg'])
                      S.op('act', lambda e: e.copy(out=yfull[:, n, :], in_=ps[b][:, :]), reads=[f'ps{b}'], writes=[f'{yk}_{n}'])

              def p5_rms(tt):
                  return rms_rstd(yfs[tt % 2], (lambda k, tt=tt: f'yfull{tt % 2}_{k}'), 8, D)

              def p5_app(tt, rr_):
                  resid_apply(yfs[tt % 2], (lambda k, tt=tt: f'yfull{tt % 2}_{k}'), tt, 2, 64 * KB, rr_)
                  if l == 0 and 'x1' in self.debug and tt == NT - 1:
                      o = self.dout('dbg_x1', [128, 8, T])
                      S.dma('sp', 'dbg_x1', o, xT[:], reads=[f'xT{a}_{b}' for a in range(NT) for b in range(8)], writes=['dbgo3'])
                      self.out_keys.append('dbg_x1')

              p5_mm(0)
              p5_mm(1)
              p5_app(0, p5_rms(0))
              p5_mm(2)
              n0 = norm_rms(0)
              r1 = p5_rms(1)
              norm_apply(l, 1, 0, n0)
              p5_app(1, r1)
              n1 = norm_rms(1)
              r2 = p5_rms(2)
              norm_apply(l, 1, 1, n1)
              p5_app(2, r2)
              norm_apply(l, 1, 2, norm_rms(2))
              self.cut(f'P5_{l}')
              S.barrier()

              for g in range(8):
                  wt, wk = wload(wff1_d[l, g].rearrange("p k j -> p (k j)"), 4096)
                  w3 = wt[:, 0:4096].rearrange("p (k j) -> p k j", k=8)
                  fi = g % 2
                  for jj in range(4):
                      for tt in range(NT):
                          b = self.psum()
                          fm_mm(ps[b][:, :], f'ps{b}', w3, wk, jj * 128, 128, lambda k: hT[:, k, tl(tt)], [f'hT{tt}'])
                          self.ft_i = (getattr(self, 'ft_i', 0) + 1) % 2
                          ftmp = carve(72 * KB + self.ft_i * 2 * KB, [128, 512], F32)
                          S.op('act', lambda e: e.activation(out=ftmp, in_=ps[b][:, :], func=AF.Relu),
                               reads=[f'ps{b}'], writes=[f'sq{2 * self.ft_i}', f'sq{2 * self.ft_i + 1}'])
                          S.op('act', lambda e: e.activation(out=f1g[fi][:, jj, tl(tt)], in_=ftmp, func=AF.Square),
                               reads=[f'sq{2 * self.ft_i}', f'sq{2 * self.ft_i + 1}'], writes=[f'f1g{fi}'])
                  wt2, wk2 = wload(wff2_d[l, g].rearrange("p k j -> p (k j)"), 4096)
                  w23 = wt2[:, 0:4096].rearrange("p (k j) -> p k j", k=4)
                  for n in range(8):
                      for tt in range(NT):
                          b = self.psum()
                          fm_mm(ps[b][:, :], f'ps{b}', w23, wk2, n * 128, 128, lambda k: f1g[fi][:, k, tl(tt)], [f'f1g{fi}'], nk=4)
                          if g == 0:
                              S.op('act', lambda e: e.copy(out=ffo[:, n, tl(tt)], in_=ps[b][:, :]),
                                   reads=[f'ps{b}'], writes=[f'ffo{tt}'])
                          else:
                              S.op('dve', lambda e: e.tensor_tensor(out=ffo[:, n, tl(tt)], in0=ps[b][:, :], in1=ffo[:, n, tl(tt)],
                                                                    op=ALU.add),
                                   reads=[f'ps{b}', f'ffo{tt}'], writes=[f'ffo{tt}'])
              S.barrier()
              for tt in range(NT):
                  resid(ffo[:, :, tl(tt)], f'ffo{tt}', tt, 5, 48 * KB)
              S.barrier()

        except StopBuild:
            S.barrier()
        self.store('out_y', yT_d, xT[:], [f'xT{a}_{b}' for a in range(NT) for b in range(8)])
        S.finish(self.out_keys)
        return nc

    def mixer_B(self, l, L, pre_tile=None):
        S = self.S
        g_ = lambda n: L[n]
        ps, hT, ybr, carve, wload, fm_mm, tm_mm, win_d = (g_('ps'), g_('hT'), g_('ybr'), g_('carve'), g_('wload'),
                                                             g_('fm_mm'), g_('tm_mm'), g_('win_d'))
        gvn, bsB, wsT, smalls, tl, MS, KB = g_('gvn'), g_('bsB'), g_('wsT'), g_('smalls'), g_('tl'), g_('MS'), g_('KB')
        gB1, gB2 = WG_ORDER.index('B1'), WG_ORDER.index('B2')
        wt1, wk1 = wload(win_d[l, gB1][:, :, 0:256], 8 * 256)
        w31 = wt1[:, 0:2048].rearrange("p (k j) -> p k j", k=8)
        wt2, wk2 = wload(win_d[l, gB2][:, :, 0:256], 8 * 256)
        w32 = wt2[:, 0:2048].rearrange("p (k j) -> p k j", k=8)
        vr = [carve(MS + i * 512, [128, 256], BF16) for i in range(2)]
        junks = [carve(MS + 2 * KB + i * KB, [128, 256], F32) for i in range(2)]
        tmpms = [carve(MS + 4 * KB + i * KB, [128, 2, 128], F32) for i in range(2)]
        for tt in range(NT):
            if pre_tile is not None:
                pre_tile(tt)
            ub = [5 + (2 * tt) % 3, 5 + (2 * tt + 1) % 3]
            for kk in range(2):
                fm_mm(ps[ub[kk]][:, :], f'ps{ub[kk]}', w31, wk1, kk * 128, 128, lambda k: hT[:, k, tl(tt)], [f'hT{tt}'])
            for bi in range(4):
                blk = tt * 4 + bi
                vb = self.psum('lo')
                tm_mm(ps[vb][:, 0:256], f'ps{vb}', w32, wk2, 0, 256, blk * 128)
                pq = bi % 2
                junk, tmpm = junks[pq], tmpms[pq]
                ss = smalls[:, 2 * pq:2 * pq + 1]
                rs_ = smalls[:, 2 * pq + 1:2 * pq + 2]
                S.op('act', lambda e: e.activation(out=junk, in_=ps[vb][:, 0:256], func=AF.Square, accum_out=ss),
                     reads=[f'ps{vb}'], writes=[f'junkB{pq}', f'ssB{pq}'])
                self.rsqrt(rs_, ss, float(256 * EPS), [f'ssB{pq}'], [f'rsB{pq}'])
                v_ = vr[bi % 2]
                S.op('dve', lambda e: e.tensor_scalar(out=v_, in0=ps[vb][:, 0:256], scalar1=rs_, scalar2=16.0,
                                                      op0=ALU.mult, op1=ALU.mult),
                     reads=[f'ps{vb}', f'rsB{pq}'], writes=[f'vrB{bi % 2}'])
                mb = self.psum('lo')
                for g in range(4):
                    S.op('pe', lambda e: e.matmul(ps[mb][(g % 2) * 64:(g % 2) * 64 + 64, (g // 2) * 128:(g // 2) * 128 + 128],
                                                  lhsT=v_[:, g * 64:(g + 1) * 64], rhs=wsT[:, 0, g, :], start=True, stop=True),
                         reads=[f'vrB{bi % 2}', 'wsT'], writes=[f'ps{mb}'], inc=(g == 3))
                for kk in range(2):
                    S.op('dve', lambda e: e.scalar_tensor_tensor(
                        out=tmpm[:, kk, :], in0=ps[mb][:, kk * 128:(kk + 1) * 128], scalar=gvn[:, l, kk:kk + 1],
                        in1=bsB[:, 0, kk, :], op0=ALU.mult, op1=ALU.add),
                        reads=[f'ps{mb}', 'gvn', 'bsB'], writes=[f'tmpmB{pq}'])
                    S.op('dve', lambda e: e.tensor_tensor(
                        out=ybr[:, 1, kk, blk * 128:(blk + 1) * 128], in0=ps[ub[kk]][:, bi * 128:(bi + 1) * 128],
                        in1=tmpm[:, kk, :], op=ALU.mult),
                        reads=[f'ps{ub[kk]}', f'tmpmB{pq}'], writes=['ybr'])

    def attn_core(self, L, streams, nkc, Nt, scale):
        S = self.S
        ps, carve, MS, KB = L['ps'], L['carve'], L['MS'], L['KB']
        nb = Nt // 128
        qs = [st[0]() for st in streams]

        def stage1(si, sc):
            qa, qk = qs[si]
            ka, kk_ = streams[si][1](sc)
            sb_ = self.psum('lo')
            S.op('pe', lambda e: e.matmul(ps[sb_][:, 0:Nt], lhsT=ka, rhs=qa, start=True, stop=True),
                 reads=[kk_, qk], writes=[f'ps{sb_}'])
            pi = self.pt_i
            self.pt_i = (self.pt_i + 1) % 6
            pT = carve(MS + 42 * KB + pi * KB, [128, 512], BF16)
            S.op('act', lambda e: e.activation(out=pT[:, 0:Nt], in_=ps[sb_][:, 0:Nt], func=AF.Exp, scale=float(scale)),
                 reads=[f'ps{sb_}'], writes=[f'pT{pi}'])
            return pT, pi

        cur = [stage1(si, 0) for si in range(len(streams))]
        for sc in range(nkc):
            nxt = [stage1(si, sc + 1) if sc + 1 < nkc else None for si in range(len(streams))]
            for si, st in enumerate(streams):
                pT, pi = cur[si]
                va, vk = st[2](sc)
                accb = st[3]
                for tb in range(nb):
                    S.op('pe', lambda e: e.matmul(ps[accb][:, tb * 65:(tb + 1) * 65], lhsT=pT[:, tb * 128:(tb + 1) * 128], rhs=va,
                                                  start=(sc == 0 and tb == 0), stop=(sc == nkc - 1 and tb == nb - 1),
                                                  skip_group_check=True),
                         reads=[f'pT{pi}', vk], writes=[f'ps{accb}'], inc=(tb == nb - 1))
            cur = nxt

    @staticmethod
    def geo(samp):
        if samp:
            return dict(t0=512, nt=1024, tiles=[1, 2], seqs=[(0, 1024)], kofs=256, nkb=10)
        return dict(t0=0, nt=512, tiles=[0], seqs=[(0, 256), (256, 256)], kofs=0, nkb=4)

    def mixer_A(self, l, L, samp):
        S = self.S
        g_ = lambda n: L[n]
        ps, hT, carve, wload, fm_mm, tm_mm, win_d, tl, MS, KB = (g_('ps'), g_('hT'), g_('carve'), g_('wload'), g_('fm_mm'),
                                                                  g_('tm_mm'), g_('win_d'), g_('tl'), g_('MS'), g_('KB'))
        wuq, wukv, qn, kvn, kvnB, cosT, sinT, smalls = (g_('wuq'), g_('wukv'), g_('qn'), g_('kvn'), g_('kvnB'), g_('cosT'),
                                                         g_('sinT'), g_('smalls'))
        rms_rstd, to_fm = g_('rms_rstd'), g_('to_fm')
        G = self.geo(samp)
        t0, tiles, kofs, nkb = G['t0'], G['tiles'], G['kofs'], G['nkb']
        nk = nkb * 128
        self.pt_i = 0
        qT = carve(MS, [96, 4, 1024], BF16)
        ckvnT = carve(MS + 8 * KB, [128, 1280], BF16)
        krT = carve(MS + 8 * KB + 2560, [96, 1280], BF16)
        KT = carve(MS + 13 * KB, [96, 4, 1280], BF16)
        cqf = carve(MS + 13 * KB, [128, 2, 512], F32)
        cqn = carve(MS + 17 * KB, [128, 2, 512], BF16)
        ckf = carve(MS + 19 * KB, [128, 1, 512], F32)
        vaug = carve(MS + 23 * KB, [128, 10, 4, 65], BF16)
        krf = carve(MS + 28 * KB + 512, [96, 1024], F32)
        rt = [carve(MS + 32 * KB + 512 + i * KB, [96, 256], F32) for i in range(2)]
        ya = [carve(MS + 34 * KB + 512 + i * 512, [128, 256], BF16) for i in range(4)]
        stage = carve(MS + 36 * KB + 512, [128, 160], F32)

        if samp:
            S.dma('pool', 'ctxA0', ckvnT[:, 0:256], L['cckv_d'][l], writes=['ckvnT'])
            S.dma('pool', 'ctxA1', krT[64:96, 0:256], L['ckr_d'][l], writes=['krT'])
        gA1, gA2, gA3 = WG_ORDER.index('A1'), WG_ORDER.index('A2'), WG_ORDER.index('A3')
        wt, wk = wload(win_d[l, gA1][:, :, 0:480], 8 * 480)
        w3 = wt[:, 0:3840].rearrange("p (k j) -> p k j", k=8)
        for kk in range(2):
            S.op('dve', lambda e: e.tensor_scalar(out=smalls[:, 2 + kk:3 + kk], in0=qn[:, l, kk:kk + 1], scalar1=16.0,
                                                  scalar2=None, op0=ALU.mult), reads=['qn'], writes=['qn16'])
        S.op('dve', lambda e: e.tensor_scalar(out=smalls[:, 4:5], in0=kvn[:, l, 0:1], scalar1=float(math.sqrt(128.0)),
                                              scalar2=None, op0=ALU.mult), reads=['kvn'], writes=['kvn11'])
        for tt in tiles:
            lc = (tt - tiles[0]) * 512
            kc = kofs + lc
            rh = lambda k: hT[:, k, tl(tt)]
            for kk in range(2):
                b = self.psum('lo')
                fm_mm(ps[b][:, :], f'ps{b}', w3, wk, kk * 128, 128, rh, [f'hT{tt}'])
                S.op('act', lambda e: e.copy(out=cqf[:, kk, :], in_=ps[b][:, :]), reads=[f'ps{b}'], writes=['cqf'])
            r, rk = rms_rstd(cqf[:, :, :], 'cqf', 2, 256)
            for kk in range(2):
                S.op('dve', lambda e: e.scalar_tensor_tensor(out=cqn[:, kk, :], in0=cqf[:, kk, :], scalar=smalls[:, 2 + kk:3 + kk],
                                                             in1=r[:, :], op0=ALU.mult, op1=ALU.mult),
                     reads=['cqf', 'qn16', rk], writes=['cqn'])
            for h in range(4):
                bq = self.psum('lo')
                for kk in range(2):
                    S.op('pe', lambda e: e.matmul(ps[bq][0:96, :], lhsT=wuq[:, 0, kk, h * 96:(h + 1) * 96], rhs=cqn[:, kk, :],
                                                  start=(kk == 0), stop=(kk == 1)),
                         reads=['wuq', 'cqn'], writes=[f'ps{bq}'], inc=(kk == 1))
                S.op('act', lambda e: e.copy(out=qT[0:64, h, lc:lc + 512], in_=ps[bq][0:64, :]), reads=[f'ps{bq}'], writes=['qTA'])
                if not samp:
                    S.op('act', lambda e: e.copy(out=qT[64:96, h, lc:lc + 512], in_=ps[bq][64:96, :]), reads=[f'ps{bq}'], writes=['qTA'])
                else:
                    bs_ = self.psum('lo')
                    for kk in range(2):
                        S.op('pe', lambda e: e.matmul(ps[bs_][0:96, :], lhsT=wuq[:, 0, kk, 384 + h * 96:384 + (h + 1) * 96],
                                                      rhs=cqn[:, kk, :], start=(kk == 0), stop=(kk == 1)),
                             reads=['wuq', 'cqn'], writes=[f'ps{bs_}'], inc=(kk == 1))
                    for hh in range(2):
                        cs = slice(hh * 256, (hh + 1) * 256)
                        p0 = lc + hh * 256
                        S.op('dve', lambda e: e.tensor_tensor(out=rt[0][64:96, :], in0=ps[bq][64:96, cs], in1=cosT[64:96, p0:p0 + 256],
                                                              op=ALU.mult), reads=[f'ps{bq}', 'cosT'], writes=['rt0'])
                        S.op('dve', lambda e: e.tensor_tensor(out=rt[1][64:96, :], in0=ps[bs_][64:96, cs], in1=sinT[64:96, p0:p0 + 256],
                                                              op=ALU.mult), reads=[f'ps{bs_}', 'sinT'], writes=['rt1'])
                        S.op('pool', lambda e: e.tensor_tensor(out=qT[64:96, h, p0:p0 + 256], in0=rt[0][64:96, :], in1=rt[1][64:96, :],
                                                               op=ALU.add), reads=['rt0', 'rt1'], writes=['qTA'])
            b = self.psum('lo')
            fm_mm(ps[b][:, :], f'ps{b}', w3, wk, 256, 128, rh, [f'hT{tt}'])
            S.op('act', lambda e: e.copy(out=ckf[:, 0, :], in_=ps[b][:, :]), reads=[f'ps{b}'], writes=['ckf'])
            r, rk = rms_rstd(ckf[:, :, :], 'ckf', 1, 128)
            S.op('dve', lambda e: e.scalar_tensor_tensor(out=ckvnT[:, kc:kc + 512], in0=ckf[:, 0, :], scalar=smalls[:, 4:5],
                                                         in1=r[:, :], op0=ALU.mult, op1=ALU.mult),
                 reads=['ckf', 'kvn11', rk], writes=['ckvnT'])
            b = self.psum('lo')
            fm_mm(ps[b][0:96, :], f'ps{b}', w3, wk, 384, 96, rh, [f'hT{tt}'])
            if not samp:
                S.op('act', lambda e: e.copy(out=krT[64:96, kc:kc + 512], in_=ps[b][64:96, :]), reads=[f'ps{b}'], writes=['krT'])
            else:
                S.op('act', lambda e: e.copy(out=krf[64:96, lc:lc + 512], in_=ps[b][64:96, :]), reads=[f'ps{b}'], writes=['krf'])
        if samp:
            wt, wk = wload(win_d[l, gA2][:, :, 0:96], 8 * 96)
            w3 = wt[:, 0:768].rearrange("p (k j) -> p k j", k=8)
            for tt in tiles:
                lc = (tt - tiles[0]) * 512
                b = self.psum('lo')
                fm_mm(ps[b][0:96, :], f'ps{b}', w3, wk, 0, 96, lambda k: hT[:, k, tl(tt)], [f'hT{tt}'])
                for hh in range(2):
                    p0 = lc + hh * 256
                    S.op('pool', lambda e: e.tensor_tensor(out=rt[0][64:96, :], in0=krf[64:96, p0:p0 + 256], in1=cosT[64:96, p0:p0 + 256],
                                                           op=ALU.mult), reads=['krf', 'cosT'], writes=['rt0'])
                    S.op('dve', lambda e: e.tensor_tensor(out=rt[1][64:96, :], in0=ps[b][64:96, hh * 256:(hh + 1) * 256],
                                                          in1=sinT[64:96, p0:p0 + 256], op=ALU.mult),
                         reads=[f'ps{b}', 'sinT'], writes=['rt1'])
                    S.op('pool', lambda e: e.tensor_tensor(out=krT[64:96, 256 + p0:256 + p0 + 256], in0=rt[0][64:96, :],
                                                           in1=rt[1][64:96, :], op=ALU.add), reads=['rt0', 'rt1'], writes=['krT'])
        else:
            wt, wk = wload(win_d[l, gA3][:, :, 0:160], 8 * 160)
            w3 = wt[:, 0:1280].rearrange("p (k j) -> p k j", k=8)
            for blk in range(4):
                b = self.psum('lo')
                tm_mm(ps[b][:, 0:160], f'ps{b}', w3, wk, 0, 160, blk * 128)
                S.op('act', lambda e: e.activation(out=stage[:, 0:128], in_=ps[b][:, 0:128], func=AF.Square, accum_out=smalls[:, 5:6]),
                     reads=[f'ps{b}'], writes=['stageA', 'ssA'])
                self.rsqrt(smalls[:, 6:7], smalls[:, 5:6], float(128 * EPS), ['ssA'], ['rsA'])
                S.op('dve', lambda e: e.tensor_scalar(out=stage[:, 0:128], in0=ps[b][:, 0:128], scalar1=smalls[:, 6:7],
                                                      scalar2=float(math.sqrt(128.0)), op0=ALU.mult, op1=ALU.mult),
                     reads=[f'ps{b}', 'rsA', 'stageA'], writes=['stageA'])
                S.op('dve', lambda e: e.tensor_tensor(out=stage[:, 0:128], in0=stage[:, 0:128], in1=kvnB[:, 0:128],
                                                      op=ALU.mult), reads=['stageA', 'kvnB'], writes=['stageA'])
                S.op('act', lambda e: e.copy(out=stage[:, 128:160], in_=ps[b][:, 128:160]), reads=[f'ps{b}', 'stageA'], writes=['stageA'])
                sq_, bl = blk // 2, blk % 2
                self.store('o_ckv', L['o_ckv'][sq_, l, bl * 128:(bl + 1) * 128, :], stage[:, 0:128], ['stageA'])
                self.store('o_kr', L['o_kr'][sq_, l, bl * 128:(bl + 1) * 128, :], stage[:, 128:160], ['stageA'])
        S.barrier()
        S.op('pool', lambda e: e.memset(vaug[:, :, :, 64:65], 1.0), writes=['vaugA'])
        for c0 in range(0, nk, 512):
            c1 = min(nk, c0 + 512)
            n = c1 - c0
            for h in range(4):
                b = self.psum('lo')
                S.op('pe', lambda e: e.matmul(ps[b][0:64, 0:n], lhsT=wukv[:, 0, h * 64:(h + 1) * 64], rhs=ckvnT[:, c0:c1],
                                              start=True, stop=True), reads=['wukv', 'ckvnT'], writes=[f'ps{b}'])
                S.op('act', lambda e: e.copy(out=KT[0:64, h, c0:c1], in_=ps[b][0:64, 0:n]), reads=[f'ps{b}'], writes=['KTA'])
                S.op('pool', lambda e: e.tensor_copy(out=KT[64:96, h, c0:c1], in_=krT[64:96, c0:c1]), reads=['krT'], writes=['KTA'])
        for kb in range(nkb):
            b = self.psum('lo')
            S.op('pe', lambda e: e.matmul(ps[b][:, 0:256], lhsT=ckvnT[:, kb * 128:(kb + 1) * 128], rhs=wukv[:, 0, 256:512],
                                          start=True, stop=True), reads=['wukv', 'ckvnT'], writes=[f'ps{b}'])
            S.op('dve', lambda e: e.tensor_copy(out=vaug[:, kb, :, 0:64], in_=ps[b][:, 0:256].rearrange("p (h e) -> p h e", h=4)),
                 reads=[f'ps{b}'], writes=['vaugA'])
        yaall = [carve(MS + 34 * KB + 512 + i * 2 * KB, [128, 4, 256], BF16) for i in range(2)]
        yakey = {id(yaall[0]): 'yaA0', id(yaall[1]): 'yaA1'}
        deferred = []
        qi = 0

        def make_fin(hs, banks, nb, ya_):
            def fin():
                for i, h in enumerate(hs):
                    accb = banks[i]
                    av = ps[accb][:, 0:nb * 65].rearrange("p (b e) -> p b e", e=65)
                    S.op('dve', lambda e: e.reciprocal(out=smalls[:, 8 + 4 * i:8 + 4 * i + nb], in_=av[:, :, 64]),
                         reads=[f'ps{accb}'], writes=[f'rdA{i}'])
                    S.op('dve', lambda e: e.tensor_tensor(out=ya_[:, 0:nb, h * 64:(h + 1) * 64], in0=av[:, :, 0:64],
                                                          in1=smalls[:, 8 + 4 * i:8 + 4 * i + nb].unsqueeze(2).to_broadcast([128, nb, 64]),
                                                          op=ALU.mult),
                         reads=[f'ps{accb}', f'rdA{i}'], writes=[yakey[id(ya_)]])
            return fin

        for (s0, Ls) in G['seqs']:
            kb0 = 0 if samp else s0 // 128
            nkc = (Ls + kofs) // 128
            Nt = min(512, Ls)
            nb = Nt // 128
            for tq in range(Ls // Nt):
                q0 = s0 + tq * Nt
                ya_ = yaall[qi % 2]
                qi += 1
                for hp in range(2):
                    banks = (4, 5) if hp == 0 else (6, 7)
                    hs = (2 * hp, 2 * hp + 1)
                    self.attn_core(L, [((lambda h=h: (qT[0:96, h, q0:q0 + Nt], 'qTA')),
                                        (lambda sc, h=h: (KT[0:96, h, (kb0 + sc) * 128:(kb0 + sc + 1) * 128], 'KTA')),
                                        (lambda sc, h=h: (vaug[:, kb0 + sc, h, :], 'vaugA')), banks[i]) for i, h in enumerate(hs)],
                                   nkc, Nt, MLA_SCALE)
                    self.p0_hook(MS + 40 * KB)
                    for f in deferred:
                        f()
                    deferred = [make_fin(hs, banks, nb, ya_)]
                    if hp == 1:
                        def tofm(ya_=ya_, nb=nb, q0=q0):
                            for tb in range(nb):
                                to_fm(ya_[:, tb, :], yakey[id(ya_)], 0, (t0 + q0) // 128 + tb)
                        deferred.append(tofm)
        for f in deferred:
            f()

    def mixer_D(self, l, L, samp):
        S = self.S
        g_ = lambda n: L[n]
        ps, hT, carve, wload, fm_mm, tm_mm, win_d, tl, MS, KB = (g_('ps'), g_('hT'), g_('carve'), g_('wload'), g_('fm_mm'),
                                                                  g_('tm_mm'), g_('win_d'), g_('tl'), g_('MS'), g_('KB'))
        cosT, sinT, smalls, lamB, snB, to_fm = g_('cosT'), g_('sinT'), g_('smalls'), g_('lamB'), g_('snB'), g_('to_fm')
        G = self.geo(samp)
        t0, tiles, kofs, nkb = G['t0'], G['tiles'], G['kofs'], G['nkb']
        self.pt_i = 0
        lam_init = 0.8 - 0.6 * math.exp(-0.3 * l)
        QT = carve(MS, [96, 3, 1024], BF16)
        KT = carve(MS + 6 * KB, [96, 3, 1280], BF16)
        vaug = carve(MS + 13 * KB + 512, [128, 10, 4, 65], BF16)
        rt = [carve(MS + 19 * KB + i * 2 * KB, [96, 512], F32) for i in range(2)]
        stage = carve(MS + 23 * KB, [128, 512], F32)
        dtmp = [carve(MS + 25 * KB + i * 256, [128, 64], F32) for i in range(3)]
        junk = carve(MS + 25 * KB + 768, [128, 64], F32)
        lt = carve(MS + 26 * KB, [128, 32], F32)
        yd = [carve(MS + 27 * KB + i * 512, [128, 256], BF16) for i in range(4)]

        for i in range(2):
            S.op('dve', lambda e: e.tensor_tensor(out=lt, in0=lamB[:, 64 * i:64 * i + 32],
                                                  in1=lamB[:, 64 * i + 32:64 * i + 64], op=ALU.mult),
                 reads=['lamB'], writes=['ltD'])
            S.op('dve', lambda e: e.reduce_sum(out=smalls[:, 30 + i:31 + i], in_=lt, axis=AX.X), reads=['ltD'], writes=['lsD'])
        S.op('act', lambda e: e.activation(out=smalls[:, 32:34], in_=smalls[:, 30:32], func=AF.Exp), reads=['lsD'], writes=['leD'])
        S.op('dve', lambda e: e.tensor_tensor(out=smalls[:, 34:35], in0=smalls[:, 33:34], in1=smalls[:, 32:33], op=ALU.subtract),
             reads=['leD'], writes=['nlD'])
        S.op('dve', lambda e: e.tensor_scalar(out=smalls[:, 34:35], in0=smalls[:, 34:35], scalar1=float(-lam_init), scalar2=None,
                                              op0=ALU.add), reads=['nlD'], writes=['nlD'])
        S.barrier()
        self.cut('Dlam')
        S.op('pool', lambda e: e.memset(vaug[:, :, :, 64:65], 1.0), writes=['vaugD'])
        if samp:
            S.dma('pool', 'ctxD0', KT[:, :, 0:256], L['cdk_d'][l], writes=['KTD'])
            S.dma('pool', 'ctxD1', vaug[:, 0:2, :, 0:64], L['cdv_d'][:, l], writes=['vaugD'])
        plan = [('D1', 0, QT, 0, 96), ('D1', 256, QT, 1, 96), ('D2', 0, QT, 2, 64),
                ('D2', 256, KT, 0, 96), ('D3', 0, KT, 1, 96), ('D3', 256, KT, 2, 64)]
        cur = None
        for (gn, c0, dst, dc, M) in plan:
            if gn != cur:
                wt, wk = wload(win_d[l, WG_ORDER.index(gn)][:, :, 0:512], 4096)
                w3 = wt[:, 0:4096].rearrange("p (k j) -> p k j", k=8)
                cur = gn
            isK = dst is KT
            dkey = 'KTD' if isK else 'QTD'
            for tt in tiles:
                lc = (tt - tiles[0]) * 512
                b = self.psum('lo')
                fm_mm(ps[b][0:M, :], f'ps{b}', w3, wk, c0, M, lambda k: hT[:, k, tl(tt)], [f'hT{tt}'])
                d0 = (kofs + lc) if isK else lc
                if not samp:
                    S.op('act', lambda e: e.copy(out=dst[0:M, dc, d0:d0 + 512], in_=ps[b][0:M, :]), reads=[f'ps{b}'], writes=[dkey])
                else:
                    b2 = self.psum('lo')
                    fm_mm(ps[b2][0:M, :], f'ps{b2}', w3, wk, c0 + 128, M, lambda k: hT[:, k, tl(tt)], [f'hT{tt}'])
                    S.op('dve', lambda e: e.tensor_tensor(out=rt[0][0:M, :], in0=ps[b][0:M, :], in1=cosT[0:M, lc:lc + 512], op=ALU.mult),
                         reads=[f'ps{b}', 'cosT'], writes=['rtD0'])
                    S.op('dve', lambda e: e.tensor_tensor(out=rt[1][0:M, :], in0=ps[b2][0:M, :], in1=sinT[0:M, lc:lc + 512], op=ALU.mult),
                         reads=[f'ps{b2}', 'sinT'], writes=['rtD1'])
                    S.op('pool', lambda e: e.tensor_tensor(out=dst[0:M, dc, d0:d0 + 512], in0=rt[0][0:M, :], in1=rt[1][0:M, :], op=ALU.add),
                         reads=['rtD0', 'rtD1'], writes=[dkey])
        S.barrier()
        self.cut('Dz')
        wt, wk = wload(win_d[l, WG_ORDER.index('D4')][:, :, 0:512], 4096)
        w3 = wt[:, 0:4096].rearrange("p (k j) -> p k j", k=8)
        for bi in range(G['nt'] // 128):
            blk = t0 // 128 + bi
            N = 256 if samp else 512
            b = self.psum('lo')
            tm_mm(ps[b][:, 0:N], f'ps{b}', w3, wk, 0, N, blk * 128)
            kb = kofs // 128 + bi
            S.op('dve', lambda e: e.tensor_copy(out=vaug[:, kb, :, 0:64], in_=ps[b][:, 0:256].rearrange("p (h e) -> p h e", h=4)),
                 reads=[f'ps{b}'], writes=['vaugD'])
            if not samp:
                S.op('act', lambda e: e.copy(out=stage[:, :], in_=ps[b][:, :]), reads=[f'ps{b}'], writes=['stageD'])
                sq_, bl = blk // 2, blk % 2
                self.store('o_dv', L['o_dv'][sq_, l, bl * 128:(bl + 1) * 128, :], stage[:, 0:256], ['stageD'])
                self.store('o_dk', L['o_dk'][sq_, l, bl * 128:(bl + 1) * 128, :], stage[:, 256:512], ['stageD'])
        S.barrier()
        self.cut('Dtm')
        fin_scale = 8.0 * (1.0 - lam_init)
        ydall = [carve(MS + 27 * KB + i * 2 * KB, [128, 4, 256], BF16) for i in range(2)]
        dA = carve(MS + 31 * KB, [128, 4, 64], F32)
        dB = carve(MS + 32 * KB, [128, 4, 64], F32)
        rr = smalls[:, 36:44].rearrange("p (j b) -> p j b", j=2)
        deferred = []
        qi = 0

        def make_fin(h, accb, nb, yd_):
            def fin():
                av = [ps[accb[j]][:, 0:nb * 65].rearrange("p (b e) -> p b e", e=65) for j in range(2)]
                for j in range(2):
                    S.op('dve', lambda e: e.reciprocal(out=rr[:, j, 0:nb], in_=av[j][:, :, 64]),
                         reads=[f'ps{accb[j]}'], writes=[f'rdD{j}'])
                S.op('dve', lambda e: e.tensor_scalar(out=rr[:, 1, 0:nb], in0=rr[:, 1, 0:nb], scalar1=smalls[:, 34:35], scalar2=None,
                                                      op0=ALU.mult), reads=['rdD1', 'nlD'], writes=['rdD1'])
                S.op('dve', lambda e: e.tensor_tensor(out=dA[:, 0:nb, :], in0=av[0][:, :, 0:64],
                                                      in1=rr[:, 0, 0:nb].unsqueeze(2).to_broadcast([128, nb, 64]), op=ALU.mult),
                     reads=[f'ps{accb[0]}', 'rdD0'], writes=['dA'])
                S.op('dve', lambda e: e.tensor_tensor(out=dB[:, 0:nb, :], in0=av[1][:, :, 0:64],
                                                      in1=rr[:, 1, 0:nb].unsqueeze(2).to_broadcast([128, nb, 64]), op=ALU.mult),
                     reads=[f'ps{accb[1]}', 'rdD1'], writes=['dB'])
                S.op('dve', lambda e: e.tensor_tensor(out=dA[:, 0:nb, :], in0=dA[:, 0:nb, :], in1=dB[:, 0:nb, :], op=ALU.add),
                     reads=['dA', 'dB'], writes=['dA'])
                S.op('dve', lambda e: e.tensor_tensor(out=dB[:, 0:nb, :], in0=dA[:, 0:nb, :], in1=dA[:, 0:nb, :], op=ALU.mult),
                     reads=['dA'], writes=['dB'])
                S.op('dve', lambda e: e.reduce_sum(out=smalls[:, 44:44 + nb], in_=dB[:, 0:nb, :], axis=AX.X), reads=['dB'], writes=['ssD'])
                self.rsqrt(smalls[:, 44:44 + nb], smalls[:, 44:44 + nb], float(64 * EPS), ['ssD'], ['ssD'])
                S.op('dve', lambda e: e.tensor_tensor(out=dA[:, 0:nb, :], in0=dA[:, 0:nb, :],
                                                      in1=smalls[:, 44:44 + nb].unsqueeze(2).to_broadcast([128, nb, 64]), op=ALU.mult),
                     reads=['dA', 'ssD'], writes=['dA'])
                S.op('dve', lambda e: e.scalar_tensor_tensor(out=yd_[:, 0:nb, h * 64:(h + 1) * 64], in0=dA[:, 0:nb, :], scalar=float(fin_scale),
                                                             in1=snB[:, 0:64].unsqueeze(1).to_broadcast([128, nb, 64]),
                                                             op0=ALU.mult, op1=ALU.mult),
                     reads=['dA', 'snB'], writes=[yd_.__dict__.get('k', 'ydD')] if False else [ydkey[id(yd_)]])
            return fin

        ydkey = {id(ydall[0]): 'ydD0', id(ydall[1]): 'ydD1'}
        for (s0, Ls) in G['seqs']:
            kb0 = 0 if samp else s0 // 128
            nkc = (Ls + kofs) // 128
            Nt = min(512, Ls)
            nb = Nt // 128
            for tq in range(Ls // Nt):
                q0 = s0 + tq * Nt
                yd_ = ydall[qi % 2]
                qi += 1
                for h in range(4):
                    accb = [4, 5] if h % 2 == 0 else [6, 7]
                    strs = []
                    for j in range(2):
                        p = 2 * h + j
                        c, pb = p // 3, (p % 3) * 32
                        strs.append(((lambda c=c, pb=pb: (QT[pb:pb + 32, c, q0:q0 + Nt], 'QTD')),
                                     (lambda sc, c=c, pb=pb: (KT[pb:pb + 32, c, (kb0 + sc) * 128:(kb0 + sc + 1) * 128], 'KTD')),
                                     (lambda sc, h=h: (vaug[:, kb0 + sc, h, :], 'vaugD')), accb[j]))
                    self.attn_core(L, strs, nkc, Nt, DIFF_SCALE)
                    self.p0_hook(MS + 40 * KB)
                    for f in deferred:
                        f()
                    deferred = [make_fin(h, accb, nb, yd_)]
                    if h == 3:
                        def tofm(yd_=yd_, nb=nb, q0=q0):
                            for tb in range(nb):
                                to_fm(yd_[:, tb, :], ydkey[id(yd_)], 3, (t0 + q0) // 128 + tb)
                        deferred.append(tofm)
        for f in deferred:
            f()

    def mixer_C(self, l, L, samp):
        S = self.S
        g_ = lambda n: L[n]
        ps, hT, carve, wload, fm_mm, tm_mm, win_d, tl, MS, KB = (g_('ps'), g_('hT'), g_('carve'), g_('wload'), g_('fm_mm'),
                                                                  g_('tm_mm'), g_('win_d'), g_('tl'), g_('MS'), g_('KB'))
        smalls, hnB, gbT, m0T, m0B, C0, sel, identf, mle, mge, to_fm, zero1 = (
            g_('smalls'), g_('hnB'), g_('gbT'), g_('m0T'), g_('m0B'), g_('C0'), g_('sel'), g_('identf'), g_('mle'), g_('mge'),
            g_('to_fm'), g_('zero1'))
        G = self.geo(samp)
        t0, tiles = G['t0'], G['tiles']
        nbp = G['nt'] // 128
        qT = carve(MS, [128, 2, 1024], BF16)
        kT = carve(MS + 4 * KB, [128, 2, 1024], BF16)
        vaug = carve(MS + 8 * KB, [128, 8, 4, 65], BF16)
        sgo = carve(MS + 12 * KB + 512, [128, 8, 256], BF16)
        giT = carve(MS + 16 * KB + 512, [36, 1024], F32)
        lfT = carve(MS + 20 * KB + 512, [36, 1024], F32)
        BT = carve(MS + 24 * KB + 512, [36, 1024], F32)
        GT = carve(MS + 28 * KB + 512, [36, 1024], F32)
        hb = carve(MS + 32 * KB + 512, [128, 8, 256], F32)
        kTM = carve(MS + 40 * KB + 512, [128, 4, 256], BF16)
        negG = [carve(MS + 20 * KB + 512, [128, 1024], F32), carve(MS + 16 * KB + 512, [128, 1024], F32)]
        sq = g_('sq')
        sqb = sq[:, :, :].rearrange("p a b -> p (a b)")
        Wt = [sqb[:, i * 512:(i + 1) * 512] for i in range(4)]
        Dt = [sqb[:, 2048 + i * 1024:2048 + (i + 1) * 1024].bitcast(F32) for i in range(2)]
        rstd = g_('rstd')
        gTM = rstd[0][:, 0:288].rearrange("p (b r) -> p b r", r=36)
        emt = rstd[1][:, 0:288].rearrange("p (b r) -> p b r", r=36)

        S.op('pool', lambda e: e.memset(vaug[:, :, :, 64:65], 1.0), writes=['vaugC'])
        S.op('pool', lambda e: e.memset(BT[:, :], 0.0), writes=['BT'])
        wt, wk = wload(win_d[l, WG_ORDER.index('C1')][:, :, 0:512], 4096)
        w3 = wt[:, 0:4096].rearrange("p (k j) -> p k j", k=8)
        for ci in range(4):
            dst = qT if ci < 2 else kT
            for tt in tiles:
                lc = (tt - tiles[0]) * 512
                b = self.psum('lo')
                fm_mm(ps[b][:, :], f'ps{b}', w3, wk, ci * 128, 128, lambda k: hT[:, k, tl(tt)], [f'hT{tt}'])
                S.op('act', lambda e: e.activation(out=dst[:, ci % 2, lc:lc + 512], in_=ps[b][:, :], func=AF.Copy,
                                                   scale=(1.0 if ci < 2 else 0.125)),
                     reads=[f'ps{b}'], writes=['qTC' if ci < 2 else 'kTC'])
        wt, wk = wload(win_d[l, WG_ORDER.index('C2')][:, :, 0:128], 1024)
        w3 = wt[:, 0:1024].rearrange("p (k j) -> p k j", k=8)
        for tt in tiles:
            lc = (tt - tiles[0]) * 512
            cs = slice(lc, lc + 512)
            b = self.psum('lo')
            fm_mm(ps[b][0:36, :], f'ps{b}', w3, wk, 0, 36, lambda k: hT[:, k, tl(tt)], [f'hT{tt}'])
            S.op('act', lambda e: e.activation(out=giT[:, cs], in_=ps[b][0:36, :], func=AF.Identity, bias=gbT[:, l, 0:1], scale=1.0),
                 reads=[f'ps{b}', 'gbT'], writes=['giT'])
            b = self.psum('lo')
            fm_mm(ps[b][0:36, :], f'ps{b}', w3, wk, 64, 36, lambda k: hT[:, k, tl(tt)], [f'hT{tt}'])
            S.op('dve', lambda e: e.tensor_scalar(out=lfT[:, cs], in0=ps[b][0:36, :], scalar1=gbT[:, l, 1:2], scalar2=-1.0,
                                                  op0=ALU.add, op1=ALU.mult), reads=[f'ps{b}', 'gbT'], writes=['lfT'])
            S.op('act', lambda e: e.activation(out=lfT[:, cs], in_=lfT[:, cs], func=AF.Exp), reads=['lfT'], writes=['lfT'])
            S.op('act', lambda e: e.activation(out=lfT[:, cs], in_=lfT[:, cs], func=AF.Ln, bias=self.cbias(1.0, lfT[:, cs]), scale=1.0),
                 reads=['lfT', 'cbias'], writes=['lfT'])
            S.op('dve', lambda e: e.tensor_scalar(out=lfT[:, cs], in0=lfT[:, cs], scalar1=-1.0, scalar2=None, op0=ALU.mult),
                 reads=['lfT'], writes=['lfT'])
        wt, wk = wload(win_d[l, WG_ORDER.index('C3')][:, :, 0:512], 4096)
        w3 = wt[:, 0:4096].rearrange("p (k j) -> p k j", k=8)
        for bi in range(nbp):
            b = self.psum('lo')
            tm_mm(ps[b][:, :], f'ps{b}', w3, wk, 0, 512, t0 + bi * 128)
            S.op('dve', lambda e: e.tensor_copy(out=vaug[:, bi, :, 0:64], in_=ps[b][:, 0:256].rearrange("p (h e) -> p h e", h=4)),
                 reads=[f'ps{b}'], writes=['vaugC'])
            S.op('act', lambda e: e.activation(out=sgo[:, bi, :], in_=ps[b][:, 256:512], func=AF.Sigmoid),
                 reads=[f'ps{b}'], writes=['sgo'])
        if not samp:
            wt, wk = wload(win_d[l, WG_ORDER.index('C4')][:, :, 0:256], 2048)
            w3 = wt[:, 0:2048].rearrange("p (k j) -> p k j", k=8)
            for bi in range(4):
                b = self.psum('lo')
                tm_mm(ps[b][:, 0:256], f'ps{b}', w3, wk, 0, 256, bi * 128)
                S.op('act', lambda e: e.activation(out=kTM[:, bi, :], in_=ps[b][:, 0:256], func=AF.Copy, scale=0.125),
                     reads=[f'ps{b}'], writes=['kTM'])

        for si, (s0, Ls) in enumerate(G['seqs']):
            nb = Ls // 128
            b0 = s0 // 128
            sl = slice(s0, s0 + Ls)
            S.op('pool', lambda e: e.memset(GT[:, 0:Ls], 1.0), writes=['GT'])
            S.op('dve', lambda e: e.tensor_tensor_scan(out=BT[0:4, 0:Ls], data0=GT[0:4, 0:Ls], data1=lfT[0:4, sl], initial=0.0,
                                                       op0=ALU.mult, op1=ALU.add), reads=['GT', 'lfT'], writes=['BT'])
            S.op('dve', lambda e: e.tensor_tensor_scan(out=BT[32:36, 0:Ls][:, ::-1], data0=GT[32:36, 0:Ls],
                                                       data1=lfT[32:36, sl][:, ::-1], initial=0.0,
                                                       op0=ALU.mult, op1=ALU.add), reads=['GT', 'lfT'], writes=['BT'])
            for r0 in (0, 32):
                S.op('dve', lambda e: e.tensor_tensor(out=giT[r0:r0 + 4, sl], in0=giT[r0:r0 + 4, sl], in1=BT[r0:r0 + 4, 0:Ls],
                                                      op=ALU.subtract), reads=['giT', 'BT'], writes=['giT'])
            init_f = m0T[0:4, l:l + 1] if samp else zero1[0:4, 0:1]
            init_b = m0T[32:36, l:l + 1] if samp else zero1[32:36, 0:1]
            S.op('dve', lambda e: e.tensor_tensor_scan(out=GT[0:4, 0:Ls], data0=giT[0:4, sl], data1=giT[0:4, sl], initial=init_f,
                                                       op0=ALU.max, op1=ALU.max), reads=['giT', 'm0T', 'zero1'], writes=['GT'])
            S.op('dve', lambda e: e.tensor_tensor_scan(out=GT[32:36, 0:Ls][:, ::-1], data0=giT[32:36, sl][:, ::-1],
                                                       data1=giT[32:36, sl][:, ::-1], initial=init_b,
                                                       op0=ALU.max, op1=ALU.max), reads=['giT', 'm0T', 'zero1'], writes=['GT'])
            for r0 in (0, 32):
                S.op('dve', lambda e: e.tensor_tensor(out=BT[r0:r0 + 4, 0:Ls], in0=BT[r0:r0 + 4, 0:Ls], in1=GT[r0:r0 + 4, 0:Ls],
                                                      op=ALU.add), reads=['BT', 'GT'], writes=['BT'])
                S.op('dve', lambda e: e.tensor_scalar(out=GT[r0:r0 + 4, 0:Ls], in0=GT[r0:r0 + 4, 0:Ls], scalar1=-1.0, scalar2=None,
                                                      op0=ALU.mult), reads=['GT'], writes=['GT'])
            if not samp:
                S.dma('sp', 'o_m', L['o_m'][si, l, 0, :].rearrange("(h o) -> h o", o=1), BT[0:4, Ls - 1:Ls], reads=['BT'],
                      writes=['dram_o_m'])
                S.dma('sp', 'o_m', L['o_m'][si, l, 1, :].rearrange("(h o) -> h o", o=1), BT[32:36, 0:1], reads=['BT'],
                      writes=['dram_o_m'])
                if 'o_m' not in self.out_keys:
                    self.out_keys.append('o_m')
            for bi in range(nb):
                b = self.psum('lo')
                S.op('pe', lambda e: e.matmul(ps[b][:, 0:36], lhsT=giT[0:36, s0 + bi * 128:s0 + (bi + 1) * 128], rhs=identf[:, :],
                                              start=True, stop=True), reads=['giT', 'identf'], writes=[f'ps{b}'], inc=False)
                S.op('pe', lambda e: e.matmul(ps[b][:, 64:100], lhsT=BT[0:36, bi * 128:(bi + 1) * 128], rhs=identf[:, :],
                                              start=True, stop=True), reads=['BT', 'identf'], writes=[f'ps{b}'])
                S.op('dve', lambda e: e.tensor_copy(out=gTM[:, bi, :], in_=ps[b][:, 0:36]), reads=[f'ps{b}'], writes=['gTM'])
                S.op('act', lambda e: e.activation(out=emt[:, bi, :], in_=ps[b][:, 64:100], func=AF.Exp, scale=-1.0),
                     reads=[f'ps{b}'], writes=['emt'])
            S.barrier()
            for h in range(4):
                kk, pb = h // 2, (h % 2) * 64
                rf, rb = h, 32 + h
                for d_ in (0, 1):
                    r0 = 0 if d_ == 0 else 32
                    for c0 in range(0, Ls, 512):
                        n = min(512, Ls - c0)
                        b = self.psum('lo')
                        S.op('pe', lambda e: e.matmul(ps[b][:, 0:n], lhsT=sel[r0:r0 + 4, h, :], rhs=GT[r0:r0 + 4, c0:c0 + n],
                                                      start=True, stop=True), reads=['sel', 'GT'], writes=[f'ps{b}'])
                        S.op('act', lambda e: e.copy(out=negG[d_][:, c0:c0 + n], in_=ps[b][:, 0:n]), reads=[f'ps{b}'],
                             writes=[f'negG{d_}'])
                Nt = min(512, Ls)
                ntb = Nt // 128
                for tq in range(Ls // Nt):
                    q0l = tq * Nt
                    tb0 = q0l // 128
                    accf, accb_ = (4, 5) if h % 2 == 0 else (6, 7)
                    af = ps[accf][:, 0:ntb * 65].rearrange("p (b e) -> p b e", e=65)
                    ab = ps[accb_][:, 0:ntb * 65].rearrange("p (b e) -> p b e", e=65)
                    bank_started = {0: False, 1: False}
                    if samp:
                        for d_, acc, an in ((0, af, accf), (1, ab, accb_)):
                            r = d_ * 4 + h
                            wq = Dt[d_][pb:pb + 64, 0:Nt]
                            S.op('act', lambda e: e.activation(out=wq, in_=negG[d_][pb:pb + 64, q0l:q0l + Nt], func=AF.Exp,
                                                               bias=m0B[pb:pb + 64, l * 8 + r:l * 8 + r + 1], scale=1.0),
                                 reads=[f'negG{d_}', 'm0B'], writes=[f'Dt{d_}'])
                            qs = Wt[d_][pb:pb + 64, 0:Nt]
                            S.op('dve', lambda e: e.tensor_tensor(out=qs, in0=qT[pb:pb + 64, kk, s0 + q0l:s0 + q0l + Nt], in1=wq,
                                                                  op=ALU.mult), reads=['qTC', f'Dt{d_}'], writes=[f'Wt{d_}'])
                            for i in range(ntb):
                                S.op('pe', lambda e: e.matmul(acc[:, i, :], lhsT=qs[:, i * 128:(i + 1) * 128],
                                                              rhs=C0[pb:pb + 64, l, d_, kk, :], start=(not bank_started[d_]), stop=False,
                                                              skip_group_check=True),
                                     reads=[f'Wt{d_}', 'C0'], writes=[f'ps{an}'], inc=(i == ntb - 1))
                                bank_started[d_] = True
                    def stage1(j):
                        sb_ = self.psum('lo')
                        S.op('pe', lambda e: e.matmul(ps[sb_][:, 0:Nt], lhsT=kT[pb:pb + 64, kk, s0 + j * 128:s0 + (j + 1) * 128],
                                                      rhs=qT[pb:pb + 64, kk, s0 + q0l:s0 + q0l + Nt], start=True, stop=True),
                             reads=['kTC', 'qTC'], writes=[f'ps{sb_}'])
                        jl = j - tb0
                        lo = max(jl, 0)
                        wf = wb_ = None
                        if lo < ntb:
                            c0_, c1_ = lo * 128, Nt
                            self.wt_i = (getattr(self, 'wt_i', 0) + 1) % 2
                            wf = self.wt_i
                            S.op('act', lambda e: e.activation(out=Dt[0][:, c0_:c1_], in_=negG[0][:, q0l + c0_:q0l + c1_], func=AF.Exp,
                                                               bias=gTM[:, j, rf:rf + 1], scale=1.0),
                                 reads=['negG0', 'gTM'], writes=['Dt0'])
                            S.op('dve', lambda e: e.tensor_tensor(out=Wt[wf][:, c0_:c1_], in0=ps[sb_][:, c0_:c1_], in1=Dt[0][:, c0_:c1_],
                                                                  op=ALU.mult), reads=[f'ps{sb_}', 'Dt0'], writes=[f'Wt{wf}'])
                            if jl >= 0:
                                S.op('pool', lambda e: e.tensor_tensor(out=Wt[wf][:, c0_:c0_ + 128], in0=Wt[wf][:, c0_:c0_ + 128],
                                                                       in1=mle[:, :], op=ALU.mult), reads=[f'Wt{wf}', 'mle'],
                                     writes=[f'Wt{wf}'])
                        hi = min(jl, ntb - 1)
                        if hi >= 0:
                            c0_, c1_ = 0, (hi + 1) * 128
                            self.wt_j = (getattr(self, 'wt_j', 0) + 1) % 2
                            wb_ = 2 + self.wt_j
                            S.op('act', lambda e: e.activation(out=Dt[1][:, c0_:c1_], in_=negG[1][:, q0l + c0_:q0l + c1_], func=AF.Exp,
                                                               bias=gTM[:, j, rb:rb + 1], scale=1.0),
                                 reads=['negG1', 'gTM'], writes=['Dt1'])
                            S.op('dve', lambda e: e.tensor_tensor(out=Wt[wb_][:, c0_:c1_], in0=ps[sb_][:, c0_:c1_], in1=Dt[1][:, c0_:c1_],
                                                                  op=ALU.mult), reads=[f'ps{sb_}', 'Dt1'], writes=[f'Wt{wb_}'])
                            if jl <= ntb - 1:
                                S.op('pool', lambda e: e.tensor_tensor(out=Wt[wb_][:, hi * 128:(hi + 1) * 128],
                                                                       in0=Wt[wb_][:, hi * 128:(hi + 1) * 128], in1=mge[:, :], op=ALU.mult),
                                     reads=[f'Wt{wb_}', 'mge'], writes=[f'Wt{wb_}'])
                        return (lo, wf, hi, wb_)

                    def stage2(j, info):
                        lo, wf, hi, wb_ = info
                        if wf is not None:
                            for i in range(lo, ntb):
                                S.op('pe', lambda e: e.matmul(af[:, i, :], lhsT=Wt[wf][:, i * 128:(i + 1) * 128],
                                                              rhs=vaug[:, b0 + j, h, :], start=(not bank_started[0]), stop=False,
                                                              skip_group_check=True),
                                     reads=[f'Wt{wf}', 'vaugC'], writes=[f'ps{accf}'], inc=(i == ntb - 1))
                                bank_started[0] = True
                        if wb_ is not None:
                            for i in range(0, hi + 1):
                                S.op('pe', lambda e: e.matmul(ab[:, i, :], lhsT=Wt[wb_][:, i * 128:(i + 1) * 128],
                                                              rhs=vaug[:, b0 + j, h, :], start=(not bank_started[1]), stop=False,
                                                              skip_group_check=True),
                                     reads=[f'Wt{wb_}', 'vaugC'], writes=[f'ps{accb_}'], inc=(i == hi))
                                bank_started[1] = True

                    cur = stage1(0)
                    for j in range(nb):
                        nxt = stage1(j + 1) if j + 1 < nb else None
                        stage2(j, cur)
                        cur = nxt
                    dsc = smalls[:, 16:24].rearrange("p (d b) -> p d b", d=2)
                    dtm = carve(MS + 45 * KB, [128, 4, 64], F32)
                    for d_, r, acc, an in ((0, rf, af, accf), (1, rb, ab, accb_)):
                        S.op('dve', lambda e: e.tensor_scalar(out=dsc[:, d_, 0:ntb], in0=acc[:, :, 64], scalar1=-1.0, scalar2=None,
                                                              op0=ALU.mult), reads=[f'ps{an}'], writes=[f'dnC{d_}'])
                        S.op('dve', lambda e: e.tensor_tensor(out=dsc[:, d_, 0:ntb], in0=acc[:, :, 64], in1=dsc[:, d_, 0:ntb], op=ALU.max),
                             reads=[f'ps{an}', f'dnC{d_}'], writes=[f'dnC{d_}'])
                        S.op('dve', lambda e: e.tensor_tensor(out=dsc[:, d_, 0:ntb], in0=dsc[:, d_, 0:ntb], in1=emt[:, tb0:tb0 + ntb, r],
                                                              op=ALU.max), reads=['emt', f'dnC{d_}'], writes=[f'dnC{d_}'])
                        S.op('dve', lambda e: e.reciprocal(out=dsc[:, d_, 0:ntb], in_=dsc[:, d_, 0:ntb]), reads=[f'dnC{d_}'],
                             writes=[f'dnC{d_}'])
                    hdst = hb[:, b0 + tb0:b0 + tb0 + ntb, h * 64:(h + 1) * 64]
                    S.op('dve', lambda e: e.tensor_tensor(out=hdst, in0=af[:, :, 0:64],
                                                          in1=dsc[:, 0, 0:ntb].unsqueeze(2).to_broadcast([128, ntb, 64]), op=ALU.mult),
                         reads=[f'ps{accf}', 'dnC0'], writes=['hbC'])
                    S.op('dve', lambda e: e.tensor_tensor(out=dtm[:, 0:ntb, :], in0=ab[:, :, 0:64],
                                                          in1=dsc[:, 1, 0:ntb].unsqueeze(2).to_broadcast([128, ntb, 64]), op=ALU.mult),
                         reads=[f'ps{accb_}', 'dnC1'], writes=['dtmC'])
                    S.op('dve', lambda e: e.tensor_tensor(out=hdst, in0=hdst, in1=dtm[:, 0:ntb, :], op=ALU.add),
                         reads=['hbC', 'dtmC'], writes=['hbC'])
                self.p0_hook(MS + 46 * KB)
                if not samp:
                    for d_, r, col_last in ((0, rf, Ls - 1), (1, rb, 0)):
                        wS = smalls[:, 56:56 + nb]
                        S.op('act', lambda e: e.activation(out=wS, in_=gTM[:, 0:nb, r], func=AF.Exp,
                                                           bias=negG[d_][:, col_last:col_last + 1], scale=1.0),
                             reads=['gTM', f'negG{d_}'], writes=['wSC'])
                        cb = self.psum('lo')
                        for j in range(nb):
                            kw = Wt[j % 2][:, 0:64]
                            S.op('dve', lambda e: e.tensor_scalar(out=kw, in0=kTM[:, b0 + j, h * 64:(h + 1) * 64], scalar1=wS[:, j:j + 1],
                                                                  scalar2=None, op0=ALU.mult), reads=['kTM', 'wSC'], writes=[f'Wt{j % 2}'])
                            S.op('pe', lambda e: e.matmul(ps[cb][0:64, 0:65], lhsT=kw, rhs=vaug[:, b0 + j, h, :], start=(j == 0),
                                                          stop=(j == nb - 1)), reads=[f'Wt{j % 2}', 'vaugC'], writes=[f'ps{cb}'],
                                 inc=(j == nb - 1))
                        self.cs_i = (getattr(self, 'cs_i', 0) + 1) % 4
                        cst = carve(MS + 42 * KB + 512 + self.cs_i * 512, [64, 65], F32)
                        S.op('act', lambda e: e.copy(out=cst, in_=ps[cb][0:64, 0:65]), reads=[f'ps{cb}'], writes=[f'cst{self.cs_i}'])
                        self.store(f'o_C{self.cs_i}', L['o_C'][si, l, d_, h, :, :], cst[:, 0:64], [f'cst{self.cs_i}'])
                        self.store(f'o_n{self.cs_i}', L['o_n'][si, l, d_, h, :].rearrange("(d o) -> d o", o=1), cst[:, 64:65],
                                   [f'cst{self.cs_i}'])
            for bi in range(nb):
                hsrc = hb[:, b0 + bi, :]
                h3 = hsrc.rearrange("p (h e) -> p h e", h=4)
                ycb = Wt[bi % 2][:, 0:256]
                sq3 = Dt[0][:, 0:256].rearrange("p (h e) -> p h e", h=4)
                S.op('dve', lambda e: e.tensor_tensor(out=sq3, in0=h3, in1=h3, op=ALU.mult), reads=['hbC'], writes=['Dt0'])
                S.op('dve', lambda e: e.reduce_sum(out=smalls[:, 60:64], in_=sq3, axis=AX.X), reads=['Dt0'], writes=['ssC'])
                self.rsqrt(smalls[:, 60:64], smalls[:, 60:64], float(64 * EPS), ['ssC'], ['ssC'])
                S.op('dve', lambda e: e.tensor_tensor(out=h3, in0=h3, in1=smalls[:, 60:64].unsqueeze(2).to_broadcast([128, 4, 64]),
                                                      op=ALU.mult), reads=['hbC', 'ssC'], writes=['hbC'])
                S.op('dve', lambda e: e.scalar_tensor_tensor(out=hsrc, in0=hsrc, scalar=8.0, in1=hnB[:, 0:256], op0=ALU.mult, op1=ALU.mult),
                     reads=['hbC', 'hnB'], writes=['hbC'])
                S.op('dve', lambda e: e.tensor_tensor(out=ycb, in0=hsrc, in1=sgo[:, b0 + bi, :], op=ALU.mult),
                     reads=['hbC', 'sgo'], writes=[f'Wt{bi % 2}'])
                to_fm(ycb, f'Wt{bi % 2}', 2, (t0 + s0) // 128 + bi)
            S.barrier()


def build_program(debug=(), nlayers=DEPTH, stop_after=None):
    b0 = Builder(debug=debug, nlayers=nlayers, stop_after=stop_after)
    b0.build()
    b = Builder(debug=debug, nlayers=nlayers, stop_after=stop_after, wplan=b0.wreq)
    b.build()
    return b


_CACHE = {}


def kernel(**inputs):
    inp = {k: np.asarray(v) for k, v in inputs.items()}
    if 'b' not in _CACHE:
        _CACHE['b'] = build_program()
    b = _CACHE['b']
    sh = prep_shared(inp)
    in_maps = []
    for c in range(8):
        m = dict(sh)
        m.update(prep_core(inp, c))
        in_maps.append({k: np.ascontiguousarray(v, dtype=np.float32) for k, v in m.items()})
    res = run_bass_kernel_spmd(b.nc, in_maps, core_ids=list(range(8)))
    return assemble([r for r in res.results])


def assemble(rs):
    y_prompt = np.zeros((16, 256, D), np.float32)
    y_sample = np.zeros((8, 1024, D), np.float32)
    ckv = np.zeros((16, DEPTH, 256, 128), np.float32)
    kr = np.zeros((16, DEPTH, 256, 32), np.float32)
    dk = np.zeros((16, DEPTH, 256, 4, 2, 32), np.float32)
    dv = np.zeros((16, DEPTH, 256, 4, 64), np.float32)
    Cs = np.zeros((16, DEPTH, 2, 4, 64, 64), np.float32)
    ns = np.zeros((16, DEPTH, 2, 4, 64), np.float32)
    ms = np.zeros((16, DEPTH, 2, 4), np.float32)
    for c, r in enumerate(rs):
        yT = np.asarray(r['yT'])
        tok = yT.transpose(2, 1, 0).reshape(T, D)
        y_prompt[2 * c] = tok[0:256]
        y_prompt[2 * c + 1] = tok[256:512]
        y_sample[c] = tok[512:]
        ckv[2 * c:2 * c + 2] = np.asarray(r['o_ckv'])
        kr[2 * c:2 * c + 2] = np.asarray(r['o_kr'])
        dk[2 * c:2 * c + 2] = np.asarray(r['o_dk']).reshape(2, DEPTH, 256, 4, 2, 32)
        dv[2 * c:2 * c + 2] = np.asarray(r['o_dv']).reshape(2, DEPTH, 256, 4, 64)
        Cs[2 * c:2 * c + 2] = np.asarray(r['o_C'])
        ns[2 * c:2 * c + 2] = np.asarray(r['o_n'])
        ms[2 * c:2 * c + 2] = np.asarray(r['o_m'])
    return (y_prompt, y_sample, ckv, kr, dk, dv, Cs, ns, ms)
```
